# Optimizing a Trainium2 kernel written in Bass

```python
import math
import jax, jax.numpy as jnp
from jax import lax
import numpy as np

D_MODEL = 1024
BATCH = 2
SEQ = 8192
DEPTH = 1

MEM_TOKENS = 256
N_MEM_HEADS = 4
DH_MEM = 128
MEM_WIDTH = N_MEM_HEADS * DH_MEM
N_DIFF_HEADS = 8
DH_DIFF = 64
DIFF_Q_WIDTH = N_DIFF_HEADS * 2 * DH_DIFF
DIFF_V_WIDTH = N_DIFF_HEADS * 2 * DH_DIFF
Q_BLOCK = 128
POOL_WINDOWS = (2, 4, 8, 16)
POOL_GROUPS = len(POOL_WINDOWS)
POOL_GROUP_WIDTH = 128
POOL_WIDTH = POOL_GROUPS * POOL_GROUP_WIDTH
N_BRANCHES = 3
IN_COLS = 2 * DIFF_Q_WIDTH + DIFF_V_WIDTH + POOL_WIDTH + MEM_WIDTH + N_BRANCHES * D_MODEL
D_FF = 2816
CONV_WIDTH = 3
NORM_EPS = 1e-6

kernel_name = "hybrid_diffattn_pool_memxattn_convffn"


def rms_norm(x, g):
    xf = x.astype(jnp.float32)
    y = xf * lax.rsqrt(jnp.mean(xf * xf, axis=-1, keepdims=True) + NORM_EPS)
    return (y * g.astype(jnp.float32)).astype(x.dtype)


def alibi_slopes(n_heads):
    return jnp.exp2(-8.0 * jnp.arange(1, n_heads + 1, dtype=jnp.float32) / n_heads)


def lambda_init_for(layer_idx):
    return 0.8 - 0.6 * math.exp(-0.3 * layer_idx)


def diff_attention(q, k, v, lam):
    B, S = q.shape[0], q.shape[1]
    n_blocks = S // Q_BLOCK
    scale = DH_DIFF ** -0.5
    slopes = alibi_slopes(N_DIFF_HEADS)
    kf = k.astype(jnp.float32)
    vf = v.astype(jnp.float32)
    kpos = jnp.arange(S)
    qb = q.astype(jnp.float32).reshape(B, n_blocks, Q_BLOCK, N_DIFF_HEADS, 2, DH_DIFF)
    qb = jnp.moveaxis(qb, 1, 0)

    def one_block(args):
        qblk, i = args
        qpos = i * Q_BLOCK + jnp.arange(Q_BLOCK)
        s = jnp.einsum('bqhcd,bkhcd->bhcqk', qblk, kf) * scale
        dist = qpos[:, None] - kpos[None, :]
        bias = -slopes[:, None, None] * dist.astype(jnp.float32)
        s = s + bias[None, :, None]
        s = jnp.where((dist >= 0)[None, None, None], s, jnp.finfo(jnp.float32).min)
        p = jax.nn.softmax(s, axis=-1)
        a = p[:, :, 0] - lam * p[:, :, 1]
        return jnp.einsum('bhqk,bkhe->bqhe', a, vf)

    out = lax.map(one_block, (qb, jnp.arange(n_blocks)))
    return jnp.moveaxis(out, 0, 1).reshape(B, S, N_DIFF_HEADS, 2 * DH_DIFF)


def multiscale_pool(u, pool_w, pool_scale):
    B, S, _ = u.shape
    uf = u.astype(jnp.float32).reshape(B, S, POOL_GROUPS, POOL_GROUP_WIDTH)
    csum = jnp.cumsum(uf, axis=1)
    cpad = jnp.concatenate([jnp.zeros_like(csum[:, :1]), csum], axis=1)
    t = jnp.arange(S)
    outs = []
    for g, w in enumerate(POOL_WINDOWS):
        lo = jnp.maximum(t + 1 - w, 0)
        window_sum = csum[:, :, g] - cpad[:, lo, g]
        count = jnp.minimum(t + 1, w).astype(jnp.float32)
        outs.append(window_sum / count[None, :, None] - uf[:, :, g])
    p = jnp.stack(outs, axis=2)
    y = jnp.einsum('bsgc,gcd->bsgd', p, pool_w.astype(jnp.float32))
    return (y.reshape(B, S, POOL_WIDTH) * pool_scale.astype(jnp.float32)).astype(u.dtype)


def memory_cross_attention(q, mem_n, w_mem_kv):
    B, S, _ = q.shape
    kv = mem_n @ w_mem_kv
    k_m, v_m = jnp.split(kv, 2, axis=-1)
    qh = q.astype(jnp.float32).reshape(B, S, N_MEM_HEADS, DH_MEM)
    kh = k_m.astype(jnp.float32).reshape(B, -1, N_MEM_HEADS, DH_MEM)
    vh = v_m.astype(jnp.float32).reshape(B, -1, N_MEM_HEADS, DH_MEM)
    s = jnp.einsum('bshd,bmhd->bhsm', qh, kh) * (DH_MEM ** -0.5)
    p = jax.nn.softmax(s, axis=-1)
    o = jnp.einsum('bhsm,bmhd->bshd', p, vh)
    return o.reshape(B, S, MEM_WIDTH).astype(q.dtype)


def causal_depthwise_conv(u, w, b):
    S = u.shape[1]
    up = jnp.pad(u, ((0, 0), (CONV_WIDTH - 1, 0), (0, 0)))
    y = b
    for j in range(CONV_WIDTH):
        y = y + w[j] * up[:, j:j + S]
    return y


def setup_inputs(seed: int = 0) -> dict:
    key = jax.random.key(seed)
    ks = jax.random.split(key, 24)

    def nrm(k, shape, scale):
        return jax.random.normal(k, shape, jnp.float32) * scale

    def gain(k, n):
        return 1.0 + nrm(k, (DEPTH, n), 0.05)

    return {
        "x": nrm(ks[0], (BATCH, SEQ, D_MODEL), 1.0),
        "mem": nrm(ks[1], (BATCH, MEM_TOKENS, D_MODEL), 1.0),
        "norm_mix_pre": gain(ks[2], D_MODEL),
        "w_in": nrm(ks[3], (DEPTH, D_MODEL, IN_COLS), D_MODEL ** -0.5),
        "lambda_q1": nrm(ks[4], (DEPTH, DH_DIFF), 0.1),
        "lambda_k1": nrm(ks[5], (DEPTH, DH_DIFF), 0.1),
        "lambda_q2": nrm(ks[6], (DEPTH, DH_DIFF), 0.1),
        "lambda_k2": nrm(ks[7], (DEPTH, DH_DIFF), 0.1),
        "subln_g": gain(ks[8], 2 * DH_DIFF),
        "w_attn_branch": nrm(ks[9], (DEPTH, DIFF_V_WIDTH, D_MODEL), DIFF_V_WIDTH ** -0.5),
        "pool_w": nrm(ks[10], (DEPTH, POOL_GROUPS, POOL_GROUP_WIDTH, POOL_GROUP_WIDTH), POOL_GROUP_WIDTH ** -0.5),
        "pool_scale": gain(ks[11], POOL_WIDTH),
        "w_pool_branch": nrm(ks[12], (DEPTH, POOL_WIDTH, D_MODEL), POOL_WIDTH ** -0.5),
        "norm_mem": gain(ks[13], D_MODEL),
        "w_mem_kv": nrm(ks[14], (DEPTH, D_MODEL, 2 * MEM_WIDTH), D_MODEL ** -0.5),
        "w_mem_branch": nrm(ks[15], (DEPTH, MEM_WIDTH, D_MODEL), MEM_WIDTH ** -0.5),
        "w_out": nrm(ks[16], (DEPTH, D_MODEL, D_MODEL), D_MODEL ** -0.5),
        "norm_mix_post": gain(ks[17], D_MODEL),
        "norm_ffn_pre": gain(ks[18], D_MODEL),
        "w_up": nrm(ks[19], (DEPTH, D_MODEL, 2 * D_FF), D_MODEL ** -0.5),
        "conv_w": nrm(ks[20], (DEPTH, CONV_WIDTH, 2 * D_FF), CONV_WIDTH ** -0.5),
        "conv_b": nrm(ks[21], (DEPTH, 2 * D_FF), 0.02),
        "w_down": nrm(ks[22], (DEPTH, D_FF, D_MODEL), D_FF ** -0.5),
        "norm_ffn_post": gain(ks[23], D_MODEL),
    }


def reference(x, mem, norm_mix_pre, w_in, lambda_q1, lambda_k1, lambda_q2, lambda_k2, subln_g,
              w_attn_branch, pool_w, pool_scale, w_pool_branch, norm_mem, w_mem_kv, w_mem_branch,
              w_out, norm_mix_post, norm_ffn_pre, w_up, conv_w, conv_b, w_down, norm_ffn_post):
    B, S, D = x.shape
    split_pts = np.cumsum([DIFF_Q_WIDTH, DIFF_Q_WIDTH, DIFF_V_WIDTH, POOL_WIDTH, MEM_WIDTH]).tolist()
    for l in range(DEPTH):
        lam_init = lambda_init_for(l)
        h = rms_norm(x, norm_mix_pre[l])
        proj = h @ w_in[l]
        q_d, k_d, v_d, u_p, q_m, gate_logits = jnp.split(proj, split_pts, axis=-1)

        lam = (jnp.exp(jnp.sum(lambda_q1[l].astype(jnp.float32) * lambda_k1[l].astype(jnp.float32)))
               - jnp.exp(jnp.sum(lambda_q2[l].astype(jnp.float32) * lambda_k2[l].astype(jnp.float32)))
               + lam_init)
        q_d = q_d.reshape(B, S, N_DIFF_HEADS, 2, DH_DIFF)
        k_d = k_d.reshape(B, S, N_DIFF_HEADS, 2, DH_DIFF)
        v_d = v_d.reshape(B, S, N_DIFF_HEADS, 2 * DH_DIFF)
        a = diff_attention(q_d, k_d, v_d, lam)
        a = rms_norm(a, subln_g[l]) * (1.0 - lam_init)
        y_attn = a.reshape(B, S, DIFF_V_WIDTH).astype(x.dtype) @ w_attn_branch[l]

        y_pool = multiscale_pool(u_p, pool_w[l], pool_scale[l]) @ w_pool_branch[l]

        mem_n = rms_norm(mem, norm_mem[l])
        y_mem = memory_cross_attention(q_m, mem_n, w_mem_kv[l]) @ w_mem_branch[l]

        gates = jax.nn.sigmoid(gate_logits.astype(jnp.float32)).reshape(B, S, N_BRANCHES, D).astype(x.dtype)
        mix = gates[:, :, 0] * y_attn + gates[:, :, 1] * y_pool + gates[:, :, 2] * y_mem
        x = x + rms_norm(mix @ w_out[l], norm_mix_post[l])

        h2 = rms_norm(x, norm_ffn_pre[l])
        up = causal_depthwise_conv(h2 @ w_up[l], conv_w[l], conv_b[l])
        g, v = jnp.split(up, 2, axis=-1)
        ff = (jax.nn.gelu(g, approximate=True) * v) @ w_down[l]
        x = x + rms_norm(ff, norm_ffn_post[l])
    return x
```

```python
import contextlib
import math
import os
KF_A = os.environ.get('KF_A', '1') == '1'
KF_B = os.environ.get('KF_B', '1') == '1'
KF_C = os.environ.get('KF_C', '0') == '1'
import numpy as np
import concourse.bass as bass
import concourse.mybir as mybir
from concourse.bass_utils import run_bass_kernel_spmd

F32 = mybir.dt.float32
BF16 = mybir.dt.bfloat16
AF = mybir.ActivationFunctionType
ALU = mybir.AluOpType

D = 1024
S = 8192
NT = 17
ST = 126
NTOK = NT * 128
NH = 8
DFF = 2816
NCH = DFF // 128
EPS = 1e-6
LAM_INIT = 0.8 - 0.6 * math.exp(0.0)
SLOPES = [2.0 ** (-(i + 1)) for i in range(NH)]
GELU_C = math.sqrt(2.0 / math.pi)
AGROUPS = [(0, 4), (4, 4), (8, 4), (12, 4), (16, 1)]
CGROUPS = [(2 * i, 2) for i in range(8)] + [(16, 1)]

COMPUTE = ("pe", "act", "dve", "pool")


def tile_base(j):
    return 4 * (j + 1) if j < NT - 1 else 0


def tile_bounds(j):
    i0, i1 = tile_base(j), tile_base(j) + 3
    p_lo = min(max(ST * i0 - 2, 0), S - 1)
    p_hi = min(max(ST * i1 + 125, 0), S - 1)
    return p_lo // 128, p_hi // 128


NM = max(tile_bounds(j)[1] - tile_bounds(j)[0] + 1 for j in range(NT))


class T:
    __slots__ = ("name", "ap", "w", "rd", "sem", "ndma")

    def __init__(self, name, ap, sem=None):
        self.name = name
        self.ap = ap
        self.w = None
        self.rd = {}
        self.sem = sem
        self.ndma = 0


class KB:
    def __init__(self, nc):
        self.nc = nc
        self.q = {e: [] for e in ("pe", "act", "dve", "pool", "sp")}
        self.cnt = {e: 0 for e in COMPUTE}
        self.waited = {e: {} for e in self.q}
        self.sems = {}
        self.dma_tiles = []
        for e in COMPUTE:
            self.sems[e] = nc.alloc_semaphore(name="s_" + e)

    def tile(self, name, ap, dma=False):
        t = T(name, ap, self.nc.alloc_semaphore(name="d_" + name) if dma else None)
        if dma:
            self.dma_tiles.append(t)
        return t

    def _need(self, eng, dep, waits):
        if dep is None:
            return
        if dep[0] == "c":
            _, e, idx = dep
            if e == eng and e == "pe":
                return
            key, val, sem = ("c", e), idx, self.sems[e]
        else:
            _, t, cnt = dep
            key, val, sem = ("d", id(t)), 16 * cnt, t.sem
        if self.waited[eng].get(key, 0) >= val:
            return
        self.waited[eng][key] = val
        waits.append((sem, val))

    def _deps(self, eng, reads, writes):
        waits = []
        for t in reads:
            self._need(eng, t.w, waits)
        for t in writes:
            self._need(eng, t.w, waits)
            for d in t.rd.values():
                self._need(eng, d, waits)
        return waits

    def op(self, eng, fn, reads=(), writes=()):
        waits = self._deps(eng, reads, writes)
        self.cnt[eng] += 1
        dep = ("c", eng, self.cnt[eng])
        self.q[eng].append((waits, fn, self.sems[eng], 1))
        for t in reads:
            t.rd[("c", eng)] = dep
        for t in writes:
            t.w = dep
            t.rd = {}

    def dma(self, queue, fn, reads=(), writes=(), semtile=None):
        waits = self._deps(queue, reads, writes)
        semtile.ndma += 1
        dep = ("d", semtile, semtile.ndma)
        self.q[queue].append((waits, fn, semtile.sem, 16))
        for t in reads:
            t.rd[("d", id(semtile))] = dep
        for t in writes:
            t.w = dep
            t.rd = {}

    def barrier(self):
        for eng in self.q:
            waits = []
            for e in COMPUTE:
                if self.cnt[e] > 0 and e != eng:
                    self._need(eng, ("c", e, self.cnt[e]), waits)
            for t in self.dma_tiles:
                if t.ndma > 0:
                    self._need(eng, ("d", t, t.ndma), waits)
            if waits:
                self.q[eng].append((waits, None, None, 0))

    def emit(self):
        self.barrier()
        names = {"pe": "tensor", "act": "scalar", "dve": "vector", "pool": "gpsimd", "sp": "sync"}
        with self.nc.Block() as block:
            for e in ("sp", "pe", "act", "dve", "pool"):
                def body(eng, lst=self.q[e]):
                    for waits, fn, sem, inc in lst:
                        for (s, v) in waits:
                            eng.wait_ge(s, v)
                        if fn is not None:
                            fn(eng).then_inc(sem, inc)
                getattr(block, names[e])(body)


ARENA_WORDS = 48448


def build_program(debug=False):
    nc = bass.Bass("TRN2", target_bir_lowering=False)

    def din(name, shape):
        return nc.dram_tensor(name, list(shape), F32, kind="ExternalInput").ap()

    xb = din("xb", [S, D])
    xo = din("xo", [NTOK, D])
    xh = din("xh", [NT * 16, D])
    memb = din("memb", [256, D])
    w_in = din("w_in", [D, 7168])
    w_attn = din("w_attn", [D, D])
    pool_w = din("pool_w", [4, 128, 128])
    w_pb = din("w_pb", [512, D])
    w_mkv = din("w_mkv", [D, D])
    w_mb = din("w_mb", [512, D])
    w_out = din("w_out", [D, D])
    w_up = din("w_up", [D, 2 * DFF])
    w_down = din("w_down", [DFF, D])
    vec = {n: din(n, [1, D]) for n in ("g_mix_pre", "g_mem", "g_mix_post", "g_ffn_pre", "g_ffn_post")}
    lamv = din("lamv", [4, 64])
    subln = din("subln", [128, 1])
    pscale = din("pscale", [128, 4])
    convw = din("convw", [128, 3 * 2 * NCH])
    convb = din("convb", [128, 2 * NCH])
    ident = din("ident", [128, 128])
    kaug = din("kaug", [4, S])
    qaug = din("qaug", [NH, 4, NTOK])
    masks = din("masks", [128, NT * NM * 128])
    invc = din("invc", [4, NTOK])
    valid = din("valid", [128, NT])
    out = nc.dram_tensor("out", [NTOK, D], F32, kind="ExternalOutput").ap()

    def dscr(name, shape, dt):
        return nc.dram_tensor(name, list(shape), dt, kind="ExternalOutput" if debug else "Internal").ap()

    KT_s = dscr("KT_s", [NH, 2, 64, S], BF16)
    V_s = dscr("V_s", [S, D], BF16)
    QT_s = dscr("QT_s", [NH, 2, 64, NTOK], BF16)
    A_s = dscr("A_s", [NH, 128, NTOK], BF16)
    XM_s = dscr("XM_s", [NTOK, D], F32)

    es = contextlib.ExitStack()
    with es:
        arena = es.enter_context(nc.sbuf_tensor("arena", [128, ARENA_WORDS], F32))
        ps = es.enter_context(nc.psum_tensor("ps", [128, 4096], F32))
        kb = KB(nc)
        off = [0]

        def carve(name, shape, dt, dma=False):
            n = int(np.prod(shape))
            nw = (n * (2 if dt == BF16 else 4) + 3) // 4
            assert off[0] + nw <= ARENA_WORDS, (name, off[0], nw)
            ap = arena[:, off[0]:off[0] + nw]
            off[0] += nw
            if dt != F32:
                ap = ap.bitcast(dt)
            if len(shape) == 2:
                ap = ap.rearrange("p (a b) -> p a b", b=shape[1])
            elif len(shape) == 3:
                ap = ap.rearrange("p (a b c) -> p a b c", b=shape[1], c=shape[2])
            return kb.tile(name, ap, dma=dma)

        def bank(k, n=1):
            return ps[:, k * 512:(k + n) * 512]

        PB = [kb.tile(f"bank{k}", bank(k)) for k in range(8)]

        identb = carve("identb", [128], BF16, dma=True)
        onesb = carve("onesb", [128], BF16)
        lam4 = carve("lam4", [4, 64], F32, dma=True)
        lamt = carve("lamt", [2, 64], F32)
        lams = carve("lams", [4], F32)
        neglam = carve("neglam", [1], F32)
        gsub = carve("gsub", [1], F32, dma=True)
        psc = carve("psc", [4], F32, dma=True)
        cw = carve("cw", [3 * 2 * NCH], F32, dma=True)
        cb = carve("cb", [2 * NCH], F32, dma=True)
        vld = carve("vld", [NT], F32, dma=True)
        persist_ffn = off[0]
        KmT = carve("KmT", [4, 256], BF16)
        Vm = carve("Vm", [2, 512], BF16)
        junk = carve("junk", [64], BF16)
        persist_end = off[0]

        kb.dma("pool", lambda e: e.dma_start(out=identb.ap, in_=ident), writes=[identb], semtile=identb)
        kb.op("pool", lambda e: e.memset(onesb.ap, 1.0), writes=[onesb])
        kb.dma("sp", lambda e: e.dma_start(out=lam4.ap,
                                            in_=lamv.partition_broadcast(128)),
               writes=[lam4], semtile=lam4)
        kb.dma("sp", lambda e: e.dma_start(out=gsub.ap, in_=subln), writes=[gsub], semtile=gsub)
        kb.dma("sp", lambda e: e.dma_start(out=psc.ap, in_=pscale), writes=[psc], semtile=psc)
        kb.dma("sp", lambda e: e.dma_start(out=cw.ap, in_=convw), writes=[cw], semtile=cw)
        kb.dma("sp", lambda e: e.dma_start(out=cb.ap, in_=convb), writes=[cb], semtile=cb)
        kb.dma("sp", lambda e: e.dma_start(out=vld.ap, in_=valid), writes=[vld], semtile=vld)
        kb.op("dve", lambda e: e.tensor_tensor(out=lamt.ap[:, 0, :], in0=lam4.ap[:, 0, :], in1=lam4.ap[:, 1, :], op=ALU.mult),
              reads=[lam4], writes=[lamt])
        kb.op("dve", lambda e: e.tensor_tensor(out=lamt.ap[:, 1, :], in0=lam4.ap[:, 2, :], in1=lam4.ap[:, 3, :], op=ALU.mult),
              reads=[lam4], writes=[lamt])
        for k in range(2):
            kb.op("act", lambda e, k=k: e.activation(out=junk.ap[:, 0:64], in_=lamt.ap[:, k, :], func=AF.Copy,
                                                     accum_out=lams.ap[:, k:k + 1]),
                  reads=[lamt], writes=[junk, lams])
        kb.op("act", lambda e: e.activation(out=lams.ap[:, 2:4], in_=lams.ap[:, 0:2], func=AF.Exp), reads=[lams], writes=[lams])
        kb.op("dve", lambda e: e.tensor_tensor(out=neglam.ap, in0=lams.ap[:, 3:4], in1=lams.ap[:, 2:3], op=ALU.subtract),
              reads=[lams], writes=[neglam])
        kb.op("dve", lambda e: e.tensor_scalar(out=neglam.ap, in0=neglam.ap, scalar1=-LAM_INIT, scalar2=None, op0=ALU.add),
              reads=[neglam], writes=[neglam])
        kb.op("dve", lambda e: e.tensor_scalar(out=gsub.ap, in0=gsub.ap, scalar1=1.0 - LAM_INIT, scalar2=None, op0=ALU.mult),
              reads=[gsub], writes=[gsub])

        rr = [0]

        def evac(out_ap, in_ap, reads, writes, scale=None):
            rr[0] += 1
            if rr[0] % 2 == 0:
                kb.op("dve", lambda e: e.tensor_copy(out=out_ap, in_=in_ap), reads=reads, writes=writes)
            else:
                kb.op("act", lambda e: e.activation(out=out_ap, in_=in_ap, func=AF.Copy), reads=reads, writes=writes)

        def load_w(t, src_ap):
            kb.dma("pool", lambda e: e.dma_start(out=t.ap, in_=src_ap), writes=[t], semtile=t)

        def wview(w, c0, c1):
            return w.rearrange("(kc p) n -> p kc n", p=128)[:, :, c0:c1]

        def rstd_from_ss(rs, ss, n, extra_reads=()):
            kb.op("act", lambda e: e.activation(out=rs.ap, in_=ss.ap, func=AF.Ln, scale=1.0 / n, bias=EPS),
                  reads=[ss], writes=[rs])
            kb.op("act", lambda e: e.activation(out=rs.ap, in_=rs.ap, func=AF.Exp, scale=-0.5), reads=[rs], writes=[rs])

        def norm_tile(xt, rows, gt, ht, ss, rs, vcol=None):
            kb.op("act", lambda e: e.activation(out=ht.ap[0:rows, :], in_=xt.ap[0:rows, :], func=AF.Square,
                                                accum_out=ss.ap[0:rows, :]),
                  reads=[xt], writes=[ht, ss])
            rstd_from_ss(rs, ss, D)
            if vcol is not None:
                kb.op("dve", lambda e: e.tensor_tensor(out=rs.ap, in0=rs.ap, in1=vcol, op=ALU.mult), reads=[rs, vld], writes=[rs])
            kb.op("dve", lambda e: e.scalar_tensor_tensor(out=ht.ap[0:rows, :], in0=xt.ap[0:rows, :], scalar=rs.ap[0:rows, :],
                                                          in1=gt.ap[0:rows, :], op0=ALU.mult, op1=ALU.mult),
                  reads=[xt, rs, gt], writes=[ht])

        def transpose_to(ht, rows, pbank_idx, dst_tile, dst_ap3):
            pb = PB[pbank_idx]
            pv = bank(pbank_idx).bitcast(BF16)
            for kc in range(8):
                kb.op("pe", lambda e, kc=kc: e.transpose(out=pv[:, kc * 128:kc * 128 + rows],
                                                         in_=ht.ap[0:rows, kc * 128:(kc + 1) * 128],
                                                         identity=identb.ap[0:rows, 0:rows]),
                      reads=[ht, identb], writes=[pb])
            src = pv.rearrange("p (k c) -> p k c", c=128)[:, :, 0:rows]
            evac(dst_ap3, src, [pb], [dst_tile])

        def load_g(t, name):
            kb.dma("sp", lambda e: e.dma_start(out=t.ap, in_=vec[name].partition_broadcast(128)),
                   writes=[t], semtile=t)

        off[0] = persist_end
        g1 = carve("g1", [1024], F32, dma=True)
        g2 = carve("g2", [1024], F32, dma=True)
        load_g(g1, "g_mix_pre")
        load_g(g2, "g_mem")
        Wkv = carve("Wkv", [8, 2048], BF16, dma=True)
        Wm = carve("Wm", [8, 1024], BF16, dma=True)
        load_w(Wkv, wview(w_in, 1024, 3072))
        load_w(Wm, wview(w_mkv, 0, 1024))
        xt2 = [carve(f"xt{i}", [1024], F32, dma=True) for i in range(2)]
        ht2 = [carve(f"ht{i}", [1024], BF16) for i in range(2)]
        ss2 = [carve(f"ss{i}", [1], F32) for i in range(2)]
        rs2 = [carve(f"rs{i}", [1], F32) for i in range(2)]
        hT2 = [carve(f"hT{i}", [8, 512], BF16) for i in range(2)]
        kst2 = [carve(f"kst{i}", [8, 512], BF16, dma=True) for i in range(2)]
        vst2 = [carve(f"vst{i}", [4, 1024], BF16, dma=True) for i in range(2)]

        hTm = hT2[0]
        for blk in range(2):
            xt, ht, ss, rs = xt2[blk], ht2[blk], ss2[blk], rs2[blk]
            kb.dma("sp", lambda e, xt=xt, blk=blk: e.dma_start(out=xt.ap, in_=memb[blk * 128:(blk + 1) * 128, :]),
                   writes=[xt], semtile=xt)
            norm_tile(xt, 128, g2, ht, ss, rs)
            transpose_to(ht, 128, blk, hTm, hTm.ap[:, :, blk * 128:(blk + 1) * 128])
        for hd in range(4):
            for kc in range(8):
                kb.op("pe", lambda e, hd=hd, kc=kc: e.matmul(out=bank(2 + hd % 2)[:, 0:256],
                                                             lhsT=Wm.ap[:, kc, hd * 128:(hd + 1) * 128],
                                                             rhs=hTm.ap[:, kc, 0:256], start=(kc == 0), stop=(kc == 7)),
                      reads=[Wm, hTm], writes=[PB[2 + hd % 2]])
            evac(KmT.ap[:, hd, :], bank(2 + hd % 2)[:, 0:256], [PB[2 + hd % 2]], [KmT])
        for ch in range(2):
            for kc in range(8):
                kb.op("pe", lambda e, ch=ch, kc=kc: e.matmul(out=bank(4 + ch), lhsT=hTm.ap[:, kc, ch * 128:(ch + 1) * 128],
                                                             rhs=Wm.ap[:, kc, 512:1024], start=(kc == 0), stop=(kc == 7)),
                      reads=[Wm, hTm], writes=[PB[4 + ch]])
            evac(Vm.ap[:, ch, :], bank(4 + ch), [PB[4 + ch]], [Vm])

        def prep1(G, blk):
            hT = hT2[G % 2]
            n = G * 4 + blk
            xt, ht, ss, rs = xt2[n % 2], ht2[n % 2], ss2[n % 2], rs2[n % 2]
            kb.dma("sp", lambda e: e.dma_start(out=xt.ap, in_=xb[n * 128:(n + 1) * 128, :]), writes=[xt], semtile=xt)
            norm_tile(xt, 128, g1, ht, ss, rs)

        def prep1t(G, blk):
            hT = hT2[G % 2]
            n = G * 4 + blk
            transpose_to(ht2[n % 2], 128, n % 2, hT, hT.ap[:, :, blk * 128:(blk + 1) * 128])

        def kpart(G, heads):
            hT, kst = hT2[G % 2], kst2[G % 2]
            for h in heads:
                bk = 2 + h % 2
                for kc in range(8):
                    kb.op("pe", lambda e, h=h, kc=kc, bk=bk: e.matmul(out=bank(bk), lhsT=Wkv.ap[:, kc, h * 128:(h + 1) * 128],
                                                                     rhs=hT.ap[:, kc, :], start=(kc == 0), stop=(kc == 7)),
                          reads=[Wkv, hT], writes=[PB[bk]])
                evac(kst.ap[:, h, :], bank(bk), [PB[bk]], [kst])
            if heads[-1] == NH - 1:
                for m in range(2):
                    kb.dma("sp", lambda e, m=m: e.dma_start(
                        out=KT_s[:, m, :, G * 512:(G + 1) * 512].rearrange("h d c -> d h c"),
                        in_=kst.ap[m * 64:(m + 1) * 64, :, :]), reads=[kst], semtile=kst)

        def vpart(G, blks):
            hT, vst = hT2[G % 2], vst2[G % 2]
            for blk in blks:
                for half in range(2):
                    bk = 4 + (blk * 2 + half) % 4
                    for kc in range(8):
                        kb.op("pe", lambda e, blk=blk, half=half, kc=kc, bk=bk: e.matmul(
                            out=bank(bk), lhsT=hT.ap[:, kc, blk * 128:(blk + 1) * 128],
                            rhs=Wkv.ap[:, kc, 1024 + half * 512:1024 + (half + 1) * 512], start=(kc == 0), stop=(kc == 7)),
                            reads=[Wkv, hT], writes=[PB[bk]])
                    evac(vst.ap[:, blk, half * 512:(half + 1) * 512], bank(bk), [PB[bk]], [vst])
            if blks[-1] == 3:
                kb.dma("sp", lambda e: e.dma_start(
                    out=V_s[G * 512:(G + 1) * 512, :].rearrange("(b p) c -> p b c", p=128), in_=vst.ap),
                    reads=[vst], semtile=vst)

        for blk in range(4):
            prep1(0, blk)
            prep1t(0, blk)
        for G in range(16):
            parts = [lambda: kpart(G, [0, 1, 2, 3]), lambda: kpart(G, [4, 5, 6, 7]), lambda: vpart(G, [0, 1]), lambda: vpart(G, [2, 3])]
            for p in range(4):
                if G + 1 < 16:
                    prep1(G + 1, p)
                parts[p]()
                if G + 1 < 16:
                    prep1t(G + 1, p)
        kb.barrier()

        off[0] = persist_end
        g1 = carve("g1b", [1024], F32, dma=True)
        load_g(g1, "g_mix_pre")
        Wq = carve("Wq", [8, 1024], BF16, dma=True)
        load_w(Wq, wview(w_in, 0, 1024))
        xt2 = [carve(f"xtb{i}", [1024], F32, dma=True) for i in range(2)]
        ht2 = [carve(f"htb{i}", [1024], BF16) for i in range(2)]
        ss2 = [carve(f"ssb{i}", [1], F32) for i in range(2)]
        rs2 = [carve(f"rsb{i}", [1], F32) for i in range(2)]
        hT2 = [carve(f"hTb{i}", [8, 512], BF16) for i in range(2)]
        qst2 = [carve(f"qst{i}", [8, 512], BF16, dma=True) for i in range(2)]
        def prep2a(gi, t):
            j0, nt = AGROUPS[gi]
            hT = hT2[gi % 2]
            n = j0 + t
            xt, ht, ss, rs = xt2[n % 2], ht2[n % 2], ss2[n % 2], rs2[n % 2]
            kb.dma("sp", lambda e: e.dma_start(out=xt.ap, in_=xo[n * 128:(n + 1) * 128, :]), writes=[xt], semtile=xt)
            norm_tile(xt, 128, g1, ht, ss, rs)

        def prep2at(gi, t):
            j0, nt = AGROUPS[gi]
            hT = hT2[gi % 2]
            n = j0 + t
            transpose_to(ht2[n % 2], 128, n % 2, hT, hT.ap[:, :, t * 128:(t + 1) * 128])

        def qpart(gi, heads):
            j0, nt = AGROUPS[gi]
            hT, qst = hT2[gi % 2], qst2[gi % 2]
            N = nt * 128
            for h in heads:
                bk = 2 + h % 4
                for kc in range(8):
                    kb.op("pe", lambda e, h=h, kc=kc, bk=bk: e.matmul(
                        out=bank(bk)[:, 0:N], lhsT=Wq.ap[:, kc, h * 128:(h + 1) * 128], rhs=hT.ap[:, kc, 0:N],
                        start=(kc == 0), stop=(kc == 7)), reads=[Wq, hT], writes=[PB[bk]])
                evac(qst.ap[:, h, 0:N], bank(bk)[:, 0:N], [PB[bk]], [qst])
            if heads[-1] == NH - 1:
                for m in range(2):
                    kb.dma("sp", lambda e, m=m: e.dma_start(
                        out=QT_s[:, m, :, j0 * 128:j0 * 128 + N].rearrange("h d c -> d h c"),
                        in_=qst.ap[m * 64:(m + 1) * 64, :, 0:N]), reads=[qst], semtile=qst)

        for t in range(AGROUPS[0][1]):
            prep2a(0, t)
            prep2at(0, t)
        for gi in range(len(AGROUPS)):
            for p in range(4):
                nxt = gi + 1 < len(AGROUPS) and p < AGROUPS[gi + 1][1]
                if nxt:
                    prep2a(gi + 1, p)
                qpart(gi, [2 * p, 2 * p + 1])
                if nxt:
                    prep2at(gi + 1, p)
        kb.barrier()

        off[0] = persist_end
        Kt = [[carve(f"Kt{b}{m}", [S], BF16, dma=True) for m in range(2)] for b in range(2)]
        Qt = [[carve(f"Qt{b}{m}", [NTOK], BF16, dma=True) for m in range(2)] for b in range(2)]
        Vh = [carve(f"Vh{b}", [64, 128], BF16, dma=True) for b in range(2)]
        mk = carve("mk", [NT * NM, 128], BF16, dma=True)
        PT = [carve(f"PT{b}", [2, 512], BF16) for b in range(2)]
        rl = carve("rl", [2, 512], F32)
        a1 = carve("a1", [512], F32)
        a2 = carve("a2", [512], F32)
        sqb = carve("sqb", [512], BF16)
        rsa = carve("rsa", [512], F32)
        ast = [carve(f"ast{b}", [512], BF16, dma=True) for b in range(2)]
        kb.dma("pool", lambda e: e.dma_start(out=mk.ap, in_=masks.rearrange("p (a b) -> p a b", b=128)), writes=[mk], semtile=mk)
        KtA = [[kb.tile(f"KtA{b}{m}", Kt[b][m].ap[64:68, :], dma=True) for m in range(2)] for b in range(2)]
        QtA = [[kb.tile(f"QtA{b}{m}", Qt[b][m].ap[64:68, :], dma=True) for m in range(2)] for b in range(2)]
        for b in range(2):
            for m in range(2):
                kb.dma("pool", lambda e, b=b, m=m: e.dma_start(out=Kt[b][m].ap[64:68, :].rearrange("a (b c) -> a b c", c=2048), in_=kaug.rearrange("a (b c) -> a b c", c=2048)),
                       writes=[KtA[b][m]], semtile=KtA[b][m])
        SB = [kb.tile("SB0", bank(0, 2)), kb.tile("SB1", bank(2, 2))]
        OB = [PB[4], PB[5]]
        DB = [PB[6], PB[7]]
        def head_loads(h):
            bsel = h % 2
            for m in range(2):
                kb.dma("sp", lambda e, h=h, m=m, bsel=bsel: e.dma_start(out=Kt[bsel][m].ap[0:64, :], in_=KT_s[h, m, :, :]),
                       writes=[Kt[bsel][m]], semtile=Kt[bsel][m])
                kb.dma("sp", lambda e, h=h, m=m, bsel=bsel: e.dma_start(out=Qt[bsel][m].ap[0:64, :], in_=QT_s[h, m, :, :]),
                       writes=[Qt[bsel][m]], semtile=Qt[bsel][m])
                kb.dma("pool", lambda e, h=h, m=m, bsel=bsel: e.dma_start(
                    out=Qt[bsel][m].ap[64:68, :].rearrange("a (b c) -> a b c", c=1088),
                    in_=qaug[h, :, :].rearrange("a (b c) -> a b c", c=1088)), writes=[QtA[bsel][m]], semtile=QtA[bsel][m])
            kb.dma("sp", lambda e, h=h, bsel=bsel: e.dma_start(
                out=Vh[bsel].ap, in_=V_s.rearrange("(kb p) c -> p kb c", p=128)[:, :, h * 128:(h + 1) * 128]),
                writes=[Vh[bsel]], semtile=Vh[bsel])

        items = []
        for h in range(NH):
            for gi, (j0, nt) in enumerate(AGROUPS):
                bnds = [tile_bounds(j0 + t) for t in range(nt)]
                kb_last = bnds[-1][1]
                for kbi in range(kb_last + 1):
                    tmin = min(t for t in range(nt) if bnds[t][1] >= kbi)
                    items.append(dict(h=h, gi=gi, j0=j0, nt=nt, bnds=bnds, kb_last=kb_last, kbi=kbi, tmin=tmin,
                                      last_of_head=(gi == len(AGROUPS) - 1 and kbi == kb_last)))
        for idx, it in enumerate(items):
            it["par"] = idx % 2

        pending = []

        def emit_scores(it):
            h, j0, nt, bnds, kbi, tmin = it["h"], it["j0"], it["nt"], it["bnds"], it["kbi"], it["tmin"]
            c0, c1 = tmin * 128, nt * 128
            gc0 = j0 * 128
            sb, pt, sbase = SB[it["par"]], PT[it["par"]], it["par"] * 1024
            for pnd in list(pending):
                pnd[0] -= 1
                if pnd[0] <= 0:
                    pending.remove(pnd)
                    pnd[1](sb, sbase)
            band = [t for t in range(tmin, nt) if bnds[t][0] <= kbi <= bnds[t][1]]
            for m in range(2):
                Kx, Qx = Kt[h % 2][m], Qt[h % 2][m]
                kb.op("pe", lambda e, m=m, Kx=Kx, Qx=Qx: e.matmul(
                    out=ps[:, sbase + m * 512 + c0:sbase + m * 512 + c1],
                    lhsT=Kx.ap[0:68, kbi * 128:(kbi + 1) * 128], rhs=Qx.ap[0:68, gc0 + c0:gc0 + c1],
                    start=True, stop=(len(band) == 0)), reads=[Kx, Qx, KtA[h % 2][m], QtA[h % 2][m]], writes=[sb])
                for t in band:
                    mi = (j0 + t) * NM + (kbi - bnds[t][0])
                    kb.op("pe", lambda e, m=m, t=t, mi=mi: e.matmul(
                        out=ps[:, sbase + m * 512 + t * 128:sbase + m * 512 + (t + 1) * 128],
                        lhsT=identb.ap, rhs=mk.ap[:, mi, :], start=False, stop=(t == band[-1])),
                        reads=[identb, mk], writes=[sb])
            kb.op("act", lambda e: e.activation(
                out=pt.ap[:, :, c0:c1], in_=ps[:, sbase:sbase + 1024].rearrange("p (m c) -> p m c", c=512)[:, :, c0:c1],
                func=AF.Exp, scale=0.125), reads=[sb], writes=[pt])

        def emit_pv(it):
            h, j0, nt, kbi, tmin, kb_last = it["h"], it["j0"], it["nt"], it["kbi"], it["tmin"], it["kb_last"]
            c0, c1 = tmin * 128, nt * 128
            gc0 = j0 * 128
            pt = PT[it["par"]]
            V = Vh[h % 2]
            for m in range(2):
                kb.op("pe", lambda e, m=m: e.matmul(out=bank(4 + m)[:, c0:c1], lhsT=V.ap[:, kbi, :], rhs=pt.ap[:, m, c0:c1],
                                                    start=(kbi == 0), stop=(kbi == kb_last)), reads=[V, pt], writes=[OB[m]])
                kb.op("pe", lambda e, m=m: e.matmul(out=bank(6 + m)[:, c0:c1], lhsT=onesb.ap, rhs=pt.ap[:, m, c0:c1],
                                                    start=(kbi == 0), stop=(kbi == kb_last)), reads=[onesb, pt], writes=[DB[m]])
            if kbi != kb_last:
                return
            N = nt * 128
            aout = ast[(h * len(AGROUPS) + it["gi"]) % 2]
            kb.op("act", lambda e: e.activation(out=rl.ap[:, :, 0:N], in_=bank(6, 2).rearrange("p (m c) -> p m c", c=512)[:, :, 0:N], func=AF.Ln),
                  reads=[DB[0], DB[1]], writes=[rl])
            kb.op("act", lambda e: e.activation(out=rl.ap[:, :, 0:N], in_=rl.ap[:, :, 0:N], func=AF.Exp, scale=-1.0),
                  reads=[rl], writes=[rl])
            kb.op("dve", lambda e: e.tensor_tensor(out=a1.ap[:, 0:N], in0=bank(4)[:, 0:N], in1=rl.ap[:, 0, 0:N], op=ALU.mult),
                  reads=[OB[0], rl], writes=[a1])
            kb.op("dve", lambda e: e.tensor_tensor(out=a2.ap[:, 0:N], in0=bank(5)[:, 0:N], in1=rl.ap[:, 1, 0:N], op=ALU.mult),
                  reads=[OB[1], rl], writes=[a2])
            kb.op("dve", lambda e: e.scalar_tensor_tensor(out=a1.ap[:, 0:N], in0=a2.ap[:, 0:N], scalar=neglam.ap,
                                                          in1=a1.ap[:, 0:N], op0=ALU.mult, op1=ALU.add),
                  reads=[a1, a2, neglam], writes=[a1])
            kb.op("dve", lambda e: e.tensor_tensor(out=sqb.ap[:, 0:N], in0=a1.ap[:, 0:N], in1=a1.ap[:, 0:N], op=ALU.mult),
                  reads=[a1], writes=[sqb])
            def tail(sb, sbase):
                kb.op("pe", lambda e: e.matmul(out=ps[:, sbase:sbase + N], lhsT=onesb.ap, rhs=sqb.ap[:, 0:N], start=True, stop=True),
                      reads=[onesb, sqb], writes=[sb])
                kb.op("act", lambda e: e.activation(out=rsa.ap[:, 0:N], in_=ps[:, sbase:sbase + N], func=AF.Ln, scale=1.0 / 128, bias=EPS),
                      reads=[sb], writes=[rsa])
                kb.op("act", lambda e: e.activation(out=rsa.ap[:, 0:N], in_=rsa.ap[:, 0:N], func=AF.Exp, scale=-0.5),
                      reads=[rsa], writes=[rsa])
                kb.op("dve", lambda e: e.scalar_tensor_tensor(out=aout.ap[:, 0:N], in0=a1.ap[:, 0:N], scalar=gsub.ap,
                                                              in1=rsa.ap[:, 0:N], op0=ALU.mult, op1=ALU.mult),
                      reads=[a1, gsub, rsa], writes=[aout])
                kb.dma("sp", lambda e: e.dma_start(out=A_s[h, :, gc0:gc0 + N], in_=aout.ap[:, 0:N]), reads=[aout], semtile=aout)

            pending.append([2, tail])
            if it["last_of_head"] and h + 2 < NH:
                head_loads(h + 2)

        head_loads(0)
        head_loads(1)
        if KF_A:
            for idx in range(len(items) + 1):
                if idx < len(items):
                    emit_scores(items[idx])
                if idx >= 1:
                    emit_pv(items[idx - 1])
            for pnd in pending:
                pnd[1](SB[0], 0)
        else:
            for it in items:
                emit_scores(it)
                emit_pv(it)
        kb.barrier()

        off[0] = persist_end
        g1 = carve("g1c", [1024], F32, dma=True)
        g3 = carve("g3c", [1024], F32, dma=True)
        load_g(g1, "g_mix_pre")
        load_g(g3, "g_mix_post")
        Wu = carve("Wu", [8, 512], BF16, dma=True)
        Wqm = carve("Wqm", [8, 512], BF16, dma=True)
        Wg = [carve(f"Wg{i}", [8, 1024], BF16, dma=True) for i in range(3)]
        Wat = carve("Wat", [8, 1024], BF16, dma=True)
        Wpw = carve("Wpw", [4, 128], BF16, dma=True)
        Wpb = carve("Wpb", [4, 1024], BF16, dma=True)
        Wmb = carve("Wmb", [4, 1024], BF16, dma=True)
        Wo = carve("Wo", [8, 1024], BF16, dma=True)
        load_w(Wu, wview(w_in, 3072, 3584))
        load_w(Wqm, wview(w_in, 3584, 4096))
        for i in range(3):
            load_w(Wg[i], wview(w_in, 4096 + i * 1024, 4096 + (i + 1) * 1024))
        load_w(Wat, wview(w_attn, 0, 1024))
        kb.dma("pool", lambda e: e.dma_start(out=Wpw.ap, in_=pool_w.rearrange("g c d -> c g d")), writes=[Wpw], semtile=Wpw)
        load_w(Wpb, wview(w_pb, 0, 1024))
        load_w(Wmb, wview(w_mb, 0, 1024))
        load_w(Wo, wview(w_out, 0, 1024))
        xt2 = [carve(f"xtc{i}", [1024], F32, dma=True) for i in range(2)]
        ht2 = [carve(f"htc{i}", [1024], BF16) for i in range(2)]
        ss2 = [carve(f"ssc{i}", [1], F32) for i in range(2)]
        rs2 = [carve(f"rsc{i}", [1], F32) for i in range(2)]
        hT = carve("hTc", [8, 256], BF16)
        hTh = carve("hThc", [8, 32], BF16)
        aT = carve("aTc", [8, 256], BF16, dma=True)
        ivc = carve("ivc", [4, 256], F32, dma=True)
        uext = carve("uext", [2, 144], F32)
        sA = carve("sA", [2, 144], F32)
        sB = carve("sB", [2, 144], F32)
        pp = carve("pp", [256], F32)
        pbf = carve("pbf", [256], BF16)
        ypT = carve("ypT", [4, 256], BF16)
        qmT = carve("qmT", [4, 256], BF16)
        PTm = carve("PTm", [2, 256], BF16)
        rlm = carve("rlm", [256], F32)
        omT = carve("omT", [4, 256], BF16)
        gat = carve("gat", [3, 512], F32)
        prod = carve("prod", [3, 512], F32)
        mixtok = carve("mixtok", [1024], BF16)
        mixT = carve("mixT", [8, 256], BF16)
        xm2 = [carve("xm0", [1024], F32, dma=True)] * 2
        xhalo = xt2[1]
        GB = kb.tile("GB", bank(0, 3))
        YB = kb.tile("YB", bank(3, 3))
        GTs = [kb.tile(f"GT{i}", ps[:, i * 1536:i * 1536 + 768]) for i in range(2)]
        YTs = [kb.tile(f"YT{i}", ps[:, i * 1536 + 768:i * 1536 + 1536]) for i in range(2)]
        tcount = 0
        for (j0, nt) in CGROUPS:
            N = nt * 128
            gc0 = j0 * 128
            kb.dma("sp", lambda e, gc0=gc0, N=N: e.dma_start(out=aT.ap[:, :, 0:N], in_=A_s[:, :, gc0:gc0 + N].rearrange("h e c -> e h c")),
                   writes=[aT], semtile=aT)
            kb.dma("sp", lambda e, gc0=gc0, N=N: e.dma_start(
                out=ivc.ap[:, :, 0:N], in_=invc[:, gc0:gc0 + N].partition_broadcast(128)), writes=[ivc], semtile=ivc)
            kb.dma("sp", lambda e, j0=j0, nt=nt: e.dma_start(out=xhalo.ap[0:nt * 16, :], in_=xh[j0 * 16:(j0 + nt) * 16, :]),
                   writes=[xhalo], semtile=xhalo)
            norm_tile(xhalo, nt * 16, g1, ht2[0], ss2[0], rs2[0])
            transpose_to(ht2[0], nt * 16, 6, hTh, hTh.ap[:, :, 0:nt * 16])
            for t in range(nt):
                n = j0 + t
                xt, ht, ss, rs = xt2[t % 2], ht2[1], ss2[1], rs2[1]
                kb.dma("sp", lambda e, xt=xt, n=n: e.dma_start(out=xt.ap, in_=xo[n * 128:(n + 1) * 128, :]),
                       writes=[xt], semtile=xt)
                norm_tile(xt, 128, g1, ht, ss, rs)
                transpose_to(ht, 128, 7, hT, hT.ap[:, :, t * 128:(t + 1) * 128])
            def pool_chain(g, N=N, nt=nt):
                w = 2 ** (g + 1)
                for kc in range(8):
                    kb.op("pe", lambda e, kc=kc: e.matmul(out=bank(0)[:, 0:N], lhsT=Wu.ap[:, kc, g * 128:(g + 1) * 128],
                                                          rhs=hT.ap[:, kc, 0:N], start=(kc == 0), stop=(kc == 7)),
                          reads=[Wu, hT], writes=[PB[0]])
                for kc in range(8):
                    kb.op("pe", lambda e, kc=kc: e.matmul(out=bank(1)[:, 0:nt * 16], lhsT=Wu.ap[:, kc, g * 128:(g + 1) * 128],
                                                          rhs=hTh.ap[:, kc, 0:nt * 16], start=(kc == 0), stop=(kc == 7)),
                          reads=[Wu, hTh], writes=[PB[1]])
                yield
                kb.op("act", lambda e: e.activation(out=uext.ap[:, 0:nt, 16:144],
                                                    in_=bank(0)[:, 0:N].rearrange("p (t c) -> p t c", c=128), func=AF.Copy),
                      reads=[PB[0]], writes=[uext])
                kb.op("dve", lambda e: e.tensor_copy(out=uext.ap[:, 0:nt, 0:16],
                                                     in_=bank(1)[:, 0:nt * 16].rearrange("p (t c) -> p t c", c=16)),
                      reads=[PB[1]], writes=[uext])
                yield
                cur = uext
                step = 1
                bufs = [sA, sB]
                bi = 0
                while step < w:
                    nxt = bufs[bi]
                    bi ^= 1
                    kb.op("pool", lambda e, cur=cur, nxt=nxt, step=step: e.tensor_tensor(
                        out=nxt.ap[:, 0:nt, step:144], in0=cur.ap[:, 0:nt, step:144], in1=cur.ap[:, 0:nt, 0:144 - step], op=ALU.add),
                        reads=[cur], writes=[nxt])
                    cur = nxt
                    step *= 2
                    yield
                kb.op("dve", lambda e, cur=cur: e.tensor_tensor(
                    out=pp.ap[:, 0:N].rearrange("p (t c) -> p t c", c=128), in0=cur.ap[:, 0:nt, 16:144],
                    in1=ivc.ap[:, g, 0:N].rearrange("p (t c) -> p t c", c=128), op=ALU.mult), reads=[cur, ivc], writes=[pp])
                kb.op("dve", lambda e: e.tensor_tensor(
                    out=pbf.ap[:, 0:N].rearrange("p (t c) -> p t c", c=128), in0=pp.ap[:, 0:N].rearrange("p (t c) -> p t c", c=128),
                    in1=uext.ap[:, 0:nt, 16:144], op=ALU.subtract), reads=[pp, uext], writes=[pbf])
                yield
                kb.op("pe", lambda e: e.matmul(out=bank(2)[:, 0:N], lhsT=Wpw.ap[:, g, :], rhs=pbf.ap[:, 0:N], start=True, stop=True),
                      reads=[Wpw, pbf], writes=[PB[2]])
                yield
                kb.op("dve", lambda e: e.tensor_scalar(out=ypT.ap[:, g, 0:N], in0=bank(2)[:, 0:N], scalar1=psc.ap[:, g:g + 1],
                                                       scalar2=None, op0=ALU.mult), reads=[PB[2], psc], writes=[ypT])

            def mem_chain(hd, N=N, nt=nt):
                for kc in range(8):
                    kb.op("pe", lambda e, kc=kc: e.matmul(out=bank(3)[:, 0:N], lhsT=Wqm.ap[:, kc, hd * 128:(hd + 1) * 128],
                                                          rhs=hT.ap[:, kc, 0:N], start=(kc == 0), stop=(kc == 7)),
                          reads=[Wqm, hT], writes=[PB[3]])
                yield
                evac(qmT.ap[:, hd, 0:N], bank(3)[:, 0:N], [PB[3]], [qmT])
                yield
                for ch in range(2):
                    kb.op("pe", lambda e, ch=ch: e.matmul(out=ps[:, 2048 + ch * 256:2048 + ch * 256 + N],
                                                          lhsT=KmT.ap[:, hd, ch * 128:(ch + 1) * 128], rhs=qmT.ap[:, hd, 0:N],
                                                          start=True, stop=True), reads=[KmT, qmT], writes=[PB[4]])
                yield
                kb.op("act", lambda e: e.activation(out=PTm.ap[:, :, 0:N], in_=bank(4).rearrange("p (a c) -> p a c", c=256)[:, :, 0:N],
                                                    func=AF.Exp, scale=128.0 ** -0.5), reads=[PB[4]], writes=[PTm])
                yield
                for ch in range(2):
                    kb.op("pe", lambda e, ch=ch: e.matmul(out=bank(5)[:, 0:N], lhsT=Vm.ap[:, ch, hd * 128:(hd + 1) * 128],
                                                          rhs=PTm.ap[:, ch, 0:N], start=(ch == 0), stop=(ch == 1)),
                          reads=[Vm, PTm], writes=[PB[5]])
                for ch in range(2):
                    kb.op("pe", lambda e, ch=ch: e.matmul(out=bank(5)[:, 256:256 + N], lhsT=onesb.ap, rhs=PTm.ap[:, ch, 0:N],
                                                          start=False if ch else True, stop=(ch == 1)),
                          reads=[onesb, PTm], writes=[PB[5]])
                yield
                kb.op("dve", lambda e: e.reciprocal(out=rlm.ap[:, 0:N], in_=bank(5)[:, 256:256 + N]), reads=[PB[5]], writes=[rlm])
                kb.op("dve", lambda e: e.tensor_tensor(out=omT.ap[:, hd, 0:N], in0=bank(5)[:, 0:N], in1=rlm.ap[:, 0:N], op=ALU.mult),
                      reads=[PB[5], rlm], writes=[omT])

            for k in range(4):
                gens = [pool_chain(k), mem_chain(k)]
                while gens:
                    for gen in list(gens):
                        try:
                            next(gen)
                        except StopIteration:
                            gens.remove(gen)
            for t in range(nt):
                tc0 = t * 128
                for b in range(2):
                    for br in range(3):
                        for kc in range(8):
                            kb.op("pe", lambda e, br=br, kc=kc, b=b, tc0=tc0: e.matmul(
                                out=bank(br), lhsT=hT.ap[:, kc, tc0:tc0 + 128], rhs=Wg[br].ap[:, kc, b * 512:(b + 1) * 512],
                                start=(kc == 0), stop=(kc == 7)), reads=[Wg[br], hT], writes=[PB[br]])
                    g3v = bank(0, 3).rearrange("p (a c) -> p a c", c=512)
                    kb.op("act", lambda e, g3v=g3v: e.activation(out=gat.ap, in_=g3v, func=AF.Exp, scale=-1.0), reads=[PB[0], PB[1], PB[2]], writes=[gat])
                    kb.op("act", lambda e: e.activation(out=gat.ap, in_=gat.ap, func=AF.Ln, scale=1.0, bias=1.0), reads=[gat], writes=[gat])
                    kb.op("act", lambda e: e.activation(out=gat.ap, in_=gat.ap, func=AF.Exp, scale=-1.0), reads=[gat], writes=[gat])
                    for hh in range(8):
                        kb.op("pe", lambda e, hh=hh, b=b, tc0=tc0: e.matmul(out=bank(3), lhsT=aT.ap[:, hh, tc0:tc0 + 128],
                                                                           rhs=Wat.ap[:, hh, b * 512:(b + 1) * 512],
                                                                           start=(hh == 0), stop=(hh == 7)), reads=[Wat, aT], writes=[PB[3]])
                    for g in range(4):
                        kb.op("pe", lambda e, g=g, b=b, tc0=tc0: e.matmul(out=bank(4), lhsT=ypT.ap[:, g, tc0:tc0 + 128],
                                                                         rhs=Wpb.ap[:, g, b * 512:(b + 1) * 512],
                                                                         start=(g == 0), stop=(g == 3)), reads=[Wpb, ypT], writes=[PB[4]])
                    for g in range(4):
                        kb.op("pe", lambda e, g=g, b=b, tc0=tc0: e.matmul(out=bank(5), lhsT=omT.ap[:, g, tc0:tc0 + 128],
                                                                         rhs=Wmb.ap[:, g, b * 512:(b + 1) * 512],
                                                                         start=(g == 0), stop=(g == 3)), reads=[Wmb, omT], writes=[PB[5]])
                    y3v = bank(3, 3).rearrange("p (a c) -> p a c", c=512)
                    kb.op("dve", lambda e, y3v=y3v: e.tensor_tensor(out=prod.ap, in0=y3v, in1=gat.ap, op=ALU.mult),
                          reads=[PB[3], PB[4], PB[5], gat], writes=[prod])
                    kb.op("pool", lambda e: e.tensor_tensor(out=prod.ap[:, 0, :], in0=prod.ap[:, 0, :], in1=prod.ap[:, 1, :], op=ALU.add),
                          reads=[prod], writes=[prod])
                    kb.op("pool", lambda e, b=b: e.tensor_tensor(out=mixtok.ap[:, b * 512:(b + 1) * 512], in0=prod.ap[:, 0, :],
                                                                 in1=prod.ap[:, 2, :], op=ALU.add), reads=[prod], writes=[mixtok])
                transpose_to(mixtok, 128, 7, mixT, mixT.ap[:, :, tc0:tc0 + 128])
            for t in range(nt):
                n = j0 + t
                xt = xt2[t % 2]
                xm = xm2[tcount % 2]
                tcount += 1
                ob = 2 * (t % 2)
                for half in range(2):
                    for kc in range(8):
                        kb.op("pe", lambda e, half=half, kc=kc, t=t, ob=ob: e.matmul(
                            out=bank(ob + half), lhsT=mixT.ap[:, kc, t * 128:(t + 1) * 128], rhs=Wo.ap[:, kc, half * 512:(half + 1) * 512],
                            start=(kc == 0), stop=(kc == 7)), reads=[mixT, Wo], writes=[PB[ob + half]])
                ss, rs = ss2[0], rs2[0]
                kb.op("act", lambda e, ss=ss, ob=ob: e.activation(out=prod.ap[:, 0:2, :], in_=bank(ob, 2).rearrange("p (a c) -> p a c", c=512),
                                                                func=AF.Square, accum_out=ss.ap),
                      reads=[PB[ob], PB[ob + 1]], writes=[prod, ss])
                rstd_from_ss(rs, ss, D)
                kb.op("dve", lambda e, xm=xm, rs=rs, ob=ob: e.scalar_tensor_tensor(out=xm.ap, in0=bank(ob, 2), scalar=rs.ap, in1=g3.ap,
                                                                            op0=ALU.mult, op1=ALU.mult),
                      reads=[PB[ob], PB[ob + 1], rs, g3], writes=[xm])
                kb.op("dve", lambda e, xm=xm, xt=xt: e.tensor_tensor(out=xm.ap, in0=xm.ap, in1=xt.ap, op=ALU.add),
                      reads=[xm, xt], writes=[xm])
                kb.dma("sp", lambda e, xm=xm, n=n: e.dma_start(out=XM_s[n * 128:(n + 1) * 128, :], in_=xm.ap), reads=[xm], semtile=xm)
        kb.barrier()

        off[0] = persist_ffn
        ACT_s = dscr("ACT_s", [NCH, 128, NTOK], BF16)
        g4 = carve("g4", [1024], F32, dma=True)
        load_g(g4, "g_ffn_pre")
        Wup = carve("Wup", [8, 2 * DFF], BF16, dma=True)
        for q4 in range(4):
            c0 = q4 * (2 * DFF // 4)
            kb.dma("pool", lambda e, c0=c0: e.dma_start(out=Wup.ap[:, :, c0:c0 + 2 * DFF // 4], in_=wview(w_up, c0, c0 + 2 * DFF // 4)),
                   writes=[Wup], semtile=Wup)
        xm2 = [carve(f"xmf{i}", [1024], F32, dma=True) for i in range(2)]
        h2s = [carve(f"h2{i}", [1024], BF16) for i in range(2)]
        ssfs = [carve(f"ssf{i}", [1], F32) for i in range(2)]
        rsfs = [carve(f"rsf{i}", [1], F32) for i in range(2)]
        h2Ts = [carve(f"h2T{i}", [8, 512], BF16) for i in range(2)]
        actTs = [carve("actT0", [NCH, 512], BF16, dma=True)] * 2
        cgs = [carve(f"cg{i}", [4, 126], F32) for i in range(4)]
        cvs = [carve(f"cv{i}", [4, 126], F32) for i in range(4)]
        z1s = [carve(f"z1{i}", [4, 126], F32) for i in range(4)]
        z2s = [carve(f"z2{i}", [4, 126], F32) for i in range(4)]
        kb.op("pool", lambda e, a0=actTs[0]: e.memset(a0.ap, 0.0), writes=[actTs[0]])
        UB = [kb.tile("UB0", bank(0, 2)), kb.tile("UB1", bank(2, 2))]
        ucount = 0

        def prep3(gi, t):
            j0, nt = AGROUPS[gi]
            n = j0 + t
            xm, h2, ssf, rsf = xm2[n % 2], h2s[n % 2], ssfs[n % 2], rsfs[n % 2]
            kb.dma("sp", lambda e: e.dma_start(out=xm.ap, in_=XM_s[n * 128:(n + 1) * 128, :]), writes=[xm], semtile=xm)
            norm_tile(xm, 128, g4, h2, ssf, rsf, vcol=vld.ap[:, n:n + 1])
            transpose_to(h2, 128, 4 + n % 2, h2Ts[gi % 2], h2Ts[gi % 2].ap[:, :, t * 128:(t + 1) * 128])

        for t in range(AGROUPS[0][1]):
            prep3(0, t)
        def ffn_s1(gi, c, ctx):
            j0, nt = AGROUPS[gi]
            N = nt * 128
            h2T = h2Ts[gi % 2]
            cg, cv = cgs[c % 4], cvs[c % 4]
            ub, ubase = UB[ctx["u"] % 2], (ctx["u"] % 2) * 1024
            ctx["u"] += 1
            for gv in range(2):
                col = gv * DFF + c * 128
                for kc in range(8):
                    kb.op("pe", lambda e, gv=gv, kc=kc, col=col: e.matmul(
                        out=ps[:, ubase + gv * 512:ubase + gv * 512 + N], lhsT=Wup.ap[:, kc, col:col + 128], rhs=h2T.ap[:, kc, 0:N],
                        start=(kc == 0), stop=(kc == 7)), reads=[Wup, h2T], writes=[ub])
            for gv, dst in ((0, cg), (1, cv)):
                ch = gv * NCH + c
                pv3 = ps[:, ubase + gv * 512:ubase + gv * 512 + N].rearrange("p (t c) -> p t c", c=128)
                kb.op("act", lambda e, dst=dst, pv3=pv3, ch=ch: e.activation(
                    out=dst.ap[:, 0:nt, :], in_=pv3[:, :, 2:128], func=AF.Identity,
                    scale=cw.ap[:, 2 * 2 * NCH + ch:2 * 2 * NCH + ch + 1], bias=cb.ap[:, ch:ch + 1]),
                    reads=[ub, cw, cb], writes=[dst])
                for jtap in (1, 0):
                    kb.op("dve", lambda e, dst=dst, pv3=pv3, ch=ch, jtap=jtap: e.scalar_tensor_tensor(
                        out=dst.ap[:, 0:nt, :], in0=pv3[:, :, jtap:jtap + 126],
                        scalar=cw.ap[:, jtap * 2 * NCH + ch:jtap * 2 * NCH + ch + 1], in1=dst.ap[:, 0:nt, :],
                        op0=ALU.mult, op1=ALU.add), reads=[ub, cw, dst], writes=[dst])

        def ffn_s2(gi, c):
            nt = AGROUPS[gi][1]
            cg, cv, z1, z2 = cgs[c % 4], cvs[c % 4], z1s[c % 4], z2s[c % 4]
            kb.op("act", lambda e: e.activation(out=z1.ap[:, 0:nt, :], in_=cg.ap[:, 0:nt, :], func=AF.Square), reads=[cg], writes=[z1])
            kb.op("pool", lambda e: e.tensor_scalar(out=z1.ap[:, 0:nt, :], in0=z1.ap[:, 0:nt, :], scalar1=0.044715, scalar2=1.0,
                                                    op0=ALU.mult, op1=ALU.add), reads=[z1], writes=[z1])
            kb.op("pool", lambda e: e.tensor_tensor(out=z1.ap[:, 0:nt, :], in0=z1.ap[:, 0:nt, :], in1=cg.ap[:, 0:nt, :], op=ALU.mult),
                  reads=[z1, cg], writes=[z1])
            kb.op("pool", lambda e: e.tensor_tensor(out=z2.ap[:, 0:nt, :], in0=cg.ap[:, 0:nt, :], in1=cv.ap[:, 0:nt, :], op=ALU.mult),
                  reads=[cg, cv], writes=[z2])
            kb.op("dve", lambda e: e.tensor_scalar(out=z1.ap[:, 0:nt, :], in0=z1.ap[:, 0:nt, :], scalar1=-20.0, scalar2=None, op0=ALU.max),
                  reads=[z1], writes=[z1])

        def ffn_s3(gi, c):
            nt = AGROUPS[gi][1]
            N = nt * 128
            actT = actTs[gi % 2]
            z1, z2 = z1s[c % 4], z2s[c % 4]
            kb.op("act", lambda e: e.activation(out=z1.ap[:, 0:nt, :], in_=z1.ap[:, 0:nt, :], func=AF.Exp, scale=-2.0 * GELU_C),
                  reads=[z1], writes=[z1])
            kb.op("act", lambda e: e.activation(out=z1.ap[:, 0:nt, :], in_=z1.ap[:, 0:nt, :], func=AF.Ln, scale=1.0, bias=1.0),
                  reads=[z1], writes=[z1])
            kb.op("act", lambda e: e.activation(out=z1.ap[:, 0:nt, :], in_=z1.ap[:, 0:nt, :], func=AF.Exp, scale=-1.0),
                  reads=[z1], writes=[z1])
            kb.op("dve", lambda e: e.tensor_tensor(
                out=actT.ap[:, c, 0:N].rearrange("p (t c) -> p t c", c=128)[:, :, 2:128], in0=z2.ap[:, 0:nt, :], in1=z1.ap[:, 0:nt, :],
                op=ALU.mult), reads=[z1, z2], writes=[actT])

        fctx = {"u": 0}
        for gi, (j0, nt) in enumerate(AGROUPS):
            N = nt * 128
            actT = actTs[gi % 2]
            for step in range(NCH + 2):
                if gi + 1 < len(AGROUPS) and step in (2, 7, 12, 17) and (step - 2) // 5 < AGROUPS[gi + 1][1]:
                    prep3(gi + 1, (step - 2) // 5)
                if step < NCH:
                    ffn_s1(gi, step, fctx)
                if 1 <= step <= NCH:
                    ffn_s2(gi, step - 1)
                if step >= 2:
                    ffn_s3(gi, step - 2)
            kb.dma("sp", lambda e, actT=actT, j0=j0, N=N: e.dma_start(out=ACT_s[:, :, j0 * 128:j0 * 128 + N].rearrange("c p n -> p c n"),
                                                                    in_=actT.ap[:, :, 0:N]), reads=[actT], semtile=actT)
        kb.barrier()

        off[0] = persist_ffn
        g5 = carve("g5", [1024], F32, dma=True)
        load_g(g5, "g_ffn_post")
        Wdn = carve("Wdn", [NCH, 1024], BF16, dma=True)
        load_w(Wdn, wview(w_down, 0, 1024))
        actTs = [carve(f"actTb{i}", [NCH, 512], BF16, dma=True) for i in range(2)]
        xm4 = [carve(f"xmg{i}", [1024], F32, dma=True) for i in range(4)]
        ost = [carve(f"ost{i}", [1024], F32, dma=True) for i in range(2)]
        ssf = carve("ssfb", [1], F32)
        rsf = carve("rsfb", [1], F32)
        ocount = 0

        def load3b(gi):
            j0, nt = AGROUPS[gi]
            N = nt * 128
            actT = actTs[gi % 2]
            kb.dma("sp", lambda e: e.dma_start(out=actT.ap[:, :, 0:N], in_=ACT_s[:, :, j0 * 128:j0 * 128 + N].rearrange("c p n -> p c n")),
                   writes=[actT], semtile=actT)

        load3b(0)
        for gi, (j0, nt) in enumerate(AGROUPS):
            if gi + 1 < len(AGROUPS):
                load3b(gi + 1)
            actT = actTs[gi % 2]
            for t in range(nt):
                n = j0 + t
                xm = xm4[n % 4]
                kb.dma("sp", lambda e, xm=xm, n=n: e.dma_start(out=xm.ap, in_=XM_s[n * 128:(n + 1) * 128, :]), writes=[xm], semtile=xm)
            for t in range(nt):
                n = j0 + t
                xm = xm4[n % 4]
                o = ost[ocount % 2]
                ob = 4 + 2 * (ocount % 2)
                ocount += 1
                for half in range(2):
                    for c in range(NCH):
                        kb.op("pe", lambda e, half=half, c=c, t=t, actT=actT, ob=ob: e.matmul(
                            out=bank(ob + half), lhsT=actT.ap[:, c, t * 128:(t + 1) * 128], rhs=Wdn.ap[:, c, half * 512:(half + 1) * 512],
                            start=(c == 0), stop=(c == NCH - 1)), reads=[actT, Wdn], writes=[PB[ob + half]])
                kb.op("act", lambda e, o=o, ob=ob: e.activation(out=o.ap, in_=bank(ob, 2), func=AF.Square, accum_out=ssf.ap),
                      reads=[PB[ob], PB[ob + 1]], writes=[o, ssf])
                rstd_from_ss(rsf, ssf, D)
                kb.op("dve", lambda e, o=o, ob=ob: e.scalar_tensor_tensor(out=o.ap, in0=bank(ob, 2), scalar=rsf.ap, in1=g5.ap, op0=ALU.mult, op1=ALU.mult),
                      reads=[PB[ob], PB[ob + 1], rsf, g5], writes=[o])
                kb.op("pool", lambda e, o=o, xm=xm: e.tensor_tensor(out=o.ap, in0=o.ap, in1=xm.ap, op=ALU.add), reads=[o, xm], writes=[o])
                kb.dma("sp", lambda e, o=o, n=n: e.dma_start(out=out[n * 128:(n + 1) * 128, :], in_=o.ap), reads=[o], semtile=o)
        kb.emit()
    return nc


def _core_consts(r):
    pos = np.zeros((NT, 128), np.int64)
    val = np.zeros((NT, 128), np.float32)
    for j in range(NT):
        p = ST * (tile_base(j) + r) - 2 + np.arange(128)
        val[j] = ((p >= 0) & (p < S)).astype(np.float32)
        pos[j] = np.clip(p, 0, S - 1)
    fpos = pos.reshape(-1)
    qaug = np.zeros((NH, 4, NTOK), np.float32)
    for h in range(NH):
        s8 = 8.0 * SLOPES[h]
        qaug[h, 0] = s8 * 128.0
        qaug[h, 1] = s8
        qaug[h, 2] = -s8 * 128.0 * (fpos // 128)
        qaug[h, 3] = -s8 * (fpos % 128)
    masks = np.zeros((128, NT, NM, 128), np.float32)
    for j in range(NT):
        lo, hi = tile_bounds(j)
        for mi in range(NM):
            kpos = (lo + mi) * 128 + np.arange(128)
            masks[:, j, mi, :] = np.where(kpos[:, None] <= pos[j][None, :], 0.0, -30000.0).astype(np.float32)
    invc = np.zeros((4, NTOK), np.float32)
    for g, w in enumerate((2, 4, 8, 16)):
        invc[g] = 1.0 / np.minimum(fpos + 1, w)
    return pos, val, qaug, masks.reshape(128, -1), invc


_PROGRAM = None


def kernel(x, mem, norm_mix_pre, w_in, lambda_q1, lambda_k1, lambda_q2, lambda_k2, subln_g,
           w_attn_branch, pool_w, pool_scale, w_pool_branch, norm_mem, w_mem_kv, w_mem_branch,
           w_out, norm_mix_post, norm_ffn_pre, w_up, conv_w, conv_b, w_down, norm_ffn_post):
    global _PROGRAM
    f = lambda a: np.ascontiguousarray(np.asarray(a, dtype=np.float32))
    x = f(x)
    mem = f(mem)
    if _PROGRAM is None:
        _PROGRAM = build_program()
    nc = _PROGRAM
    kaug = np.zeros((4, S), np.float32)
    kp = np.arange(S)
    kaug[0] = kp // 128
    kaug[1] = kp % 128
    kaug[2] = 1.0
    kaug[3] = 1.0
    cwv = f(conv_w)[0]
    convw = np.ascontiguousarray(cwv.reshape(3, 2 * NCH, 128).transpose(2, 0, 1).reshape(128, 3 * 2 * NCH))
    convb = np.ascontiguousarray(f(conv_b)[0].reshape(2 * NCH, 128).T)
    shared = {
        "w_in": f(w_in)[0], "w_attn": f(w_attn_branch)[0], "pool_w": f(pool_w)[0], "w_pb": f(w_pool_branch)[0],
        "w_mkv": f(w_mem_kv)[0], "w_mb": f(w_mem_branch)[0], "w_out": f(w_out)[0], "w_up": f(w_up)[0],
        "w_down": f(w_down)[0],
        "g_mix_pre": f(norm_mix_pre), "g_mem": f(norm_mem), "g_mix_post": f(norm_mix_post),
        "g_ffn_pre": f(norm_ffn_pre), "g_ffn_post": f(norm_ffn_post),
        "lamv": np.ascontiguousarray(np.concatenate([f(lambda_q1), f(lambda_k1), f(lambda_q2), f(lambda_k2)], 0)),
        "subln": np.ascontiguousarray(f(subln_g)[0].reshape(128, 1)),
        "pscale": np.ascontiguousarray(f(pool_scale)[0].reshape(4, 128).T),
        "convw": convw, "convb": convb, "ident": np.eye(128, dtype=np.float32), "kaug": kaug,
    }
    consts = [_core_consts(r) for r in range(4)]
    in_maps = []
    for c in range(8):
        b, r = c // 4, c % 4
        pos, val, qaug, masks, invc = consts[r]
        xo = x[b][pos.reshape(-1)]
        xo[val.reshape(-1) == 0] = 0.0
        hp = (ST * (np.array([tile_base(j) for j in range(NT)]) + r) - 2)[:, None] - 16 + np.arange(16)[None, :]
        hv = ((hp >= 0) & (hp < S)).astype(np.float32)
        xhh = x[b][np.clip(hp, 0, S - 1).reshape(-1)]
        xhh[hv.reshape(-1) == 0] = 0.0
        m = dict(shared)
        m.update({"xb": x[b], "xo": np.ascontiguousarray(xo), "xh": np.ascontiguousarray(xhh), "memb": mem[b],
                  "qaug": qaug, "masks": masks, "invc": invc, "valid": np.ascontiguousarray(val.T)})
        in_maps.append(m)
    res = run_bass_kernel_spmd(nc, in_maps, core_ids=list(range(8)))
    outp = np.zeros((2, S, D), np.float32)
    for c in range(8):
        b, r = c // 4, c % 4
        o = np.asarray(res.results[c]["out"]).reshape(NT, 128, D)
        for j in range(NT):
            p0 = ST * (tile_base(j) + r)
            n = min(ST, S - p0)
            if n > 0:
                outp[b, p0:p0 + n] = o[j, 2:2 + n]
    return outp
```

```python
import contextlib
import math
import os
KF_A = os.environ.get('KF_A', '1') == '1'
KF_B = os.environ.get('KF_B', '1') == '1'
KF_C = os.environ.get('KF_C', '0') == '1'
import numpy as np
import concourse.bass as bass
import concourse.mybir as mybir
from concourse.bass_utils import run_bass_kernel_spmd

F32 = mybir.dt.float32
BF16 = mybir.dt.bfloat16
AF = mybir.ActivationFunctionType
ALU = mybir.AluOpType

D = 1024
S = 8192
NT = 17
ST = 126
NTOK = NT * 128
NH = 8
DFF = 2816
NCH = DFF // 128
EPS = 1e-6
LAM_INIT = 0.8 - 0.6 * math.exp(0.0)
SLOPES = [2.0 ** (-(i + 1)) for i in range(NH)]
GELU_C = math.sqrt(2.0 / math.pi)
AGROUPS = [(0, 4), (4, 4), (8, 4), (12, 4), (16, 1)]
CGROUPS = [(2 * i, 2) for i in range(8)] + [(16, 1)]

COMPUTE = ("pe", "act", "dve", "pool")


def tile_base(j):
    return 4 * (j + 1) if j < NT - 1 else 0


def tile_bounds(j):
    i0, i1 = tile_base(j), tile_base(j) + 3
    p_lo = min(max(ST * i0 - 2, 0), S - 1)
    p_hi = min(max(ST * i1 + 125, 0), S - 1)
    return p_lo // 128, p_hi // 128


NM = max(tile_bounds(j)[1] - tile_bounds(j)[0] + 1 for j in range(NT))


class T:
    __slots__ = ("name", "ap", "w", "rd", "sem", "ndma")

    def __init__(self, name, ap, sem=None):
        self.name = name
        self.ap = ap
        self.w = None
        self.rd = {}
        self.sem = sem
        self.ndma = 0


class KB:
    def __init__(self, nc):
        self.nc = nc
        self.q = {e: [] for e in ("pe", "act", "dve", "pool", "sp")}
        self.cnt = {e: 0 for e in COMPUTE}
        self.waited = {e: {} for e in self.q}
        self.sems = {}
        self.dma_tiles = []
        for e in COMPUTE:
            self.sems[e] = nc.alloc_semaphore(name="s_" + e)

    def tile(self, name, ap, dma=False):
        t = T(name, ap, self.nc.alloc_semaphore(name="d_" + name) if dma else None)
        if dma:
            self.dma_tiles.append(t)
        return t

    def _need(self, eng, dep, waits):
        if dep is None:
            return
        if dep[0] == "c":
            _, e, idx = dep
            if e == eng and e == "pe":
                return
            key, val, sem = ("c", e), idx, self.sems[e]
        else:
            _, t, cnt = dep
            key, val, sem = ("d", id(t)), 16 * cnt, t.sem
        if self.waited[eng].get(key, 0) >= val:
            return
        self.waited[eng][key] = val
        waits.append((sem, val))

    def _deps(self, eng, reads, writes):
        waits = []
        for t in reads:
            self._need(eng, t.w, waits)
        for t in writes:
            self._need(eng, t.w, waits)
            for d in t.rd.values():
                self._need(eng, d, waits)
        return waits

    def op(self, eng, fn, reads=(), writes=()):
        waits = self._deps(eng, reads, writes)
        self.cnt[eng] += 1
        dep = ("c", eng, self.cnt[eng])
        self.q[eng].append((waits, fn, self.sems[eng], 1))
        for t in reads:
            t.rd[("c", eng)] = dep
        for t in writes:
            t.w = dep
            t.rd = {}

    def dma(self, queue, fn, reads=(), writes=(), semtile=None):
        waits = self._deps(queue, reads, writes)
        semtile.ndma += 1
        dep = ("d", semtile, semtile.ndma)
        self.q[queue].append((waits, fn, semtile.sem, 16))
        for t in reads:
            t.rd[("d", id(semtile))] = dep
        for t in writes:
            t.w = dep
            t.rd = {}

    def barrier(self):
        for eng in self.q:
            waits = []
            for e in COMPUTE:
                if self.cnt[e] > 0 and e != eng:
                    self._need(eng, ("c", e, self.cnt[e]), waits)
            for t in self.dma_tiles:
                if t.ndma > 0:
                    self._need(eng, ("d", t, t.ndma), waits)
            if waits:
                self.q[eng].append((waits, None, None, 0))

    def emit(self):
        self.barrier()
        names = {"pe": "tensor", "act": "scalar", "dve": "vector", "pool": "gpsimd", "sp": "sync"}
        with self.nc.Block() as block:
            for e in ("sp", "pe", "act", "dve", "pool"):
                def body(eng, lst=self.q[e]):
                    for waits, fn, sem, inc in lst:
                        for (s, v) in waits:
                            eng.wait_ge(s, v)
                        if fn is not None:
                            fn(eng).then_inc(sem, inc)
                getattr(block, names[e])(body)


ARENA_WORDS = 48448


def build_program(debug=False):
    nc = bass.Bass("TRN2", target_bir_lowering=False)

    def din(name, shape):
        return nc.dram_tensor(name, list(shape), F32, kind="ExternalInput").ap()

    xb = din("xb", [S, D])
    xo = din("xo", [NTOK, D])
    xh = din("xh", [NT * 16, D])
    memb = din("memb", [256, D])
    w_in = din("w_in", [D, 7168])
    w_attn = din("w_attn", [D, D])
    pool_w = din("pool_w", [4, 128, 128])
    w_pb = din("w_pb", [512, D])
    w_mkv = din("w_mkv", [D, D])
    w_mb = din("w_mb", [512, D])
    w_out = din("w_out", [D, D])
    w_up = din("w_up", [D, 2 * DFF])
    w_down = din("w_down", [DFF, D])
    vec = {n: din(n, [1, D]) for n in ("g_mix_pre", "g_mem", "g_mix_post", "g_ffn_pre", "g_ffn_post")}
    lamv = din("lamv", [4, 64])
    subln = din("subln", [128, 1])
    pscale = din("pscale", [128, 4])
    convw = din("convw", [128, 3 * 2 * NCH])
    convb = din("convb", [128, 2 * NCH])
    ident = din("ident", [128, 128])
    kaug = din("kaug", [4, S])
    qaug = din("qaug", [NH, 4, NTOK])
    masks = din("masks", [128, NT * NM * 128])
    invc = din("invc", [4, NTOK])
    valid = din("valid", [128, NT])
    out = nc.dram_tensor("out", [NTOK, D], F32, kind="ExternalOutput").ap()

    def dscr(name, shape, dt):
        return nc.dram_tensor(name, list(shape), dt, kind="ExternalOutput" if debug else "Internal").ap()

    KT_s = dscr("KT_s", [NH, 2, 64, S], BF16)
    V_s = dscr("V_s", [S, D], BF16)
    QT_s = dscr("QT_s", [NH, 2, 64, NTOK], BF16)
    A_s = dscr("A_s", [NH, 128, NTOK], BF16)
    XM_s = dscr("XM_s", [NTOK, D], F32)

    es = contextlib.ExitStack()
    with es:
        arena = es.enter_context(nc.sbuf_tensor("arena", [128, ARENA_WORDS], F32))
        ps = es.enter_context(nc.psum_tensor("ps", [128, 4096], F32))
        kb = KB(nc)
        off = [0]

        def carve(name, shape, dt, dma=False):
            n = int(np.prod(shape))
            nw = (n * (2 if dt == BF16 else 4) + 3) // 4
            assert off[0] + nw <= ARENA_WORDS, (name, off[0], nw)
            ap = arena[:, off[0]:off[0] + nw]
            off[0] += nw
            if dt != F32:
                ap = ap.bitcast(dt)
            if len(shape) == 2:
                ap = ap.rearrange("p (a b) -> p a b", b=shape[1])
            elif len(shape) == 3:
                ap = ap.rearrange("p (a b c) -> p a b c", b=shape[1], c=shape[2])
            return kb.tile(name, ap, dma=dma)

        def bank(k, n=1):
            return ps[:, k * 512:(k + n) * 512]

        PB = [kb.tile(f"bank{k}", bank(k)) for k in range(8)]

        identb = carve("identb", [128], BF16, dma=True)
        onesb = carve("onesb", [128], BF16)
        lam4 = carve("lam4", [4, 64], F32, dma=True)
        lamt = carve("lamt", [2, 64], F32)
        lams = carve("lams", [4], F32)
        neglam = carve("neglam", [1], F32)
        gsub = carve("gsub", [1], F32, dma=True)
        psc = carve("psc", [4], F32, dma=True)
        cw = carve("cw", [3 * 2 * NCH], F32, dma=True)
        cb = carve("cb", [2 * NCH], F32, dma=True)
        vld = carve("vld", [NT], F32, dma=True)
        persist_ffn = off[0]
        KmT = carve("KmT", [4, 256], BF16)
        Vm = carve("Vm", [2, 512], BF16)
        junk = carve("junk", [64], BF16)
        persist_end = off[0]

        kb.dma("pool", lambda e: e.dma_start(out=identb.ap, in_=ident), writes=[identb], semtile=identb)
        kb.op("pool", lambda e: e.memset(onesb.ap, 1.0), writes=[onesb])
        kb.dma("sp", lambda e: e.dma_start(out=lam4.ap,
                                            in_=lamv.partition_broadcast(128)),
               writes=[lam4], semtile=lam4)
        kb.dma("sp", lambda e: e.dma_start(out=gsub.ap, in_=subln), writes=[gsub], semtile=gsub)
        kb.dma("sp", lambda e: e.dma_start(out=psc.ap, in_=pscale), writes=[psc], semtile=psc)
        kb.dma("sp", lambda e: e.dma_start(out=cw.ap, in_=convw), writes=[cw], semtile=cw)
        kb.dma("sp", lambda e: e.dma_start(out=cb.ap, in_=convb), writes=[cb], semtile=cb)
        kb.dma("sp", lambda e: e.dma_start(out=vld.ap, in_=valid), writes=[vld], semtile=vld)
        kb.op("dve", lambda e: e.tensor_tensor(out=lamt.ap[:, 0, :], in0=lam4.ap[:, 0, :], in1=lam4.ap[:, 1, :], op=ALU.mult),
              reads=[lam4], writes=[lamt])
        kb.op("dve", lambda e: e.tensor_tensor(out=lamt.ap[:, 1, :], in0=lam4.ap[:, 2, :], in1=lam4.ap[:, 3, :], op=ALU.mult),
              reads=[lam4], writes=[lamt])
        for k in range(2):
            kb.op("act", lambda e, k=k: e.activation(out=junk.ap[:, 0:64], in_=lamt.ap[:, k, :], func=AF.Copy,
                                                     accum_out=lams.ap[:, k:k + 1]),
                  reads=[lamt], writes=[junk, lams])
        kb.op("act", lambda e: e.activation(out=lams.ap[:, 2:4], in_=lams.ap[:, 0:2], func=AF.Exp), reads=[lams], writes=[lams])
        kb.op("dve", lambda e: e.tensor_tensor(out=neglam.ap, in0=lams.ap[:, 3:4], in1=lams.ap[:, 2:3], op=ALU.subtract),
              reads=[lams], writes=[neglam])
        kb.op("dve", lambda e: e.tensor_scalar(out=neglam.ap, in0=neglam.ap, scalar1=-LAM_INIT, scalar2=None, op0=ALU.add),
              reads=[neglam], writes=[neglam])
        kb.op("dve", lambda e: e.tensor_scalar(out=gsub.ap, in0=gsub.ap, scalar1=1.0 - LAM_INIT, scalar2=None, op0=ALU.mult),
              reads=[gsub], writes=[gsub])

        rr = [0]

        def evac(out_ap, in_ap, reads, writes, scale=None):
            rr[0] += 1
            if rr[0] % 2 == 0:
                kb.op("dve", lambda e: e.tensor_copy(out=out_ap, in_=in_ap), reads=reads, writes=writes)
            else:
                kb.op("act", lambda e: e.activation(out=out_ap, in_=in_ap, func=AF.Copy), reads=reads, writes=writes)

        def load_w(t, src_ap):
            kb.dma("pool", lambda e: e.dma_start(out=t.ap, in_=src_ap), writes=[t], semtile=t)

        def wview(w, c0, c1):
            return w.rearrange("(kc p) n -> p kc n", p=128)[:, :, c0:c1]

        def rstd_from_ss(rs, ss, n, extra_reads=()):
            kb.op("act", lambda e: e.activation(out=rs.ap, in_=ss.ap, func=AF.Ln, scale=1.0 / n, bias=EPS),
                  reads=[ss], writes=[rs])
            kb.op("act", lambda e: e.activation(out=rs.ap, in_=rs.ap, func=AF.Exp, scale=-0.5), reads=[rs], writes=[rs])

        def norm_tile(xt, rows, gt, ht, ss, rs, vcol=None):
            kb.op("act", lambda e: e.activation(out=ht.ap[0:rows, :], in_=xt.ap[0:rows, :], func=AF.Square,
                                                accum_out=ss.ap[0:rows, :]),
                  reads=[xt], writes=[ht, ss])
            rstd_from_ss(rs, ss, D)
            if vcol is not None:
                kb.op("dve", lambda e: e.tensor_tensor(out=rs.ap, in0=rs.ap, in1=vcol, op=ALU.mult), reads=[rs, vld], writes=[rs])
            kb.op("dve", lambda e: e.scalar_tensor_tensor(out=ht.ap[0:rows, :], in0=xt.ap[0:rows, :], scalar=rs.ap[0:rows, :],
                                                          in1=gt.ap[0:rows, :], op0=ALU.mult, op1=ALU.mult),
                  reads=[xt, rs, gt], writes=[ht])

        def transpose_to(ht, rows, pbank_idx, dst_tile, dst_ap3):
            pb = PB[pbank_idx]
            pv = bank(pbank_idx).bitcast(BF16)
            for kc in range(8):
                kb.op("pe", lambda e, kc=kc: e.transpose(out=pv[:, kc * 128:kc * 128 + rows],
                                                         in_=ht.ap[0:rows, kc * 128:(kc + 1) * 128],
                                                         identity=identb.ap[0:rows, 0:rows]),
                      reads=[ht, identb], writes=[pb])
            src = pv.rearrange("p (k c) -> p k c", c=128)[:, :, 0:rows]
            evac(dst_ap3, src, [pb], [dst_tile])

        def load_g(t, name):
            kb.dma("sp", lambda e: e.dma_start(out=t.ap, in_=vec[name].partition_broadcast(128)),
                   writes=[t], semtile=t)

        off[0] = persist_end
        g1 = carve("g1", [1024], F32, dma=True)
        g2 = carve("g2", [1024], F32, dma=True)
        load_g(g1, "g_mix_pre")
        load_g(g2, "g_mem")
        Wkv = carve("Wkv", [8, 2048], BF16, dma=True)
        Wm = carve("Wm", [8, 1024], BF16, dma=True)
        load_w(Wkv, wview(w_in, 1024, 3072))
        load_w(Wm, wview(w_mkv, 0, 1024))
        xt2 = [carve(f"xt{i}", [1024], F32, dma=True) for i in range(2)]
        ht2 = [carve(f"ht{i}", [1024], BF16) for i in range(2)]
        ss2 = [carve(f"ss{i}", [1], F32) for i in range(2)]
        rs2 = [carve(f"rs{i}", [1], F32) for i in range(2)]
        hT2 = [carve(f"hT{i}", [8, 512], BF16) for i in range(2)]
        kst2 = [carve(f"kst{i}", [8, 512], BF16, dma=True) for i in range(2)]
        vst2 = [carve(f"vst{i}", [4, 1024], BF16, dma=True) for i in range(2)]

        hTm = hT2[0]
        for blk in range(2):
            xt, ht, ss, rs = xt2[blk], ht2[blk], ss2[blk], rs2[blk]
            kb.dma("sp", lambda e, xt=xt, blk=blk: e.dma_start(out=xt.ap, in_=memb[blk * 128:(blk + 1) * 128, :]),
                   writes=[xt], semtile=xt)
            norm_tile(xt, 128, g2, ht, ss, rs)
            transpose_to(ht, 128, blk, hTm, hTm.ap[:, :, blk * 128:(blk + 1) * 128])
        for hd in range(4):
            for kc in range(8):
                kb.op("pe", lambda e, hd=hd, kc=kc: e.matmul(out=bank(2 + hd % 2)[:, 0:256],
                                                             lhsT=Wm.ap[:, kc, hd * 128:(hd + 1) * 128],
                                                             rhs=hTm.ap[:, kc, 0:256], start=(kc == 0), stop=(kc == 7)),
                      reads=[Wm, hTm], writes=[PB[2 + hd % 2]])
            evac(KmT.ap[:, hd, :], bank(2 + hd % 2)[:, 0:256], [PB[2 + hd % 2]], [KmT])
        for ch in range(2):
            for kc in range(8):
                kb.op("pe", lambda e, ch=ch, kc=kc: e.matmul(out=bank(4 + ch), lhsT=hTm.ap[:, kc, ch * 128:(ch + 1) * 128],
                                                             rhs=Wm.ap[:, kc, 512:1024], start=(kc == 0), stop=(kc == 7)),
                      reads=[Wm, hTm], writes=[PB[4 + ch]])
            evac(Vm.ap[:, ch, :], bank(4 + ch), [PB[4 + ch]], [Vm])

        def prep1(G, blk):
            hT = hT2[G % 2]
            n = G * 4 + blk
            xt, ht, ss, rs = xt2[n % 2], ht2[n % 2], ss2[n % 2], rs2[n % 2]
            kb.dma("sp", lambda e: e.dma_start(out=xt.ap, in_=xb[n * 128:(n + 1) * 128, :]), writes=[xt], semtile=xt)
            norm_tile(xt, 128, g1, ht, ss, rs)

        def prep1t(G, blk):
            hT = hT2[G % 2]
            n = G * 4 + blk
            transpose_to(ht2[n % 2], 128, n % 2, hT, hT.ap[:, :, blk * 128:(blk + 1) * 128])

        def kpart(G, heads):
            hT, kst = hT2[G % 2], kst2[G % 2]
            for h in heads:
                bk = 2 + h % 2
                for kc in range(8):
                    kb.op("pe", lambda e, h=h, kc=kc, bk=bk: e.matmul(out=bank(bk), lhsT=Wkv.ap[:, kc, h * 128:(h + 1) * 128],
                                                                     rhs=hT.ap[:, kc, :], start=(kc == 0), stop=(kc == 7)),
                          reads=[Wkv, hT], writes=[PB[bk]])
                evac(kst.ap[:, h, :], bank(bk), [PB[bk]], [kst])
            if heads[-1] == NH - 1:
                for m in range(2):
                    kb.dma("sp", lambda e, m=m: e.dma_start(
                        out=KT_s[:, m, :, G * 512:(G + 1) * 512].rearrange("h d c -> d h c"),
                        in_=kst.ap[m * 64:(m + 1) * 64, :, :]), reads=[kst], semtile=kst)

        def vpart(G, blks):
            hT, vst = hT2[G % 2], vst2[G % 2]
            for blk in blks:
                for half in range(2):
                    bk = 4 + (blk * 2 + half) % 4
                    for kc in range(8):
                        kb.op("pe", lambda e, blk=blk, half=half, kc=kc, bk=bk: e.matmul(
                            out=bank(bk), lhsT=hT.ap[:, kc, blk * 128:(blk + 1) * 128],
                            rhs=Wkv.ap[:, kc, 1024 + half * 512:1024 + (half + 1) * 512], start=(kc == 0), stop=(kc == 7)),
                            reads=[Wkv, hT], writes=[PB[bk]])
                    evac(vst.ap[:, blk, half * 512:(half + 1) * 512], bank(bk), [PB[bk]], [vst])
            if blks[-1] == 3:
                kb.dma("sp", lambda e: e.dma_start(
                    out=V_s[G * 512:(G + 1) * 512, :].rearrange("(b p) c -> p b c", p=128), in_=vst.ap),
                    reads=[vst], semtile=vst)

        for blk in range(4):
            prep1(0, blk)
            prep1t(0, blk)
        for G in range(16):
            parts = [lambda: kpart(G, [0, 1, 2, 3]), lambda: kpart(G, [4, 5, 6, 7]), lambda: vpart(G, [0, 1]), lambda: vpart(G, [2, 3])]
            for p in range(4):
                if G + 1 < 16:
                    prep1(G + 1, p)
                parts[p]()
                if G + 1 < 16:
                    prep1t(G + 1, p)
        kb.barrier()

        off[0] = persist_end
        g1 = carve("g1b", [1024], F32, dma=True)
        load_g(g1, "g_mix_pre")
        Wq = carve("Wq", [8, 1024], BF16, dma=True)
        load_w(Wq, wview(w_in, 0, 1024))
        xt2 = [carve(f"xtb{i}", [1024], F32, dma=True) for i in range(2)]
        ht2 = [carve(f"htb{i}", [1024], BF16) for i in range(2)]
        ss2 = [carve(f"ssb{i}", [1], F32) for i in range(2)]
        rs2 = [carve(f"rsb{i}", [1], F32) for i in range(2)]
        hT2 = [carve(f"hTb{i}", [8, 512], BF16) for i in range(2)]
        qst2 = [carve(f"qst{i}", [8, 512], BF16, dma=True) for i in range(2)]
        def prep2a(gi, t):
            j0, nt = AGROUPS[gi]
            hT = hT2[gi % 2]
            n = j0 + t
            xt, ht, ss, rs = xt2[n % 2], ht2[n % 2], ss2[n % 2], rs2[n % 2]
            kb.dma("sp", lambda e: e.dma_start(out=xt.ap, in_=xo[n * 128:(n + 1) * 128, :]), writes=[xt], semtile=xt)
            norm_tile(xt, 128, g1, ht, ss, rs)

        def prep2at(gi, t):
            j0, nt = AGROUPS[gi]
            hT = hT2[gi % 2]
            n = j0 + t
            transpose_to(ht2[n % 2], 128, n % 2, hT, hT.ap[:, :, t * 128:(t + 1) * 128])

        def qpart(gi, heads):
            j0, nt = AGROUPS[gi]
            hT, qst = hT2[gi % 2], qst2[gi % 2]
            N = nt * 128
            for h in heads:
                bk = 2 + h % 4
                for kc in range(8):
                    kb.op("pe", lambda e, h=h, kc=kc, bk=bk: e.matmul(
                        out=bank(bk)[:, 0:N], lhsT=Wq.ap[:, kc, h * 128:(h + 1) * 128], rhs=hT.ap[:, kc, 0:N],
                        start=(kc == 0), stop=(kc == 7)), reads=[Wq, hT], writes=[PB[bk]])
                evac(qst.ap[:, h, 0:N], bank(bk)[:, 0:N], [PB[bk]], [qst])
            if heads[-1] == NH - 1:
                for m in range(2):
                    kb.dma("sp", lambda e, m=m: e.dma_start(
                        out=QT_s[:, m, :, j0 * 128:j0 * 128 + N].rearrange("h d c -> d h c"),
                        in_=qst.ap[m * 64:(m + 1) * 64, :, 0:N]), reads=[qst], semtile=qst)

        for t in range(AGROUPS[0][1]):
            prep2a(0, t)
            prep2at(0, t)
        for gi in range(len(AGROUPS)):
            for p in range(4):
                nxt = gi + 1 < len(AGROUPS) and p < AGROUPS[gi + 1][1]
                if nxt:
                    prep2a(gi + 1, p)
                qpart(gi, [2 * p, 2 * p + 1])
                if nxt:
                    prep2at(gi + 1, p)
        kb.barrier()

        off[0] = persist_end
        Kt = [[carve(f"Kt{b}{m}", [S], BF16, dma=True) for m in range(2)] for b in range(2)]
        Qt = [[carve(f"Qt{b}{m}", [NTOK], BF16, dma=True) for m in range(2)] for b in range(2)]
        Vh = [carve(f"Vh{b}", [64, 128], BF16, dma=True) for b in range(2)]
        mk = carve("mk", [NT * NM, 128], BF16, dma=True)
        PT = [carve(f"PT{b}", [2, 512], BF16) for b in range(2)]
        rl = carve("rl", [2, 512], F32)
        a1 = carve("a1", [512], F32)
        o1s = carve("o1s", [512], F32)
        o2s = carve("o2s", [512], F32)
        a2 = carve("a2", [512], F32)
        sqb = carve("sqb", [512], BF16)
        rsa = carve("rsa", [512], F32)
        ast = [carve(f"ast{b}", [512], BF16, dma=True) for b in range(2)]
        kb.dma("pool", lambda e: e.dma_start(out=mk.ap, in_=masks.rearrange("p (a b) -> p a b", b=128)), writes=[mk], semtile=mk)
        KtA = [[kb.tile(f"KtA{b}{m}", Kt[b][m].ap[64:68, :], dma=True) for m in range(2)] for b in range(2)]
        QtA = [[kb.tile(f"QtA{b}{m}", Qt[b][m].ap[64:68, :], dma=True) for m in range(2)] for b in range(2)]
        for b in range(2):
            for m in range(2):
                kb.dma("pool", lambda e, b=b, m=m: e.dma_start(out=Kt[b][m].ap[64:68, :].rearrange("a (b c) -> a b c", c=2048), in_=kaug.rearrange("a (b c) -> a b c", c=2048)),
                       writes=[KtA[b][m]], semtile=KtA[b][m])
        SB = [kb.tile("SB0", bank(0, 2)), kb.tile("SB1", bank(2, 2))]
        OB = [PB[4], PB[5]]
        DB = [PB[6], PB[7]]
        def head_loads(h):
            bsel = h % 2
            for m in range(2):
                kb.dma("sp", lambda e, h=h, m=m, bsel=bsel: e.dma_start(out=Kt[bsel][m].ap[0:64, :], in_=KT_s[h, m, :, :]),
                       writes=[Kt[bsel][m]], semtile=Kt[bsel][m])
                kb.dma("sp", lambda e, h=h, m=m, bsel=bsel: e.dma_start(out=Qt[bsel][m].ap[0:64, :], in_=QT_s[h, m, :, :]),
                       writes=[Qt[bsel][m]], semtile=Qt[bsel][m])
                kb.dma("pool", lambda e, h=h, m=m, bsel=bsel: e.dma_start(
                    out=Qt[bsel][m].ap[64:68, :].rearrange("a (b c) -> a b c", c=1088),
                    in_=qaug[h, :, :].rearrange("a (b c) -> a b c", c=1088)), writes=[QtA[bsel][m]], semtile=QtA[bsel][m])
            kb.dma("sp", lambda e, h=h, bsel=bsel: e.dma_start(
                out=Vh[bsel].ap, in_=V_s.rearrange("(kb p) c -> p kb c", p=128)[:, :, h * 128:(h + 1) * 128]),
                writes=[Vh[bsel]], semtile=Vh[bsel])

        items = []
        for h in range(NH):
            for gi, (j0, nt) in enumerate(AGROUPS):
                bnds = [tile_bounds(j0 + t) for t in range(nt)]
                kb_last = bnds[-1][1]
                for kbi in range(kb_last + 1):
                    tmin = min(t for t in range(nt) if bnds[t][1] >= kbi)
                    items.append(dict(h=h, gi=gi, j0=j0, nt=nt, bnds=bnds, kb_last=kb_last, kbi=kbi, tmin=tmin,
                                      last_of_head=(gi == len(AGROUPS) - 1 and kbi == kb_last)))
        for idx, it in enumerate(items):
            it["par"] = idx % 2

        pending = []

        def emit_scores(it):
            h, j0, nt, bnds, kbi, tmin = it["h"], it["j0"], it["nt"], it["bnds"], it["kbi"], it["tmin"]
            c0, c1 = tmin * 128, nt * 128
            gc0 = j0 * 128
            sb, pt, sbase = SB[it["par"]], PT[it["par"]], it["par"] * 1024
            for pnd in list(pending):
                pnd[0] -= 1
                if pnd[0] <= 0:
                    pending.remove(pnd)
                    pnd[1](sb, sbase)
            band = [t for t in range(tmin, nt) if bnds[t][0] <= kbi <= bnds[t][1]]
            for m in range(2):
                Kx, Qx = Kt[h % 2][m], Qt[h % 2][m]
                kb.op("pe", lambda e, m=m, Kx=Kx, Qx=Qx: e.matmul(
                    out=ps[:, sbase + m * 512 + c0:sbase + m * 512 + c1],
                    lhsT=Kx.ap[0:68, kbi * 128:(kbi + 1) * 128], rhs=Qx.ap[0:68, gc0 + c0:gc0 + c1],
                    start=True, stop=(len(band) == 0)), reads=[Kx, Qx, KtA[h % 2][m], QtA[h % 2][m]], writes=[sb])
                for t in band:
                    mi = (j0 + t) * NM + (kbi - bnds[t][0])
                    kb.op("pe", lambda e, m=m, t=t, mi=mi: e.matmul(
                        out=ps[:, sbase + m * 512 + t * 128:sbase + m * 512 + (t + 1) * 128],
                        lhsT=identb.ap, rhs=mk.ap[:, mi, :], start=False, stop=(t == band[-1])),
                        reads=[identb, mk], writes=[sb])
            kb.op("act", lambda e: e.activation(
                out=pt.ap[:, :, c0:c1], in_=ps[:, sbase:sbase + 1024].rearrange("p (m c) -> p m c", c=512)[:, :, c0:c1],
                func=AF.Exp, scale=0.125), reads=[sb], writes=[pt])

        def emit_pv(it):
            h, j0, nt, kbi, tmin, kb_last = it["h"], it["j0"], it["nt"], it["kbi"], it["tmin"], it["kb_last"]
            c0, c1 = tmin * 128, nt * 128
            gc0 = j0 * 128
            pt = PT[it["par"]]
            V = Vh[h % 2]
            for m in range(2):
                kb.op("pe", lambda e, m=m: e.matmul(out=bank(4 + m)[:, c0:c1], lhsT=V.ap[:, kbi, :], rhs=pt.ap[:, m, c0:c1],
                                                    start=(kbi == 0), stop=(kbi == kb_last)), reads=[V, pt], writes=[OB[m]])
                kb.op("pe", lambda e, m=m: e.matmul(out=bank(6 + m)[:, c0:c1], lhsT=onesb.ap, rhs=pt.ap[:, m, c0:c1],
                                                    start=(kbi == 0), stop=(kbi == kb_last)), reads=[onesb, pt], writes=[DB[m]])
            if kbi != kb_last:
                return
            N = nt * 128
            aout = ast[(h * len(AGROUPS) + it["gi"]) % 2]
            kb.op("act", lambda e: e.activation(out=rl.ap[:, :, 0:N], in_=bank(6, 2).rearrange("p (m c) -> p m c", c=512)[:, :, 0:N], func=AF.Ln),
                  reads=[DB[0], DB[1]], writes=[rl])
            kb.op("act", lambda e: e.activation(out=rl.ap[:, :, 0:N], in_=rl.ap[:, :, 0:N], func=AF.Exp, scale=-1.0),
                  reads=[rl], writes=[rl])
            kb.op("dve", lambda e: e.tensor_copy(out=o1s.ap[:, 0:N], in_=bank(4)[:, 0:N]), reads=[OB[0]], writes=[o1s])
            kb.op("dve", lambda e: e.tensor_copy(out=o2s.ap[:, 0:N], in_=bank(5)[:, 0:N]), reads=[OB[1]], writes=[o2s])
            kb.op("dve", lambda e: e.tensor_tensor(out=a1.ap[:, 0:N], in0=o1s.ap[:, 0:N], in1=rl.ap[:, 0, 0:N], op=ALU.mult),
                  reads=[o1s, rl], writes=[a1])
            kb.op("dve", lambda e: e.tensor_tensor(out=a2.ap[:, 0:N], in0=o2s.ap[:, 0:N], in1=rl.ap[:, 1, 0:N], op=ALU.mult),
                  reads=[o2s, rl], writes=[a2])
            kb.op("dve", lambda e: e.scalar_tensor_tensor(out=a1.ap[:, 0:N], in0=a2.ap[:, 0:N], scalar=neglam.ap,
                                                          in1=a1.ap[:, 0:N], op0=ALU.mult, op1=ALU.add),
                  reads=[a1, a2, neglam], writes=[a1])
            kb.op("dve", lambda e: e.tensor_tensor(out=sqb.ap[:, 0:N], in0=a1.ap[:, 0:N], in1=a1.ap[:, 0:N], op=ALU.mult),
                  reads=[a1], writes=[sqb])
            def tail(sb, sbase):
                kb.op("pe", lambda e: e.matmul(out=ps[:, sbase:sbase + N], lhsT=onesb.ap, rhs=sqb.ap[:, 0:N], start=True, stop=True),
                      reads=[onesb, sqb], writes=[sb])
                kb.op("act", lambda e: e.activation(out=rsa.ap[:, 0:N], in_=ps[:, sbase:sbase + N], func=AF.Ln, scale=1.0 / 128, bias=EPS),
                      reads=[sb], writes=[rsa])
                kb.op("act", lambda e: e.activation(out=rsa.ap[:, 0:N], in_=rsa.ap[:, 0:N], func=AF.Exp, scale=-0.5),
                      reads=[rsa], writes=[rsa])
                kb.op("dve", lambda e: e.scalar_tensor_tensor(out=aout.ap[:, 0:N], in0=a1.ap[:, 0:N], scalar=gsub.ap,
                                                              in1=rsa.ap[:, 0:N], op0=ALU.mult, op1=ALU.mult),
                      reads=[a1, gsub, rsa], writes=[aout])
                kb.dma("sp", lambda e: e.dma_start(out=A_s[h, :, gc0:gc0 + N], in_=aout.ap[:, 0:N]), reads=[aout], semtile=aout)

            pending.append([2, tail])
            if it["last_of_head"] and h + 2 < NH:
                head_loads(h + 2)

        head_loads(0)
        head_loads(1)
        if KF_A:
            for idx in range(len(items) + 1):
                if idx < len(items):
                    emit_scores(items[idx])
                if idx >= 1:
                    emit_pv(items[idx - 1])
            for pnd in pending:
                pnd[1](SB[0], 0)
        else:
            for it in items:
                emit_scores(it)
                emit_pv(it)
        kb.barrier()

        off[0] = persist_end
        g1 = carve("g1c", [1024], F32, dma=True)
        g3 = carve("g3c", [1024], F32, dma=True)
        load_g(g1, "g_mix_pre")
        load_g(g3, "g_mix_post")
        Wu = carve("Wu", [8, 512], BF16, dma=True)
        Wqm = carve("Wqm", [8, 512], BF16, dma=True)
        Wg = [carve(f"Wg{i}", [8, 1024], BF16, dma=True) for i in range(3)]
        Wat = carve("Wat", [8, 1024], BF16, dma=True)
        Wpw = carve("Wpw", [4, 128], BF16, dma=True)
        Wpb = carve("Wpb", [4, 1024], BF16, dma=True)
        Wmb = carve("Wmb", [4, 1024], BF16, dma=True)
        Wo = carve("Wo", [8, 1024], BF16, dma=True)
        load_w(Wu, wview(w_in, 3072, 3584))
        load_w(Wqm, wview(w_in, 3584, 4096))
        for i in range(3):
            load_w(Wg[i], wview(w_in, 4096 + i * 1024, 4096 + (i + 1) * 1024))
        load_w(Wat, wview(w_attn, 0, 1024))
        kb.dma("pool", lambda e: e.dma_start(out=Wpw.ap, in_=pool_w.rearrange("g c d -> c g d")), writes=[Wpw], semtile=Wpw)
        load_w(Wpb, wview(w_pb, 0, 1024))
        load_w(Wmb, wview(w_mb, 0, 1024))
        load_w(Wo, wview(w_out, 0, 1024))
        xt2 = [carve(f"xtc{i}", [1024], F32, dma=True) for i in range(2)]
        ht2 = [carve(f"htc{i}", [1024], BF16) for i in range(2)]
        ss2 = [carve(f"ssc{i}", [1], F32) for i in range(2)]
        rs2 = [carve(f"rsc{i}", [1], F32) for i in range(2)]
        hT = carve("hTc", [8, 256], BF16)
        hTh = carve("hThc", [8, 32], BF16)
        aT = carve("aTc", [8, 256], BF16, dma=True)
        ivc = carve("ivc", [4, 256], F32, dma=True)
        uext = carve("uext", [2, 144], F32)
        sA = carve("sA", [2, 144], F32)
        sB = carve("sB", [2, 144], F32)
        pp = carve("pp", [256], F32)
        pbf = carve("pbf", [256], BF16)
        ypT = carve("ypT", [4, 256], BF16)
        qmT = carve("qmT", [4, 256], BF16)
        PTm = carve("PTm", [2, 256], BF16)
        rlm = carve("rlm", [256], F32)
        omT = carve("omT", [4, 256], BF16)
        gat = carve("gat", [3, 512], F32)
        prod = carve("prod", [3, 512], F32)
        mixtok = carve("mixtok", [1024], BF16)
        mixT = carve("mixT", [8, 256], BF16)
        xm2 = [carve("xm0", [1024], F32, dma=True)] * 2
        xhalo = xt2[1]
        GB = kb.tile("GB", bank(0, 3))
        YB = kb.tile("YB", bank(3, 3))
        GTs = [kb.tile(f"GT{i}", ps[:, i * 1536:i * 1536 + 768]) for i in range(2)]
        YTs = [kb.tile(f"YT{i}", ps[:, i * 1536 + 768:i * 1536 + 1536]) for i in range(2)]
        tcount = 0
        for (j0, nt) in CGROUPS:
            N = nt * 128
            gc0 = j0 * 128
            kb.dma("sp", lambda e, gc0=gc0, N=N: e.dma_start(out=aT.ap[:, :, 0:N], in_=A_s[:, :, gc0:gc0 + N].rearrange("h e c -> e h c")),
                   writes=[aT], semtile=aT)
            kb.dma("sp", lambda e, gc0=gc0, N=N: e.dma_start(
                out=ivc.ap[:, :, 0:N], in_=invc[:, gc0:gc0 + N].partition_broadcast(128)), writes=[ivc], semtile=ivc)
            kb.dma("sp", lambda e, j0=j0, nt=nt: e.dma_start(out=xhalo.ap[0:nt * 16, :], in_=xh[j0 * 16:(j0 + nt) * 16, :]),
                   writes=[xhalo], semtile=xhalo)
            norm_tile(xhalo, nt * 16, g1, ht2[0], ss2[0], rs2[0])
            transpose_to(ht2[0], nt * 16, 6, hTh, hTh.ap[:, :, 0:nt * 16])
            for t in range(nt):
                n = j0 + t
                xt, ht, ss, rs = xt2[t % 2], ht2[1], ss2[1], rs2[1]
                kb.dma("sp", lambda e, xt=xt, n=n: e.dma_start(out=xt.ap, in_=xo[n * 128:(n + 1) * 128, :]),
                       writes=[xt], semtile=xt)
                norm_tile(xt, 128, g1, ht, ss, rs)
                transpose_to(ht, 128, 7, hT, hT.ap[:, :, t * 128:(t + 1) * 128])
            def pool_chain(g, N=N, nt=nt):
                w = 2 ** (g + 1)
                for kc in range(8):
                    kb.op("pe", lambda e, kc=kc: e.matmul(out=bank(0)[:, 0:N], lhsT=Wu.ap[:, kc, g * 128:(g + 1) * 128],
                                                          rhs=hT.ap[:, kc, 0:N], start=(kc == 0), stop=(kc == 7)),
                          reads=[Wu, hT], writes=[PB[0]])
                for kc in range(8):
                    kb.op("pe", lambda e, kc=kc: e.matmul(out=bank(1)[:, 0:nt * 16], lhsT=Wu.ap[:, kc, g * 128:(g + 1) * 128],
                                                          rhs=hTh.ap[:, kc, 0:nt * 16], start=(kc == 0), stop=(kc == 7)),
                          reads=[Wu, hTh], writes=[PB[1]])
                yield
                kb.op("act", lambda e: e.activation(out=uext.ap[:, 0:nt, 16:144],
                                                    in_=bank(0)[:, 0:N].rearrange("p (t c) -> p t c", c=128), func=AF.Copy),
                      reads=[PB[0]], writes=[uext])
                kb.op("dve", lambda e: e.tensor_copy(out=uext.ap[:, 0:nt, 0:16],
                                                     in_=bank(1)[:, 0:nt * 16].rearrange("p (t c) -> p t c", c=16)),
                      reads=[PB[1]], writes=[uext])
                yield
                cur = uext
                step = 1
                bufs = [sA, sB]
                bi = 0
                while step < w:
                    nxt = bufs[bi]
                    bi ^= 1
                    kb.op("pool", lambda e, cur=cur, nxt=nxt, step=step: e.tensor_tensor(
                        out=nxt.ap[:, 0:nt, step:144], in0=cur.ap[:, 0:nt, step:144], in1=cur.ap[:, 0:nt, 0:144 - step], op=ALU.add),
                        reads=[cur], writes=[nxt])
                    cur = nxt
                    step *= 2
                    yield
                kb.op("dve", lambda e, cur=cur: e.tensor_tensor(
                    out=pp.ap[:, 0:N].rearrange("p (t c) -> p t c", c=128), in0=cur.ap[:, 0:nt, 16:144],
                    in1=ivc.ap[:, g, 0:N].rearrange("p (t c) -> p t c", c=128), op=ALU.mult), reads=[cur, ivc], writes=[pp])
                kb.op("dve", lambda e: e.tensor_tensor(
                    out=pbf.ap[:, 0:N].rearrange("p (t c) -> p t c", c=128), in0=pp.ap[:, 0:N].rearrange("p (t c) -> p t c", c=128),
                    in1=uext.ap[:, 0:nt, 16:144], op=ALU.subtract), reads=[pp, uext], writes=[pbf])
                yield
                kb.op("pe", lambda e: e.matmul(out=bank(2)[:, 0:N], lhsT=Wpw.ap[:, g, :], rhs=pbf.ap[:, 0:N], start=True, stop=True),
                      reads=[Wpw, pbf], writes=[PB[2]])
                yield
                kb.op("dve", lambda e: e.tensor_scalar(out=ypT.ap[:, g, 0:N], in0=bank(2)[:, 0:N], scalar1=psc.ap[:, g:g + 1],
                                                       scalar2=None, op0=ALU.mult), reads=[PB[2], psc], writes=[ypT])

            def mem_chain(hd, N=N, nt=nt):
                for kc in range(8):
                    kb.op("pe", lambda e, kc=kc: e.matmul(out=bank(3)[:, 0:N], lhsT=Wqm.ap[:, kc, hd * 128:(hd + 1) * 128],
                                                          rhs=hT.ap[:, kc, 0:N], start=(kc == 0), stop=(kc == 7)),
                          reads=[Wqm, hT], writes=[PB[3]])
                yield
                evac(qmT.ap[:, hd, 0:N], bank(3)[:, 0:N], [PB[3]], [qmT])
                yield
                for ch in range(2):
                    kb.op("pe", lambda e, ch=ch: e.matmul(out=ps[:, 2048 + ch * 256:2048 + ch * 256 + N],
                                                          lhsT=KmT.ap[:, hd, ch * 128:(ch + 1) * 128], rhs=qmT.ap[:, hd, 0:N],
                                                          start=True, stop=True), reads=[KmT, qmT], writes=[PB[4]])
                yield
                kb.op("act", lambda e: e.activation(out=PTm.ap[:, :, 0:N], in_=bank(4).rearrange("p (a c) -> p a c", c=256)[:, :, 0:N],
                                                    func=AF.Exp, scale=128.0 ** -0.5), reads=[PB[4]], writes=[PTm])
                yield
                for ch in range(2):
                    kb.op("pe", lambda e, ch=ch: e.matmul(out=bank(5)[:, 0:N], lhsT=Vm.ap[:, ch, hd * 128:(hd + 1) * 128],
                                                          rhs=PTm.ap[:, ch, 0:N], start=(ch == 0), stop=(ch == 1)),
                          reads=[Vm, PTm], writes=[PB[5]])
                for ch in range(2):
                    kb.op("pe", lambda e, ch=ch: e.matmul(out=bank(5)[:, 256:256 + N], lhsT=onesb.ap, rhs=PTm.ap[:, ch, 0:N],
                                                          start=False if ch else True, stop=(ch == 1)),
                          reads=[onesb, PTm], writes=[PB[5]])
                yield
                kb.op("dve", lambda e: e.reciprocal(out=rlm.ap[:, 0:N], in_=bank(5)[:, 256:256 + N]), reads=[PB[5]], writes=[rlm])
                kb.op("dve", lambda e: e.tensor_tensor(out=omT.ap[:, hd, 0:N], in0=bank(5)[:, 0:N], in1=rlm.ap[:, 0:N], op=ALU.mult),
                      reads=[PB[5], rlm], writes=[omT])

            for k in range(4):
                gens = [pool_chain(k), mem_chain(k)]
                while gens:
                    for gen in list(gens):
                        try:
                            next(gen)
                        except StopIteration:
                            gens.remove(gen)
            for t in range(nt):
                tc0 = t * 128
                for b in range(2):
                    for br in range(3):
                        for kc in range(8):
                            kb.op("pe", lambda e, br=br, kc=kc, b=b, tc0=tc0: e.matmul(
                                out=bank(br), lhsT=hT.ap[:, kc, tc0:tc0 + 128], rhs=Wg[br].ap[:, kc, b * 512:(b + 1) * 512],
                                start=(kc == 0), stop=(kc == 7)), reads=[Wg[br], hT], writes=[PB[br]])
                    g3v = bank(0, 3).rearrange("p (a c) -> p a c", c=512)
                    kb.op("act", lambda e, g3v=g3v: e.activation(out=gat.ap, in_=g3v, func=AF.Exp, scale=-1.0), reads=[PB[0], PB[1], PB[2]], writes=[gat])
                    kb.op("act", lambda e: e.activation(out=gat.ap, in_=gat.ap, func=AF.Ln, scale=1.0, bias=1.0), reads=[gat], writes=[gat])
                    kb.op("act", lambda e: e.activation(out=gat.ap, in_=gat.ap, func=AF.Exp, scale=-1.0), reads=[gat], writes=[gat])
                    for hh in range(8):
                        kb.op("pe", lambda e, hh=hh, b=b, tc0=tc0: e.matmul(out=bank(3), lhsT=aT.ap[:, hh, tc0:tc0 + 128],
                                                                           rhs=Wat.ap[:, hh, b * 512:(b + 1) * 512],
                                                                           start=(hh == 0), stop=(hh == 7)), reads=[Wat, aT], writes=[PB[3]])
                    for g in range(4):
                        kb.op("pe", lambda e, g=g, b=b, tc0=tc0: e.matmul(out=bank(4), lhsT=ypT.ap[:, g, tc0:tc0 + 128],
                                                                         rhs=Wpb.ap[:, g, b * 512:(b + 1) * 512],
                                                                         start=(g == 0), stop=(g == 3)), reads=[Wpb, ypT], writes=[PB[4]])
                    for g in range(4):
                        kb.op("pe", lambda e, g=g, b=b, tc0=tc0: e.matmul(out=bank(5), lhsT=omT.ap[:, g, tc0:tc0 + 128],
                                                                         rhs=Wmb.ap[:, g, b * 512:(b + 1) * 512],
                                                                         start=(g == 0), stop=(g == 3)), reads=[Wmb, omT], writes=[PB[5]])
                    y3v = bank(3, 3).rearrange("p (a c) -> p a c", c=512)
                    kb.op("dve", lambda e, y3v=y3v: e.tensor_tensor(out=prod.ap, in0=y3v, in1=gat.ap, op=ALU.mult),
                          reads=[PB[3], PB[4], PB[5], gat], writes=[prod])
                    kb.op("pool", lambda e: e.tensor_tensor(out=prod.ap[:, 0, :], in0=prod.ap[:, 0, :], in1=prod.ap[:, 1, :], op=ALU.add),
                          reads=[prod], writes=[prod])
                    kb.op("pool", lambda e, b=b: e.tensor_tensor(out=mixtok.ap[:, b * 512:(b + 1) * 512], in0=prod.ap[:, 0, :],
                                                                 in1=prod.ap[:, 2, :], op=ALU.add), reads=[prod], writes=[mixtok])
                transpose_to(mixtok, 128, 7, mixT, mixT.ap[:, :, tc0:tc0 + 128])
            for t in range(nt):
                n = j0 + t
                xt = xt2[t % 2]
                xm = xm2[tcount % 2]
                tcount += 1
                ob = 2 * (t % 2)
                for half in range(2):
                    for kc in range(8):
                        kb.op("pe", lambda e, half=half, kc=kc, t=t, ob=ob: e.matmul(
                            out=bank(ob + half), lhsT=mixT.ap[:, kc, t * 128:(t + 1) * 128], rhs=Wo.ap[:, kc, half * 512:(half + 1) * 512],
                            start=(kc == 0), stop=(kc == 7)), reads=[mixT, Wo], writes=[PB[ob + half]])
                ss, rs = ss2[0], rs2[0]
                kb.op("act", lambda e, ss=ss, ob=ob: e.activation(out=prod.ap[:, 0:2, :], in_=bank(ob, 2).rearrange("p (a c) -> p a c", c=512),
                                                                func=AF.Square, accum_out=ss.ap),
                      reads=[PB[ob], PB[ob + 1]], writes=[prod, ss])
                rstd_from_ss(rs, ss, D)
                kb.op("dve", lambda e, xm=xm, rs=rs, ob=ob: e.scalar_tensor_tensor(out=xm.ap, in0=bank(ob, 2), scalar=rs.ap, in1=g3.ap,
                                                                            op0=ALU.mult, op1=ALU.mult),
                      reads=[PB[ob], PB[ob + 1], rs, g3], writes=[xm])
                kb.op("dve", lambda e, xm=xm, xt=xt: e.tensor_tensor(out=xm.ap, in0=xm.ap, in1=xt.ap, op=ALU.add),
                      reads=[xm, xt], writes=[xm])
                kb.dma("sp", lambda e, xm=xm, n=n: e.dma_start(out=XM_s[n * 128:(n + 1) * 128, :], in_=xm.ap), reads=[xm], semtile=xm)
        kb.barrier()

        off[0] = persist_ffn
        ACT_s = dscr("ACT_s", [NCH, 128, NTOK], BF16)
        g4 = carve("g4", [1024], F32, dma=True)
        load_g(g4, "g_ffn_pre")
        Wup = carve("Wup", [8, 2 * DFF], BF16, dma=True)
        for q4 in range(4):
            c0 = q4 * (2 * DFF // 4)
            kb.dma("pool", lambda e, c0=c0: e.dma_start(out=Wup.ap[:, :, c0:c0 + 2 * DFF // 4], in_=wview(w_up, c0, c0 + 2 * DFF // 4)),
                   writes=[Wup], semtile=Wup)
        xm2 = [carve(f"xmf{i}", [1024], F32, dma=True) for i in range(2)]
        h2s = [carve(f"h2{i}", [1024], BF16) for i in range(2)]
        ssfs = [carve(f"ssf{i}", [1], F32) for i in range(2)]
        rsfs = [carve(f"rsf{i}", [1], F32) for i in range(2)]
        h2Ts = [carve(f"h2T{i}", [8, 512], BF16) for i in range(2)]
        actTs = [carve("actT0", [NCH, 512], BF16, dma=True)] * 2
        cgs = [carve(f"cg{i}", [4, 126], F32) for i in range(4)]
        cvs = [carve(f"cv{i}", [4, 126], F32) for i in range(4)]
        z1s = [carve(f"z1{i}", [4, 126], F32) for i in range(4)]
        z2s = [carve(f"z2{i}", [4, 126], F32) for i in range(4)]
        kb.op("pool", lambda e, a0=actTs[0]: e.memset(a0.ap, 0.0), writes=[actTs[0]])
        UB = [kb.tile("UB0", bank(0, 2)), kb.tile("UB1", bank(2, 2))]
        ucount = 0

        def prep3(gi, t):
            j0, nt = AGROUPS[gi]
            n = j0 + t
            xm, h2, ssf, rsf = xm2[n % 2], h2s[n % 2], ssfs[n % 2], rsfs[n % 2]
            kb.dma("sp", lambda e: e.dma_start(out=xm.ap, in_=XM_s[n * 128:(n + 1) * 128, :]), writes=[xm], semtile=xm)
            norm_tile(xm, 128, g4, h2, ssf, rsf, vcol=vld.ap[:, n:n + 1])
            transpose_to(h2, 128, 4 + n % 2, h2Ts[gi % 2], h2Ts[gi % 2].ap[:, :, t * 128:(t + 1) * 128])

        for t in range(AGROUPS[0][1]):
            prep3(0, t)
        def ffn_s1(gi, c, ctx):
            j0, nt = AGROUPS[gi]
            N = nt * 128
            h2T = h2Ts[gi % 2]
            cg, cv = cgs[c % 4], cvs[c % 4]
            ub, ubase = UB[ctx["u"] % 2], (ctx["u"] % 2) * 1024
            ctx["u"] += 1
            for gv in range(2):
                col = gv * DFF + c * 128
                for kc in range(8):
                    kb.op("pe", lambda e, gv=gv, kc=kc, col=col: e.matmul(
                        out=ps[:, ubase + gv * 512:ubase + gv * 512 + N], lhsT=Wup.ap[:, kc, col:col + 128], rhs=h2T.ap[:, kc, 0:N],
                        start=(kc == 0), stop=(kc == 7)), reads=[Wup, h2T], writes=[ub])
            for gv, dst in ((0, cg), (1, cv)):
                ch = gv * NCH + c
                pv3 = ps[:, ubase + gv * 512:ubase + gv * 512 + N].rearrange("p (t c) -> p t c", c=128)
                kb.op("act", lambda e, dst=dst, pv3=pv3, ch=ch: e.activation(
                    out=dst.ap[:, 0:nt, :], in_=pv3[:, :, 2:128], func=AF.Identity,
                    scale=cw.ap[:, 2 * 2 * NCH + ch:2 * 2 * NCH + ch + 1], bias=cb.ap[:, ch:ch + 1]),
                    reads=[ub, cw, cb], writes=[dst])
                for jtap in (1, 0):
                    kb.op("dve", lambda e, dst=dst, pv3=pv3, ch=ch, jtap=jtap: e.scalar_tensor_tensor(
                        out=dst.ap[:, 0:nt, :], in0=pv3[:, :, jtap:jtap + 126],
                        scalar=cw.ap[:, jtap * 2 * NCH + ch:jtap * 2 * NCH + ch + 1], in1=dst.ap[:, 0:nt, :],
                        op0=ALU.mult, op1=ALU.add), reads=[ub, cw, dst], writes=[dst])

        def ffn_s2(gi, c):
            nt = AGROUPS[gi][1]
            cg, cv, z1, z2 = cgs[c % 4], cvs[c % 4], z1s[c % 4], z2s[c % 4]
            kb.op("act", lambda e: e.activation(out=z1.ap[:, 0:nt, :], in_=cg.ap[:, 0:nt, :], func=AF.Square), reads=[cg], writes=[z1])
            kb.op("pool", lambda e: e.tensor_scalar(out=z1.ap[:, 0:nt, :], in0=z1.ap[:, 0:nt, :], scalar1=0.044715, scalar2=1.0,
                                                    op0=ALU.mult, op1=ALU.add), reads=[z1], writes=[z1])
            kb.op("pool", lambda e: e.tensor_tensor(out=z1.ap[:, 0:nt, :], in0=z1.ap[:, 0:nt, :], in1=cg.ap[:, 0:nt, :], op=ALU.mult),
                  reads=[z1, cg], writes=[z1])
            kb.op("pool", lambda e: e.tensor_tensor(out=z2.ap[:, 0:nt, :], in0=cg.ap[:, 0:nt, :], in1=cv.ap[:, 0:nt, :], op=ALU.mult),
                  reads=[cg, cv], writes=[z2])
            kb.op("dve", lambda e: e.tensor_scalar(out=z1.ap[:, 0:nt, :], in0=z1.ap[:, 0:nt, :], scalar1=-20.0, scalar2=None, op0=ALU.max),
                  reads=[z1], writes=[z1])

        def ffn_s3(gi, c):
            nt = AGROUPS[gi][1]
            N = nt * 128
            actT = actTs[gi % 2]
            z1, z2 = z1s[c % 4], z2s[c % 4]
            kb.op("act", lambda e: e.activation(out=z1.ap[:, 0:nt, :], in_=z1.ap[:, 0:nt, :], func=AF.Exp, scale=-2.0 * GELU_C),
                  reads=[z1], writes=[z1])
            kb.op("act", lambda e: e.activation(out=z1.ap[:, 0:nt, :], in_=z1.ap[:, 0:nt, :], func=AF.Ln, scale=1.0, bias=1.0),
                  reads=[z1], writes=[z1])
            kb.op("act", lambda e: e.activation(out=z1.ap[:, 0:nt, :], in_=z1.ap[:, 0:nt, :], func=AF.Exp, scale=-1.0),
                  reads=[z1], writes=[z1])
            kb.op("dve", lambda e: e.tensor_tensor(
                out=actT.ap[:, c, 0:N].rearrange("p (t c) -> p t c", c=128)[:, :, 2:128], in0=z2.ap[:, 0:nt, :], in1=z1.ap[:, 0:nt, :],
                op=ALU.mult), reads=[z1, z2], writes=[actT])

        fctx = {"u": 0}
        for gi, (j0, nt) in enumerate(AGROUPS):
            N = nt * 128
            actT = actTs[gi % 2]
            for step in range(NCH + 2):
                if gi + 1 < len(AGROUPS) and step in (2, 7, 12, 17) and (step - 2) // 5 < AGROUPS[gi + 1][1]:
                    prep3(gi + 1, (step - 2) // 5)
                if step < NCH:
                    ffn_s1(gi, step, fctx)
                if 1 <= step <= NCH:
                    ffn_s2(gi, step - 1)
                if step >= 2:
                    ffn_s3(gi, step - 2)
            kb.dma("sp", lambda e, actT=actT, j0=j0, N=N: e.dma_start(out=ACT_s[:, :, j0 * 128:j0 * 128 + N].rearrange("c p n -> p c n"),
                                                                    in_=actT.ap[:, :, 0:N]), reads=[actT], semtile=actT)
        kb.barrier()

        off[0] = persist_ffn
        g5 = carve("g5", [1024], F32, dma=True)
        load_g(g5, "g_ffn_post")
        Wdn = carve("Wdn", [NCH, 1024], BF16, dma=True)
        load_w(Wdn, wview(w_down, 0, 1024))
        actTs = [carve(f"actTb{i}", [NCH, 512], BF16, dma=True) for i in range(2)]
        xm4 = [carve(f"xmg{i}", [1024], F32, dma=True) for i in range(4)]
        ost = [carve(f"ost{i}", [1024], F32, dma=True) for i in range(2)]
        ssf = carve("ssfb", [1], F32)
        rsf = carve("rsfb", [1], F32)
        ocount = 0

        def load3b(gi):
            j0, nt = AGROUPS[gi]
            N = nt * 128
            actT = actTs[gi % 2]
            kb.dma("sp", lambda e: e.dma_start(out=actT.ap[:, :, 0:N], in_=ACT_s[:, :, j0 * 128:j0 * 128 + N].rearrange("c p n -> p c n")),
                   writes=[actT], semtile=actT)

        load3b(0)
        for gi, (j0, nt) in enumerate(AGROUPS):
            if gi + 1 < len(AGROUPS):
                load3b(gi + 1)
            actT = actTs[gi % 2]
            for t in range(nt):
                n = j0 + t
                xm = xm4[n % 4]
                kb.dma("sp", lambda e, xm=xm, n=n: e.dma_start(out=xm.ap, in_=XM_s[n * 128:(n + 1) * 128, :]), writes=[xm], semtile=xm)
            for t in range(nt):
                n = j0 + t
                xm = xm4[n % 4]
                o = ost[ocount % 2]
                ob = 4 + 2 * (ocount % 2)
                ocount += 1
                for half in range(2):
                    for c in range(NCH):
                        kb.op("pe", lambda e, half=half, c=c, t=t, actT=actT, ob=ob: e.matmul(
                            out=bank(ob + half), lhsT=actT.ap[:, c, t * 128:(t + 1) * 128], rhs=Wdn.ap[:, c, half * 512:(half + 1) * 512],
                            start=(c == 0), stop=(c == NCH - 1)), reads=[actT, Wdn], writes=[PB[ob + half]])
                kb.op("act", lambda e, o=o, ob=ob: e.activation(out=o.ap, in_=bank(ob, 2), func=AF.Square, accum_out=ssf.ap),
                      reads=[PB[ob], PB[ob + 1]], writes=[o, ssf])
                rstd_from_ss(rsf, ssf, D)
                kb.op("dve", lambda e, o=o, ob=ob: e.scalar_tensor_tensor(out=o.ap, in0=bank(ob, 2), scalar=rsf.ap, in1=g5.ap, op0=ALU.mult, op1=ALU.mult),
                      reads=[PB[ob], PB[ob + 1], rsf, g5], writes=[o])
                kb.op("pool", lambda e, o=o, xm=xm: e.tensor_tensor(out=o.ap, in0=o.ap, in1=xm.ap, op=ALU.add), reads=[o, xm], writes=[o])
                kb.dma("sp", lambda e, o=o, n=n: e.dma_start(out=out[n * 128:(n + 1) * 128, :], in_=o.ap), reads=[o], semtile=o)
        kb.emit()
    return nc


def _core_consts(r):
    pos = np.zeros((NT, 128), np.int64)
    val = np.zeros((NT, 128), np.float32)
    for j in range(NT):
        p = ST * (tile_base(j) + r) - 2 + np.arange(128)
        val[j] = ((p >= 0) & (p < S)).astype(np.float32)
        pos[j] = np.clip(p, 0, S - 1)
    fpos = pos.reshape(-1)
    qaug = np.zeros((NH, 4, NTOK), np.float32)
    for h in range(NH):
        s8 = 8.0 * SLOPES[h]
        qaug[h, 0] = s8 * 128.0
        qaug[h, 1] = s8
        qaug[h, 2] = -s8 * 128.0 * (fpos // 128)
        qaug[h, 3] = -s8 * (fpos % 128)
    masks = np.zeros((128, NT, NM, 128), np.float32)
    for j in range(NT):
        lo, hi = tile_bounds(j)
        for mi in range(NM):
            kpos = (lo + mi) * 128 + np.arange(128)
            masks[:, j, mi, :] = np.where(kpos[:, None] <= pos[j][None, :], 0.0, -30000.0).astype(np.float32)
    invc = np.zeros((4, NTOK), np.float32)
    for g, w in enumerate((2, 4, 8, 16)):
        invc[g] = 1.0 / np.minimum(fpos + 1, w)
    return pos, val, qaug, masks.reshape(128, -1), invc


_PROGRAM = None


def kernel(x, mem, norm_mix_pre, w_in, lambda_q1, lambda_k1, lambda_q2, lambda_k2, subln_g,
           w_attn_branch, pool_w, pool_scale, w_pool_branch, norm_mem, w_mem_kv, w_mem_branch,
           w_out, norm_mix_post, norm_ffn_pre, w_up, conv_w, conv_b, w_down, norm_ffn_post):
    global _PROGRAM
    f = lambda a: np.ascontiguousarray(np.asarray(a, dtype=np.float32))
    x = f(x)
    mem = f(mem)
    if _PROGRAM is None:
        _PROGRAM = build_program()
    nc = _PROGRAM
    kaug = np.zeros((4, S), np.float32)
    kp = np.arange(S)
    kaug[0] = kp // 128
    kaug[1] = kp % 128
    kaug[2] = 1.0
    kaug[3] = 1.0
    cwv = f(conv_w)[0]
    convw = np.ascontiguousarray(cwv.reshape(3, 2 * NCH, 128).transpose(2, 0, 1).reshape(128, 3 * 2 * NCH))
    convb = np.ascontiguousarray(f(conv_b)[0].reshape(2 * NCH, 128).T)
    shared = {
        "w_in": f(w_in)[0], "w_attn": f(w_attn_branch)[0], "pool_w": f(pool_w)[0], "w_pb": f(w_pool_branch)[0],
        "w_mkv": f(w_mem_kv)[0], "w_mb": f(w_mem_branch)[0], "w_out": f(w_out)[0], "w_up": f(w_up)[0],
        "w_down": f(w_down)[0],
        "g_mix_pre": f(norm_mix_pre), "g_mem": f(norm_mem), "g_mix_post": f(norm_mix_post),
        "g_ffn_pre": f(norm_ffn_pre), "g_ffn_post": f(norm_ffn_post),
        "lamv": np.ascontiguousarray(np.concatenate([f(lambda_q1), f(lambda_k1), f(lambda_q2), f(lambda_k2)], 0)),
        "subln": np.ascontiguousarray(f(subln_g)[0].reshape(128, 1)),
        "pscale": np.ascontiguousarray(f(pool_scale)[0].reshape(4, 128).T),
        "convw": convw, "convb": convb, "ident": np.eye(128, dtype=np.float32), "kaug": kaug,
    }
    consts = [_core_consts(r) for r in range(4)]
    in_maps = []
    for c in range(8):
        b, r = c // 4, c % 4
        pos, val, qaug, masks, invc = consts[r]
        xo = x[b][pos.reshape(-1)]
        xo[val.reshape(-1) == 0] = 0.0
        hp = (ST * (np.array([tile_base(j) for j in range(NT)]) + r) - 2)[:, None] - 16 + np.arange(16)[None, :]
        hv = ((hp >= 0) & (hp < S)).astype(np.float32)
        xhh = x[b][np.clip(hp, 0, S - 1).reshape(-1)]
        xhh[hv.reshape(-1) == 0] = 0.0
        m = dict(shared)
        m.update({"xb": x[b], "xo": np.ascontiguousarray(xo), "xh": np.ascontiguousarray(xhh), "memb": mem[b],
                  "qaug": qaug, "masks": masks, "invc": invc, "valid": np.ascontiguousarray(val.T)})
        in_maps.append(m)
    res = run_bass_kernel_spmd(nc, in_maps, core_ids=list(range(8)))
    outp = np.zeros((2, S, D), np.float32)
    for c in range(8):
        b, r = c // 4, c % 4
        o = np.asarray(res.results[c]["out"]).reshape(NT, 128, D)
        for j in range(NT):
            p0 = ST * (tile_base(j) + r)
            n = min(ST, S - p0)
            if n > 0:
                outp[b, p0:p0 + n] = o[j, 2:2 + n]
    return outp
```

```python
import contextlib
import math
import os
KF_A = os.environ.get('KF_A', '1') == '1'
KF_B = os.environ.get('KF_B', '1') == '1'
KF_C = os.environ.get('KF_C', '0') == '1'
import numpy as np
import concourse.bass as bass
import concourse.mybir as mybir
from concourse.bass_utils import run_bass_kernel_spmd

F32 = mybir.dt.float32
BF16 = mybir.dt.bfloat16
AF = mybir.ActivationFunctionType
ALU = mybir.AluOpType

D = 1024
S = 8192
NT = 17
ST = 126
NTOK = NT * 128
NH = 8
DFF = 2816
NCH = DFF // 128
EPS = 1e-6
LAM_INIT = 0.8 - 0.6 * math.exp(0.0)
SLOPES = [2.0 ** (-(i + 1)) for i in range(NH)]
GELU_C = math.sqrt(2.0 / math.pi)
AGROUPS = [(0, 4), (4, 4), (8, 4), (12, 4), (16, 1)]
CGROUPS = [(2 * i, 2) for i in range(8)] + [(16, 1)]

COMPUTE = ("pe", "act", "dve", "pool")


def tile_base(j):
    return 4 * (j + 1) if j < NT - 1 else 0


def tile_bounds(j):
    i0, i1 = tile_base(j), tile_base(j) + 3
    p_lo = min(max(ST * i0 - 2, 0), S - 1)
    p_hi = min(max(ST * i1 + 125, 0), S - 1)
    return p_lo // 128, p_hi // 128


NM = max(tile_bounds(j)[1] - tile_bounds(j)[0] + 1 for j in range(NT))


class T:
    __slots__ = ("name", "ap", "w", "rd", "sem", "ndma")

    def __init__(self, name, ap, sem=None):
        self.name = name
        self.ap = ap
        self.w = None
        self.rd = {}
        self.sem = sem
        self.ndma = 0


class KB:
    def __init__(self, nc):
        self.nc = nc
        self.q = {e: [] for e in ("pe", "act", "dve", "pool", "sp")}
        self.cnt = {e: 0 for e in COMPUTE}
        self.waited = {e: {} for e in self.q}
        self.sems = {}
        self.dma_tiles = []
        for e in COMPUTE:
            self.sems[e] = nc.alloc_semaphore(name="s_" + e)

    def tile(self, name, ap, dma=False):
        t = T(name, ap, self.nc.alloc_semaphore(name="d_" + name) if dma else None)
        if dma:
            self.dma_tiles.append(t)
        return t

    def _need(self, eng, dep, waits):
        if dep is None:
            return
        if dep[0] == "c":
            _, e, idx = dep
            if e == eng and e == "pe":
                return
            key, val, sem = ("c", e), idx, self.sems[e]
        else:
            _, t, cnt = dep
            key, val, sem = ("d", id(t)), 16 * cnt, t.sem
        if self.waited[eng].get(key, 0) >= val:
            return
        self.waited[eng][key] = val
        waits.append((sem, val))

    def _deps(self, eng, reads, writes):
        waits = []
        for t in reads:
            self._need(eng, t.w, waits)
        for t in writes:
            self._need(eng, t.w, waits)
            for d in t.rd.values():
                self._need(eng, d, waits)
        return waits

    def op(self, eng, fn, reads=(), writes=()):
        waits = self._deps(eng, reads, writes)
        self.cnt[eng] += 1
        dep = ("c", eng, self.cnt[eng])
        self.q[eng].append((waits, fn, self.sems[eng], 1))
        for t in reads:
            t.rd[("c", eng)] = dep
        for t in writes:
            t.w = dep
            t.rd = {}

    def dma(self, queue, fn, reads=(), writes=(), semtile=None):
        waits = self._deps(queue, reads, writes)
        semtile.ndma += 1
        dep = ("d", semtile, semtile.ndma)
        self.q[queue].append((waits, fn, semtile.sem, 16))
        for t in reads:
            t.rd[("d", id(semtile))] = dep
        for t in writes:
            t.w = dep
            t.rd = {}

    def barrier(self):
        for eng in self.q:
            waits = []
            for e in COMPUTE:
                if self.cnt[e] > 0 and e != eng:
                    self._need(eng, ("c", e, self.cnt[e]), waits)
            for t in self.dma_tiles:
                if t.ndma > 0:
                    self._need(eng, ("d", t, t.ndma), waits)
            if waits:
                self.q[eng].append((waits, None, None, 0))

    def emit(self):
        self.barrier()
        names = {"pe": "tensor", "act": "scalar", "dve": "vector", "pool": "gpsimd", "sp": "sync"}
        with self.nc.Block() as block:
            for e in ("sp", "pe", "act", "dve", "pool"):
                def body(eng, lst=self.q[e]):
                    for waits, fn, sem, inc in lst:
                        for (s, v) in waits:
                            eng.wait_ge(s, v)
                        if fn is not None:
                            fn(eng).then_inc(sem, inc)
                getattr(block, names[e])(body)


ARENA_WORDS = 48448


def build_program(debug=False):
    nc = bass.Bass("TRN2", target_bir_lowering=False)

    def din(name, shape):
        return nc.dram_tensor(name, list(shape), F32, kind="ExternalInput").ap()

    xb = din("xb", [S, D])
    xo = din("xo", [NTOK, D])
    xh = din("xh", [NT * 16, D])
    memb = din("memb", [256, D])
    w_in = din("w_in", [D, 7168])
    w_attn = din("w_attn", [D, D])
    pool_w = din("pool_w", [4, 128, 128])
    w_pb = din("w_pb", [512, D])
    w_mkv = din("w_mkv", [D, D])
    w_mb = din("w_mb", [512, D])
    w_out = din("w_out", [D, D])
    w_up = din("w_up", [D, 2 * DFF])
    w_down = din("w_down", [DFF, D])
    vec = {n: din(n, [1, D]) for n in ("g_mix_pre", "g_mem", "g_mix_post", "g_ffn_pre", "g_ffn_post")}
    lamv = din("lamv", [4, 64])
    subln = din("subln", [128, 1])
    pscale = din("pscale", [128, 4])
    convw = din("convw", [128, 3 * 2 * NCH])
    convb = din("convb", [128, 2 * NCH])
    ident = din("ident", [128, 128])
    kaug = din("kaug", [4, S])
    qaug = din("qaug", [NH, 4, NTOK])
    masks = din("masks", [128, NT * NM * 128])
    invc = din("invc", [4, NTOK])
    valid = din("valid", [128, NT])
    out = nc.dram_tensor("out", [NTOK, D], F32, kind="ExternalOutput").ap()

    def dscr(name, shape, dt):
        return nc.dram_tensor(name, list(shape), dt, kind="ExternalOutput" if debug else "Internal").ap()

    KT_s = dscr("KT_s", [NH, 2, 64, S], BF16)
    V_s = dscr("V_s", [S, D], BF16)
    QT_s = dscr("QT_s", [NH, 2, 64, NTOK], BF16)
    A_s = dscr("A_s", [NH, 128, NTOK], BF16)
    XM_s = dscr("XM_s", [NTOK, D], F32)

    es = contextlib.ExitStack()
    with es:
        arena = es.enter_context(nc.sbuf_tensor("arena", [128, ARENA_WORDS], F32))
        ps = es.enter_context(nc.psum_tensor("ps", [128, 4096], F32))
        kb = KB(nc)
        off = [0]

        def carve(name, shape, dt, dma=False):
            n = int(np.prod(shape))
            nw = (n * (2 if dt == BF16 else 4) + 3) // 4
            assert off[0] + nw <= ARENA_WORDS, (name, off[0], nw)
            ap = arena[:, off[0]:off[0] + nw]
            off[0] += nw
            if dt != F32:
                ap = ap.bitcast(dt)
            if len(shape) == 2:
                ap = ap.rearrange("p (a b) -> p a b", b=shape[1])
            elif len(shape) == 3:
                ap = ap.rearrange("p (a b c) -> p a b c", b=shape[1], c=shape[2])
            return kb.tile(name, ap, dma=dma)

        def bank(k, n=1):
            return ps[:, k * 512:(k + n) * 512]

        PB = [kb.tile(f"bank{k}", bank(k)) for k in range(8)]

        identb = carve("identb", [128], BF16, dma=True)
        onesb = carve("onesb", [128], BF16)
        lam4 = carve("lam4", [4, 64], F32, dma=True)
        lamt = carve("lamt", [2, 64], F32)
        lams = carve("lams", [4], F32)
        neglam = carve("neglam", [1], F32)
        gsub = carve("gsub", [1], F32, dma=True)
        psc = carve("psc", [4], F32, dma=True)
        cw = carve("cw", [3 * 2 * NCH], F32, dma=True)
        cb = carve("cb", [2 * NCH], F32, dma=True)
        vld = carve("vld", [NT], F32, dma=True)
        persist_ffn = off[0]
        KmT = carve("KmT", [4, 256], BF16)
        Vm = carve("Vm", [2, 512], BF16)
        junk = carve("junk", [64], BF16)
        persist_end = off[0]

        kb.dma("pool", lambda e: e.dma_start(out=identb.ap, in_=ident), writes=[identb], semtile=identb)
        kb.op("pool", lambda e: e.memset(onesb.ap, 1.0), writes=[onesb])
        kb.dma("sp", lambda e: e.dma_start(out=lam4.ap,
                                            in_=lamv.partition_broadcast(128)),
               writes=[lam4], semtile=lam4)
        kb.dma("sp", lambda e: e.dma_start(out=gsub.ap, in_=subln), writes=[gsub], semtile=gsub)
        kb.dma("sp", lambda e: e.dma_start(out=psc.ap, in_=pscale), writes=[psc], semtile=psc)
        kb.dma("sp", lambda e: e.dma_start(out=cw.ap, in_=convw), writes=[cw], semtile=cw)
        kb.dma("sp", lambda e: e.dma_start(out=cb.ap, in_=convb), writes=[cb], semtile=cb)
        kb.dma("sp", lambda e: e.dma_start(out=vld.ap, in_=valid), writes=[vld], semtile=vld)
        kb.op("dve", lambda e: e.tensor_tensor(out=lamt.ap[:, 0, :], in0=lam4.ap[:, 0, :], in1=lam4.ap[:, 1, :], op=ALU.mult),
              reads=[lam4], writes=[lamt])
        kb.op("dve", lambda e: e.tensor_tensor(out=lamt.ap[:, 1, :], in0=lam4.ap[:, 2, :], in1=lam4.ap[:, 3, :], op=ALU.mult),
              reads=[lam4], writes=[lamt])
        for k in range(2):
            kb.op("act", lambda e, k=k: e.activation(out=junk.ap[:, 0:64], in_=lamt.ap[:, k, :], func=AF.Copy,
                                                     accum_out=lams.ap[:, k:k + 1]),
                  reads=[lamt], writes=[junk, lams])
        kb.op("act", lambda e: e.activation(out=lams.ap[:, 2:4], in_=lams.ap[:, 0:2], func=AF.Exp), reads=[lams], writes=[lams])
        kb.op("dve", lambda e: e.tensor_tensor(out=neglam.ap, in0=lams.ap[:, 3:4], in1=lams.ap[:, 2:3], op=ALU.subtract),
              reads=[lams], writes=[neglam])
        kb.op("dve", lambda e: e.tensor_scalar(out=neglam.ap, in0=neglam.ap, scalar1=-LAM_INIT, scalar2=None, op0=ALU.add),
              reads=[neglam], writes=[neglam])
        kb.op("dve", lambda e: e.tensor_scalar(out=gsub.ap, in0=gsub.ap, scalar1=1.0 - LAM_INIT, scalar2=None, op0=ALU.mult),
              reads=[gsub], writes=[gsub])

        rr = [0]

        def evac(out_ap, in_ap, reads, writes, scale=None):
            rr[0] += 1
            if rr[0] % 2 == 0:
                kb.op("dve", lambda e: e.tensor_copy(out=out_ap, in_=in_ap), reads=reads, writes=writes)
            else:
                kb.op("act", lambda e: e.activation(out=out_ap, in_=in_ap, func=AF.Copy), reads=reads, writes=writes)

        def load_w(t, src_ap):
            kb.dma("pool", lambda e: e.dma_start(out=t.ap, in_=src_ap), writes=[t], semtile=t)

        def wview(w, c0, c1):
            return w.rearrange("(kc p) n -> p kc n", p=128)[:, :, c0:c1]

        def rstd_from_ss(rs, ss, n, rows=128):
            kb.op("act", lambda e: e.activation(out=rs.ap[0:rows, :], in_=ss.ap[0:rows, :], func=AF.Ln, scale=1.0 / n, bias=EPS),
                  reads=[ss], writes=[rs])
            kb.op("act", lambda e: e.activation(out=rs.ap[0:rows, :], in_=rs.ap[0:rows, :], func=AF.Exp, scale=-0.5),
                  reads=[rs], writes=[rs])

        def norm_tile(xt, rows, gt, ht, ss, rs, vcol=None):
            kb.op("act", lambda e: e.activation(out=ht.ap[0:rows, :], in_=xt.ap[0:rows, :], func=AF.Square,
                                                accum_out=ss.ap[0:rows, :]),
                  reads=[xt], writes=[ht, ss])
            rstd_from_ss(rs, ss, D, rows)
            if vcol is not None:
                kb.op("dve", lambda e: e.tensor_tensor(out=rs.ap, in0=rs.ap, in1=vcol, op=ALU.mult), reads=[rs, vld], writes=[rs])
            kb.op("dve", lambda e: e.scalar_tensor_tensor(out=ht.ap[0:rows, :], in0=xt.ap[0:rows, :], scalar=rs.ap[0:rows, :],
                                                          in1=gt.ap[0:rows, :], op0=ALU.mult, op1=ALU.mult),
                  reads=[xt, rs, gt], writes=[ht])

        def transpose_to(ht, rows, pbank_idx, dst_tile, dst_ap3):
            pb = PB[pbank_idx]
            pv = bank(pbank_idx).bitcast(BF16)
            for kc in range(8):
                kb.op("pe", lambda e, kc=kc: e.transpose(out=pv[:, kc * 128:kc * 128 + rows],
                                                         in_=ht.ap[0:rows, kc * 128:(kc + 1) * 128],
                                                         identity=identb.ap[0:rows, 0:rows]),
                      reads=[ht, identb], writes=[pb])
            src = pv.rearrange("p (k c) -> p k c", c=128)[:, :, 0:rows]
            evac(dst_ap3, src, [pb], [dst_tile])

        def load_g(t, name):
            kb.dma("sp", lambda e: e.dma_start(out=t.ap, in_=vec[name].partition_broadcast(128)),
                   writes=[t], semtile=t)

        off[0] = persist_end
        g1 = carve("g1", [1024], F32, dma=True)
        g2 = carve("g2", [1024], F32, dma=True)
        load_g(g1, "g_mix_pre")
        load_g(g2, "g_mem")
        Wkv = carve("Wkv", [8, 2048], BF16, dma=True)
        Wm = carve("Wm", [8, 1024], BF16, dma=True)
        load_w(Wkv, wview(w_in, 1024, 3072))
        load_w(Wm, wview(w_mkv, 0, 1024))
        xt2 = [carve(f"xt{i}", [1024], F32, dma=True) for i in range(2)]
        ht2 = [carve(f"ht{i}", [1024], BF16) for i in range(2)]
        ss2 = [carve(f"ss{i}", [1], F32) for i in range(2)]
        rs2 = [carve(f"rs{i}", [1], F32) for i in range(2)]
        hT2 = [carve(f"hT{i}", [8, 512], BF16) for i in range(2)]
        kst2 = [carve(f"kst{i}", [8, 512], BF16, dma=True) for i in range(2)]
        vst2 = [carve(f"vst{i}", [4, 1024], BF16, dma=True) for i in range(2)]

        hTm = hT2[0]
        for blk in range(2):
            xt, ht, ss, rs = xt2[blk], ht2[blk], ss2[blk], rs2[blk]
            kb.dma("sp", lambda e, xt=xt, blk=blk: e.dma_start(out=xt.ap, in_=memb[blk * 128:(blk + 1) * 128, :]),
                   writes=[xt], semtile=xt)
            norm_tile(xt, 128, g2, ht, ss, rs)
            transpose_to(ht, 128, blk, hTm, hTm.ap[:, :, blk * 128:(blk + 1) * 128])
        for hd in range(4):
            for kc in range(8):
                kb.op("pe", lambda e, hd=hd, kc=kc: e.matmul(out=bank(2 + hd % 2)[:, 0:256],
                                                             lhsT=Wm.ap[:, kc, hd * 128:(hd + 1) * 128],
                                                             rhs=hTm.ap[:, kc, 0:256], start=(kc == 0), stop=(kc == 7)),
                      reads=[Wm, hTm], writes=[PB[2 + hd % 2]])
            evac(KmT.ap[:, hd, :], bank(2 + hd % 2)[:, 0:256], [PB[2 + hd % 2]], [KmT])
        for ch in range(2):
            for kc in range(8):
                kb.op("pe", lambda e, ch=ch, kc=kc: e.matmul(out=bank(4 + ch), lhsT=hTm.ap[:, kc, ch * 128:(ch + 1) * 128],
                                                             rhs=Wm.ap[:, kc, 512:1024], start=(kc == 0), stop=(kc == 7)),
                      reads=[Wm, hTm], writes=[PB[4 + ch]])
            evac(Vm.ap[:, ch, :], bank(4 + ch), [PB[4 + ch]], [Vm])

        def prep1(G, blk):
            hT = hT2[G % 2]
            n = G * 4 + blk
            xt, ht, ss, rs = xt2[n % 2], ht2[n % 2], ss2[n % 2], rs2[n % 2]
            kb.dma("sp", lambda e: e.dma_start(out=xt.ap, in_=xb[n * 128:(n + 1) * 128, :]), writes=[xt], semtile=xt)
            norm_tile(xt, 128, g1, ht, ss, rs)

        def prep1t(G, blk):
            hT = hT2[G % 2]
            n = G * 4 + blk
            transpose_to(ht2[n % 2], 128, n % 2, hT, hT.ap[:, :, blk * 128:(blk + 1) * 128])

        def kpart(G, heads):
            hT, kst = hT2[G % 2], kst2[G % 2]
            for h in heads:
                bk = 2 + h % 2
                for kc in range(8):
                    kb.op("pe", lambda e, h=h, kc=kc, bk=bk: e.matmul(out=bank(bk), lhsT=Wkv.ap[:, kc, h * 128:(h + 1) * 128],
                                                                     rhs=hT.ap[:, kc, :], start=(kc == 0), stop=(kc == 7)),
                          reads=[Wkv, hT], writes=[PB[bk]])
                evac(kst.ap[:, h, :], bank(bk), [PB[bk]], [kst])
            if heads[-1] == NH - 1:
                for m in range(2):
                    kb.dma("sp", lambda e, m=m: e.dma_start(
                        out=KT_s[:, m, :, G * 512:(G + 1) * 512].rearrange("h d c -> d h c"),
                        in_=kst.ap[m * 64:(m + 1) * 64, :, :]), reads=[kst], semtile=kst)

        def vpart(G, blks):
            hT, vst = hT2[G % 2], vst2[G % 2]
            for blk in blks:
                for half in range(2):
                    bk = 4 + (blk * 2 + half) % 4
                    for kc in range(8):
                        kb.op("pe", lambda e, blk=blk, half=half, kc=kc, bk=bk: e.matmul(
                            out=bank(bk), lhsT=hT.ap[:, kc, blk * 128:(blk + 1) * 128],
                            rhs=Wkv.ap[:, kc, 1024 + half * 512:1024 + (half + 1) * 512], start=(kc == 0), stop=(kc == 7)),
                            reads=[Wkv, hT], writes=[PB[bk]])
                    evac(vst.ap[:, blk, half * 512:(half + 1) * 512], bank(bk), [PB[bk]], [vst])
            if blks[-1] == 3:
                kb.dma("sp", lambda e: e.dma_start(
                    out=V_s[G * 512:(G + 1) * 512, :].rearrange("(b p) c -> p b c", p=128), in_=vst.ap),
                    reads=[vst], semtile=vst)

        for blk in range(4):
            prep1(0, blk)
            prep1t(0, blk)
        for G in range(16):
            parts = [lambda: kpart(G, [0, 1, 2, 3]), lambda: kpart(G, [4, 5, 6, 7]), lambda: vpart(G, [0, 1]), lambda: vpart(G, [2, 3])]
            for p in range(4):
                if G + 1 < 16:
                    prep1(G + 1, p)
                parts[p]()
                if G + 1 < 16:
                    prep1t(G + 1, p)
        kb.barrier()

        off[0] = persist_end
        g1 = carve("g1b", [1024], F32, dma=True)
        load_g(g1, "g_mix_pre")
        Wq = carve("Wq", [8, 1024], BF16, dma=True)
        load_w(Wq, wview(w_in, 0, 1024))
        xt2 = [carve(f"xtb{i}", [1024], F32, dma=True) for i in range(2)]
        ht2 = [carve(f"htb{i}", [1024], BF16) for i in range(2)]
        ss2 = [carve(f"ssb{i}", [1], F32) for i in range(2)]
        rs2 = [carve(f"rsb{i}", [1], F32) for i in range(2)]
        hT2 = [carve(f"hTb{i}", [8, 512], BF16) for i in range(2)]
        qst2 = [carve(f"qst{i}", [8, 512], BF16, dma=True) for i in range(2)]
        def prep2a(gi, t):
            j0, nt = AGROUPS[gi]
            hT = hT2[gi % 2]
            n = j0 + t
            xt, ht, ss, rs = xt2[n % 2], ht2[n % 2], ss2[n % 2], rs2[n % 2]
            kb.dma("sp", lambda e: e.dma_start(out=xt.ap, in_=xo[n * 128:(n + 1) * 128, :]), writes=[xt], semtile=xt)
            norm_tile(xt, 128, g1, ht, ss, rs)

        def prep2at(gi, t):
            j0, nt = AGROUPS[gi]
            hT = hT2[gi % 2]
            n = j0 + t
            transpose_to(ht2[n % 2], 128, n % 2, hT, hT.ap[:, :, t * 128:(t + 1) * 128])

        def qpart(gi, heads):
            j0, nt = AGROUPS[gi]
            hT, qst = hT2[gi % 2], qst2[gi % 2]
            N = nt * 128
            for h in heads:
                bk = 2 + h % 4
                for kc in range(8):
                    kb.op("pe", lambda e, h=h, kc=kc, bk=bk: e.matmul(
                        out=bank(bk)[:, 0:N], lhsT=Wq.ap[:, kc, h * 128:(h + 1) * 128], rhs=hT.ap[:, kc, 0:N],
                        start=(kc == 0), stop=(kc == 7)), reads=[Wq, hT], writes=[PB[bk]])
                evac(qst.ap[:, h, 0:N], bank(bk)[:, 0:N], [PB[bk]], [qst])
            if heads[-1] == NH - 1:
                for m in range(2):
                    kb.dma("sp", lambda e, m=m: e.dma_start(
                        out=QT_s[:, m, :, j0 * 128:j0 * 128 + N].rearrange("h d c -> d h c"),
                        in_=qst.ap[m * 64:(m + 1) * 64, :, 0:N]), reads=[qst], semtile=qst)

        for t in range(AGROUPS[0][1]):
            prep2a(0, t)
            prep2at(0, t)
        for gi in range(len(AGROUPS)):
            for p in range(4):
                nxt = gi + 1 < len(AGROUPS) and p < AGROUPS[gi + 1][1]
                if nxt:
                    prep2a(gi + 1, p)
                qpart(gi, [2 * p, 2 * p + 1])
                if nxt:
                    prep2at(gi + 1, p)
        kb.barrier()

        off[0] = persist_end
        Kt = [[carve(f"Kt{b}{m}", [S], BF16, dma=True) for m in range(2)] for b in range(2)]
        Qt = [[carve(f"Qt{b}{m}", [NTOK], BF16, dma=True) for m in range(2)] for b in range(2)]
        Vh = [carve(f"Vh{b}", [64, 128], BF16, dma=True) for b in range(2)]
        mk = carve("mk", [NT * NM, 128], BF16, dma=True)
        PT = [carve(f"PT{b}", [2, 512], BF16) for b in range(2)]
        rl = carve("rl", [2, 512], F32)
        a1 = carve("a1", [512], F32)
        o1s = carve("o1s", [512], F32)
        o2s = carve("o2s", [512], F32)
        a2 = carve("a2", [512], F32)
        sqb = carve("sqb", [512], BF16)
        rsa = carve("rsa", [512], F32)
        ast = [carve(f"ast{b}", [512], BF16, dma=True) for b in range(2)]
        kb.dma("pool", lambda e: e.dma_start(out=mk.ap, in_=masks.rearrange("p (a b) -> p a b", b=128)), writes=[mk], semtile=mk)
        KtA = [[kb.tile(f"KtA{b}{m}", Kt[b][m].ap[64:68, :], dma=True) for m in range(2)] for b in range(2)]
        QtA = [[kb.tile(f"QtA{b}{m}", Qt[b][m].ap[64:68, :], dma=True) for m in range(2)] for b in range(2)]
        for b in range(2):
            for m in range(2):
                kb.dma("pool", lambda e, b=b, m=m: e.dma_start(out=Kt[b][m].ap[64:68, :].rearrange("a (b c) -> a b c", c=2048), in_=kaug.rearrange("a (b c) -> a b c", c=2048)),
                       writes=[KtA[b][m]], semtile=KtA[b][m])
        SB = [kb.tile("SB0", bank(0, 2)), kb.tile("SB1", bank(2, 2))]
        OB = [PB[4], PB[5]]
        DB = [PB[6], PB[7]]
        def head_loads(h):
            bsel = h % 2
            for m in range(2):
                kb.dma("sp", lambda e, h=h, m=m, bsel=bsel: e.dma_start(out=Kt[bsel][m].ap[0:64, :], in_=KT_s[h, m, :, :]),
                       writes=[Kt[bsel][m]], semtile=Kt[bsel][m])
                kb.dma("sp", lambda e, h=h, m=m, bsel=bsel: e.dma_start(out=Qt[bsel][m].ap[0:64, :], in_=QT_s[h, m, :, :]),
                       writes=[Qt[bsel][m]], semtile=Qt[bsel][m])
                kb.dma("pool", lambda e, h=h, m=m, bsel=bsel: e.dma_start(
                    out=Qt[bsel][m].ap[64:68, :].rearrange("a (b c) -> a b c", c=1088),
                    in_=qaug[h, :, :].rearrange("a (b c) -> a b c", c=1088)), writes=[QtA[bsel][m]], semtile=QtA[bsel][m])
            kb.dma("sp", lambda e, h=h, bsel=bsel: e.dma_start(
                out=Vh[bsel].ap, in_=V_s.rearrange("(kb p) c -> p kb c", p=128)[:, :, h * 128:(h + 1) * 128]),
                writes=[Vh[bsel]], semtile=Vh[bsel])

        items = []
        for h in range(NH):
            for gi, (j0, nt) in enumerate(AGROUPS):
                bnds = [tile_bounds(j0 + t) for t in range(nt)]
                kb_last = bnds[-1][1]
                for kbi in range(kb_last + 1):
                    tmin = min(t for t in range(nt) if bnds[t][1] >= kbi)
                    items.append(dict(h=h, gi=gi, j0=j0, nt=nt, bnds=bnds, kb_last=kb_last, kbi=kbi, tmin=tmin,
                                      last_of_head=(gi == len(AGROUPS) - 1 and kbi == kb_last)))
        for idx, it in enumerate(items):
            it["par"] = idx % 2

        pending = []

        def emit_scores(it):
            h, j0, nt, bnds, kbi, tmin = it["h"], it["j0"], it["nt"], it["bnds"], it["kbi"], it["tmin"]
            c0, c1 = tmin * 128, nt * 128
            gc0 = j0 * 128
            sb, pt, sbase = SB[it["par"]], PT[it["par"]], it["par"] * 1024
            for pnd in list(pending):
                pnd[0] -= 1
                if pnd[0] <= 0:
                    pending.remove(pnd)
                    pnd[1](sb, sbase)
            band = [t for t in range(tmin, nt) if bnds[t][0] <= kbi <= bnds[t][1]]
            for m in range(2):
                Kx, Qx = Kt[h % 2][m], Qt[h % 2][m]
                kb.op("pe", lambda e, m=m, Kx=Kx, Qx=Qx: e.matmul(
                    out=ps[:, sbase + m * 512 + c0:sbase + m * 512 + c1],
                    lhsT=Kx.ap[0:68, kbi * 128:(kbi + 1) * 128], rhs=Qx.ap[0:68, gc0 + c0:gc0 + c1],
                    start=True, stop=(len(band) == 0)), reads=[Kx, Qx, KtA[h % 2][m], QtA[h % 2][m]], writes=[sb])
                for t in band:
                    mi = (j0 + t) * NM + (kbi - bnds[t][0])
                    kb.op("pe", lambda e, m=m, t=t, mi=mi: e.matmul(
                        out=ps[:, sbase + m * 512 + t * 128:sbase + m * 512 + (t + 1) * 128],
                        lhsT=identb.ap, rhs=mk.ap[:, mi, :], start=False, stop=(t == band[-1])),
                        reads=[identb, mk], writes=[sb])
            kb.op("act", lambda e: e.activation(
                out=pt.ap[:, :, c0:c1], in_=ps[:, sbase:sbase + 1024].rearrange("p (m c) -> p m c", c=512)[:, :, c0:c1],
                func=AF.Exp, scale=0.125), reads=[sb], writes=[pt])

        def emit_pv(it):
            h, j0, nt, kbi, tmin, kb_last = it["h"], it["j0"], it["nt"], it["kbi"], it["tmin"], it["kb_last"]
            c0, c1 = tmin * 128, nt * 128
            gc0 = j0 * 128
            pt = PT[it["par"]]
            V = Vh[h % 2]
            for m in range(2):
                kb.op("pe", lambda e, m=m: e.matmul(out=bank(4 + m)[:, c0:c1], lhsT=V.ap[:, kbi, :], rhs=pt.ap[:, m, c0:c1],
                                                    start=(kbi == 0), stop=(kbi == kb_last)), reads=[V, pt], writes=[OB[m]])
                kb.op("pe", lambda e, m=m: e.matmul(out=bank(6 + m)[:, c0:c1], lhsT=onesb.ap, rhs=pt.ap[:, m, c0:c1],
                                                    start=(kbi == 0), stop=(kbi == kb_last)), reads=[onesb, pt], writes=[DB[m]])
            if kbi != kb_last:
                return
            N = nt * 128
            aout = ast[(h * len(AGROUPS) + it["gi"]) % 2]
            kb.op("act", lambda e: e.activation(out=rl.ap[:, :, 0:N], in_=bank(6, 2).rearrange("p (m c) -> p m c", c=512)[:, :, 0:N], func=AF.Ln),
                  reads=[DB[0], DB[1]], writes=[rl])
            kb.op("act", lambda e: e.activation(out=rl.ap[:, :, 0:N], in_=rl.ap[:, :, 0:N], func=AF.Exp, scale=-1.0),
                  reads=[rl], writes=[rl])
            kb.op("dve", lambda e: e.tensor_copy(out=o1s.ap[:, 0:N], in_=bank(4)[:, 0:N]), reads=[OB[0]], writes=[o1s])
            kb.op("dve", lambda e: e.tensor_copy(out=o2s.ap[:, 0:N], in_=bank(5)[:, 0:N]), reads=[OB[1]], writes=[o2s])
            kb.op("dve", lambda e: e.tensor_tensor(out=a1.ap[:, 0:N], in0=o1s.ap[:, 0:N], in1=rl.ap[:, 0, 0:N], op=ALU.mult),
                  reads=[o1s, rl], writes=[a1])
            kb.op("dve", lambda e: e.tensor_tensor(out=a2.ap[:, 0:N], in0=o2s.ap[:, 0:N], in1=rl.ap[:, 1, 0:N], op=ALU.mult),
                  reads=[o2s, rl], writes=[a2])
            kb.op("dve", lambda e: e.scalar_tensor_tensor(out=a1.ap[:, 0:N], in0=a2.ap[:, 0:N], scalar=neglam.ap,
                                                          in1=a1.ap[:, 0:N], op0=ALU.mult, op1=ALU.add),
                  reads=[a1, a2, neglam], writes=[a1])
            kb.op("dve", lambda e: e.tensor_tensor(out=sqb.ap[:, 0:N], in0=a1.ap[:, 0:N], in1=a1.ap[:, 0:N], op=ALU.mult),
                  reads=[a1], writes=[sqb])
            def tail(sb, sbase):
                kb.op("pe", lambda e: e.matmul(out=ps[:, sbase:sbase + N], lhsT=onesb.ap, rhs=sqb.ap[:, 0:N], start=True, stop=True),
                      reads=[onesb, sqb], writes=[sb])
                kb.op("act", lambda e: e.activation(out=rsa.ap[:, 0:N], in_=ps[:, sbase:sbase + N], func=AF.Ln, scale=1.0 / 128, bias=EPS),
                      reads=[sb], writes=[rsa])
                kb.op("act", lambda e: e.activation(out=rsa.ap[:, 0:N], in_=rsa.ap[:, 0:N], func=AF.Exp, scale=-0.5),
                      reads=[rsa], writes=[rsa])
                kb.op("dve", lambda e: e.scalar_tensor_tensor(out=aout.ap[:, 0:N], in0=a1.ap[:, 0:N], scalar=gsub.ap,
                                                              in1=rsa.ap[:, 0:N], op0=ALU.mult, op1=ALU.mult),
                      reads=[a1, gsub, rsa], writes=[aout])
                kb.dma("sp", lambda e: e.dma_start(out=A_s[h, :, gc0:gc0 + N], in_=aout.ap[:, 0:N]), reads=[aout], semtile=aout)

            pending.append([2, tail])
            if it["last_of_head"] and h + 2 < NH:
                head_loads(h + 2)

        head_loads(0)
        head_loads(1)
        if KF_A:
            for idx in range(len(items) + 1):
                if idx < len(items):
                    emit_scores(items[idx])
                if idx >= 1:
                    emit_pv(items[idx - 1])
            for pnd in pending:
                pnd[1](SB[0], 0)
        else:
            for it in items:
                emit_scores(it)
                emit_pv(it)
        kb.barrier()

        off[0] = persist_end
        g1 = carve("g1c", [1024], F32, dma=True)
        g3 = carve("g3c", [1024], F32, dma=True)
        load_g(g1, "g_mix_pre")
        load_g(g3, "g_mix_post")
        Wu = carve("Wu", [8, 512], BF16, dma=True)
        Wqm = carve("Wqm", [8, 512], BF16, dma=True)
        Wg = [carve(f"Wg{i}", [8, 1024], BF16, dma=True) for i in range(3)]
        Wat = carve("Wat", [8, 1024], BF16, dma=True)
        Wpw = carve("Wpw", [4, 128], BF16, dma=True)
        Wpb = carve("Wpb", [4, 1024], BF16, dma=True)
        Wmb = carve("Wmb", [4, 1024], BF16, dma=True)
        Wo = carve("Wo", [8, 1024], BF16, dma=True)
        load_w(Wu, wview(w_in, 3072, 3584))
        load_w(Wqm, wview(w_in, 3584, 4096))
        for i in range(3):
            load_w(Wg[i], wview(w_in, 4096 + i * 1024, 4096 + (i + 1) * 1024))
        load_w(Wat, wview(w_attn, 0, 1024))
        kb.dma("pool", lambda e: e.dma_start(out=Wpw.ap, in_=pool_w.rearrange("g c d -> c g d")), writes=[Wpw], semtile=Wpw)
        load_w(Wpb, wview(w_pb, 0, 1024))
        load_w(Wmb, wview(w_mb, 0, 1024))
        load_w(Wo, wview(w_out, 0, 1024))
        xt2 = [carve(f"xtc{i}", [1024], F32, dma=True) for i in range(2)]
        ht2 = [carve(f"htc{i}", [1024], BF16) for i in range(2)]
        ss2 = [carve(f"ssc{i}", [1], F32) for i in range(2)]
        rs2 = [carve(f"rsc{i}", [1], F32) for i in range(2)]
        hT = carve("hTc", [8, 256], BF16)
        hTh = carve("hThc", [8, 32], BF16)
        aT = carve("aTc", [8, 256], BF16, dma=True)
        ivc = carve("ivc", [4, 256], F32, dma=True)
        uext = carve("uext", [2, 144], F32)
        sA = carve("sA", [2, 144], F32)
        sB = carve("sB", [2, 144], F32)
        pp = carve("pp", [256], F32)
        pbf = carve("pbf", [256], BF16)
        ypT = carve("ypT", [4, 256], BF16)
        qmT = carve("qmT", [4, 256], BF16)
        PTm = carve("PTm", [2, 256], BF16)
        rlm = carve("rlm", [256], F32)
        omT = carve("omT", [4, 256], BF16)
        gat = carve("gat", [3, 512], F32)
        prod = carve("prod", [3, 512], F32)
        mixtok = carve("mixtok", [1024], BF16)
        mixT = carve("mixT", [8, 256], BF16)
        xm2 = [carve("xm0", [1024], F32, dma=True)] * 2
        xhalo = xt2[1]
        GB = kb.tile("GB", bank(0, 3))
        YB = kb.tile("YB", bank(3, 3))
        GTs = [kb.tile(f"GT{i}", ps[:, i * 1536:i * 1536 + 768]) for i in range(2)]
        YTs = [kb.tile(f"YT{i}", ps[:, i * 1536 + 768:i * 1536 + 1536]) for i in range(2)]
        tcount = 0
        for (j0, nt) in CGROUPS:
            N = nt * 128
            gc0 = j0 * 128
            kb.dma("sp", lambda e, gc0=gc0, N=N: e.dma_start(out=aT.ap[:, :, 0:N], in_=A_s[:, :, gc0:gc0 + N].rearrange("h e c -> e h c")),
                   writes=[aT], semtile=aT)
            kb.dma("sp", lambda e, gc0=gc0, N=N: e.dma_start(
                out=ivc.ap[:, :, 0:N], in_=invc[:, gc0:gc0 + N].partition_broadcast(128)), writes=[ivc], semtile=ivc)
            kb.dma("sp", lambda e, j0=j0, nt=nt: e.dma_start(out=xhalo.ap[0:nt * 16, :], in_=xh[j0 * 16:(j0 + nt) * 16, :]),
                   writes=[xhalo], semtile=xhalo)
            norm_tile(xhalo, nt * 16, g1, ht2[0], ss2[0], rs2[0])
            transpose_to(ht2[0], nt * 16, 6, hTh, hTh.ap[:, :, 0:nt * 16])
            for t in range(nt):
                n = j0 + t
                xt, ht, ss, rs = xt2[t % 2], ht2[1], ss2[1], rs2[1]
                kb.dma("sp", lambda e, xt=xt, n=n: e.dma_start(out=xt.ap, in_=xo[n * 128:(n + 1) * 128, :]),
                       writes=[xt], semtile=xt)
                norm_tile(xt, 128, g1, ht, ss, rs)
                transpose_to(ht, 128, 7, hT, hT.ap[:, :, t * 128:(t + 1) * 128])
            def pool_chain(g, N=N, nt=nt):
                w = 2 ** (g + 1)
                for kc in range(8):
                    kb.op("pe", lambda e, kc=kc: e.matmul(out=bank(0)[:, 0:N], lhsT=Wu.ap[:, kc, g * 128:(g + 1) * 128],
                                                          rhs=hT.ap[:, kc, 0:N], start=(kc == 0), stop=(kc == 7)),
                          reads=[Wu, hT], writes=[PB[0]])
                for kc in range(8):
                    kb.op("pe", lambda e, kc=kc: e.matmul(out=bank(1)[:, 0:nt * 16], lhsT=Wu.ap[:, kc, g * 128:(g + 1) * 128],
                                                          rhs=hTh.ap[:, kc, 0:nt * 16], start=(kc == 0), stop=(kc == 7)),
                          reads=[Wu, hTh], writes=[PB[1]])
                yield
                kb.op("act", lambda e: e.activation(out=uext.ap[:, 0:nt, 16:144],
                                                    in_=bank(0)[:, 0:N].rearrange("p (t c) -> p t c", c=128), func=AF.Copy),
                      reads=[PB[0]], writes=[uext])
                kb.op("dve", lambda e: e.tensor_copy(out=uext.ap[:, 0:nt, 0:16],
                                                     in_=bank(1)[:, 0:nt * 16].rearrange("p (t c) -> p t c", c=16)),
                      reads=[PB[1]], writes=[uext])
                yield
                cur = uext
                step = 1
                bufs = [sA, sB]
                bi = 0
                while step < w:
                    nxt = bufs[bi]
                    bi ^= 1
                    kb.op("pool", lambda e, cur=cur, nxt=nxt, step=step: e.tensor_tensor(
                        out=nxt.ap[:, 0:nt, step:144], in0=cur.ap[:, 0:nt, step:144], in1=cur.ap[:, 0:nt, 0:144 - step], op=ALU.add),
                        reads=[cur], writes=[nxt])
                    cur = nxt
                    step *= 2
                    yield
                kb.op("dve", lambda e, cur=cur: e.tensor_tensor(
                    out=pp.ap[:, 0:N].rearrange("p (t c) -> p t c", c=128), in0=cur.ap[:, 0:nt, 16:144],
                    in1=ivc.ap[:, g, 0:N].rearrange("p (t c) -> p t c", c=128), op=ALU.mult), reads=[cur, ivc], writes=[pp])
                kb.op("dve", lambda e: e.tensor_tensor(
                    out=pbf.ap[:, 0:N].rearrange("p (t c) -> p t c", c=128), in0=pp.ap[:, 0:N].rearrange("p (t c) -> p t c", c=128),
                    in1=uext.ap[:, 0:nt, 16:144], op=ALU.subtract), reads=[pp, uext], writes=[pbf])
                yield
                kb.op("pe", lambda e: e.matmul(out=bank(2)[:, 0:N], lhsT=Wpw.ap[:, g, :], rhs=pbf.ap[:, 0:N], start=True, stop=True),
                      reads=[Wpw, pbf], writes=[PB[2]])
                yield
                kb.op("dve", lambda e: e.tensor_scalar(out=ypT.ap[:, g, 0:N], in0=bank(2)[:, 0:N], scalar1=psc.ap[:, g:g + 1],
                                                       scalar2=None, op0=ALU.mult), reads=[PB[2], psc], writes=[ypT])

            def mem_chain(hd, N=N, nt=nt):
                for kc in range(8):
                    kb.op("pe", lambda e, kc=kc: e.matmul(out=bank(3)[:, 0:N], lhsT=Wqm.ap[:, kc, hd * 128:(hd + 1) * 128],
                                                          rhs=hT.ap[:, kc, 0:N], start=(kc == 0), stop=(kc == 7)),
                          reads=[Wqm, hT], writes=[PB[3]])
                yield
                evac(qmT.ap[:, hd, 0:N], bank(3)[:, 0:N], [PB[3]], [qmT])
                yield
                for ch in range(2):
                    kb.op("pe", lambda e, ch=ch: e.matmul(out=ps[:, 2048 + ch * 256:2048 + ch * 256 + N],
                                                          lhsT=KmT.ap[:, hd, ch * 128:(ch + 1) * 128], rhs=qmT.ap[:, hd, 0:N],
                                                          start=True, stop=True), reads=[KmT, qmT], writes=[PB[4]])
                yield
                kb.op("act", lambda e: e.activation(out=PTm.ap[:, :, 0:N], in_=bank(4).rearrange("p (a c) -> p a c", c=256)[:, :, 0:N],
                                                    func=AF.Exp, scale=128.0 ** -0.5), reads=[PB[4]], writes=[PTm])
                yield
                for ch in range(2):
                    kb.op("pe", lambda e, ch=ch: e.matmul(out=bank(5)[:, 0:N], lhsT=Vm.ap[:, ch, hd * 128:(hd + 1) * 128],
                                                          rhs=PTm.ap[:, ch, 0:N], start=(ch == 0), stop=(ch == 1)),
                          reads=[Vm, PTm], writes=[PB[5]])
                for ch in range(2):
                    kb.op("pe", lambda e, ch=ch: e.matmul(out=bank(5)[:, 256:256 + N], lhsT=onesb.ap, rhs=PTm.ap[:, ch, 0:N],
                                                          start=False if ch else True, stop=(ch == 1)),
                          reads=[onesb, PTm], writes=[PB[5]])
                yield
                kb.op("dve", lambda e: e.reciprocal(out=rlm.ap[:, 0:N], in_=bank(5)[:, 256:256 + N]), reads=[PB[5]], writes=[rlm])
                kb.op("dve", lambda e: e.tensor_tensor(out=omT.ap[:, hd, 0:N], in0=bank(5)[:, 0:N], in1=rlm.ap[:, 0:N], op=ALU.mult),
                      reads=[PB[5], rlm], writes=[omT])

            for k in range(4):
                gens = [pool_chain(k), mem_chain(k)]
                while gens:
                    for gen in list(gens):
                        try:
                            next(gen)
                        except StopIteration:
                            gens.remove(gen)
            for t in range(nt):
                tc0 = t * 128
                for b in range(2):
                    for br in range(3):
                        for kc in range(8):
                            kb.op("pe", lambda e, br=br, kc=kc, b=b, tc0=tc0: e.matmul(
                                out=bank(br), lhsT=hT.ap[:, kc, tc0:tc0 + 128], rhs=Wg[br].ap[:, kc, b * 512:(b + 1) * 512],
                                start=(kc == 0), stop=(kc == 7)), reads=[Wg[br], hT], writes=[PB[br]])
                    g3v = bank(0, 3).rearrange("p (a c) -> p a c", c=512)
                    kb.op("act", lambda e, g3v=g3v: e.activation(out=gat.ap, in_=g3v, func=AF.Exp, scale=-1.0), reads=[PB[0], PB[1], PB[2]], writes=[gat])
                    kb.op("act", lambda e: e.activation(out=gat.ap, in_=gat.ap, func=AF.Ln, scale=1.0, bias=1.0), reads=[gat], writes=[gat])
                    kb.op("act", lambda e: e.activation(out=gat.ap, in_=gat.ap, func=AF.Exp, scale=-1.0), reads=[gat], writes=[gat])
                    for hh in range(8):
                        kb.op("pe", lambda e, hh=hh, b=b, tc0=tc0: e.matmul(out=bank(3), lhsT=aT.ap[:, hh, tc0:tc0 + 128],
                                                                           rhs=Wat.ap[:, hh, b * 512:(b + 1) * 512],
                                                                           start=(hh == 0), stop=(hh == 7)), reads=[Wat, aT], writes=[PB[3]])
                    for g in range(4):
                        kb.op("pe", lambda e, g=g, b=b, tc0=tc0: e.matmul(out=bank(4), lhsT=ypT.ap[:, g, tc0:tc0 + 128],
                                                                         rhs=Wpb.ap[:, g, b * 512:(b + 1) * 512],
                                                                         start=(g == 0), stop=(g == 3)), reads=[Wpb, ypT], writes=[PB[4]])
                    for g in range(4):
                        kb.op("pe", lambda e, g=g, b=b, tc0=tc0: e.matmul(out=bank(5), lhsT=omT.ap[:, g, tc0:tc0 + 128],
                                                                         rhs=Wmb.ap[:, g, b * 512:(b + 1) * 512],
                                                                         start=(g == 0), stop=(g == 3)), reads=[Wmb, omT], writes=[PB[5]])
                    y3v = bank(3, 3).rearrange("p (a c) -> p a c", c=512)
                    kb.op("dve", lambda e, y3v=y3v: e.tensor_tensor(out=prod.ap, in0=y3v, in1=gat.ap, op=ALU.mult),
                          reads=[PB[3], PB[4], PB[5], gat], writes=[prod])
                    kb.op("pool", lambda e: e.tensor_tensor(out=prod.ap[:, 0, :], in0=prod.ap[:, 0, :], in1=prod.ap[:, 1, :], op=ALU.add),
                          reads=[prod], writes=[prod])
                    kb.op("pool", lambda e, b=b: e.tensor_tensor(out=mixtok.ap[:, b * 512:(b + 1) * 512], in0=prod.ap[:, 0, :],
                                                                 in1=prod.ap[:, 2, :], op=ALU.add), reads=[prod], writes=[mixtok])
                transpose_to(mixtok, 128, 7, mixT, mixT.ap[:, :, tc0:tc0 + 128])
            for t in range(nt):
                n = j0 + t
                xt = xt2[t % 2]
                xm = xm2[tcount % 2]
                tcount += 1
                ob = 2 * (t % 2)
                for half in range(2):
                    for kc in range(8):
                        kb.op("pe", lambda e, half=half, kc=kc, t=t, ob=ob: e.matmul(
                            out=bank(ob + half), lhsT=mixT.ap[:, kc, t * 128:(t + 1) * 128], rhs=Wo.ap[:, kc, half * 512:(half + 1) * 512],
                            start=(kc == 0), stop=(kc == 7)), reads=[mixT, Wo], writes=[PB[ob + half]])
                ss, rs = ss2[0], rs2[0]
                kb.op("act", lambda e, ss=ss, ob=ob: e.activation(out=prod.ap[:, 0:2, :], in_=bank(ob, 2).rearrange("p (a c) -> p a c", c=512),
                                                                func=AF.Square, accum_out=ss.ap),
                      reads=[PB[ob], PB[ob + 1]], writes=[prod, ss])
                rstd_from_ss(rs, ss, D)
                kb.op("dve", lambda e, xm=xm, rs=rs, ob=ob: e.scalar_tensor_tensor(out=xm.ap, in0=bank(ob, 2), scalar=rs.ap, in1=g3.ap,
                                                                            op0=ALU.mult, op1=ALU.mult),
                      reads=[PB[ob], PB[ob + 1], rs, g3], writes=[xm])
                kb.op("dve", lambda e, xm=xm, xt=xt: e.tensor_tensor(out=xm.ap, in0=xm.ap, in1=xt.ap, op=ALU.add),
                      reads=[xm, xt], writes=[xm])
                kb.dma("sp", lambda e, xm=xm, n=n: e.dma_start(out=XM_s[n * 128:(n + 1) * 128, :], in_=xm.ap), reads=[xm], semtile=xm)
        kb.barrier()

        off[0] = persist_ffn
        ACT_s = dscr("ACT_s", [NCH, 128, NTOK], BF16)
        g4 = carve("g4", [1024], F32, dma=True)
        load_g(g4, "g_ffn_pre")
        Wup = carve("Wup", [8, 2 * DFF], BF16, dma=True)
        for q4 in range(4):
            c0 = q4 * (2 * DFF // 4)
            kb.dma("pool", lambda e, c0=c0: e.dma_start(out=Wup.ap[:, :, c0:c0 + 2 * DFF // 4], in_=wview(w_up, c0, c0 + 2 * DFF // 4)),
                   writes=[Wup], semtile=Wup)
        xm2 = [carve(f"xmf{i}", [1024], F32, dma=True) for i in range(2)]
        h2s = [carve(f"h2{i}", [1024], BF16) for i in range(2)]
        ssfs = [carve(f"ssf{i}", [1], F32) for i in range(2)]
        rsfs = [carve(f"rsf{i}", [1], F32) for i in range(2)]
        h2Ts = [carve(f"h2T{i}", [8, 512], BF16) for i in range(2)]
        actTs = [carve("actT0", [NCH, 512], BF16, dma=True)] * 2
        cgs = [carve(f"cg{i}", [4, 126], F32) for i in range(4)]
        cvs = [carve(f"cv{i}", [4, 126], F32) for i in range(4)]
        z1s = [carve(f"z1{i}", [4, 126], F32) for i in range(4)]
        z2s = [carve(f"z2{i}", [4, 126], F32) for i in range(4)]
        kb.op("pool", lambda e, a0=actTs[0]: e.memset(a0.ap, 0.0), writes=[actTs[0]])
        UB = [kb.tile("UB0", bank(0, 2)), kb.tile("UB1", bank(2, 2))]
        ucount = 0

        def prep3(gi, t):
            j0, nt = AGROUPS[gi]
            n = j0 + t
            xm, h2, ssf, rsf = xm2[n % 2], h2s[n % 2], ssfs[n % 2], rsfs[n % 2]
            kb.dma("sp", lambda e: e.dma_start(out=xm.ap, in_=XM_s[n * 128:(n + 1) * 128, :]), writes=[xm], semtile=xm)
            norm_tile(xm, 128, g4, h2, ssf, rsf, vcol=vld.ap[:, n:n + 1])
            transpose_to(h2, 128, 4 + n % 2, h2Ts[gi % 2], h2Ts[gi % 2].ap[:, :, t * 128:(t + 1) * 128])

        for t in range(AGROUPS[0][1]):
            prep3(0, t)
        def ffn_s1(gi, c, ctx):
            j0, nt = AGROUPS[gi]
            N = nt * 128
            h2T = h2Ts[gi % 2]
            cg, cv = cgs[c % 4], cvs[c % 4]
            ub, ubase = UB[ctx["u"] % 2], (ctx["u"] % 2) * 1024
            ctx["u"] += 1
            for gv in range(2):
                col = gv * DFF + c * 128
                for kc in range(8):
                    kb.op("pe", lambda e, gv=gv, kc=kc, col=col: e.matmul(
                        out=ps[:, ubase + gv * 512:ubase + gv * 512 + N], lhsT=Wup.ap[:, kc, col:col + 128], rhs=h2T.ap[:, kc, 0:N],
                        start=(kc == 0), stop=(kc == 7)), reads=[Wup, h2T], writes=[ub])
            for gv, dst in ((0, cg), (1, cv)):
                ch = gv * NCH + c
                pv3 = ps[:, ubase + gv * 512:ubase + gv * 512 + N].rearrange("p (t c) -> p t c", c=128)
                kb.op("act", lambda e, dst=dst, pv3=pv3, ch=ch: e.activation(
                    out=dst.ap[:, 0:nt, :], in_=pv3[:, :, 2:128], func=AF.Identity,
                    scale=cw.ap[:, 2 * 2 * NCH + ch:2 * 2 * NCH + ch + 1], bias=cb.ap[:, ch:ch + 1]),
                    reads=[ub, cw, cb], writes=[dst])
                for jtap in (1, 0):
                    kb.op("dve", lambda e, dst=dst, pv3=pv3, ch=ch, jtap=jtap: e.scalar_tensor_tensor(
                        out=dst.ap[:, 0:nt, :], in0=pv3[:, :, jtap:jtap + 126],
                        scalar=cw.ap[:, jtap * 2 * NCH + ch:jtap * 2 * NCH + ch + 1], in1=dst.ap[:, 0:nt, :],
                        op0=ALU.mult, op1=ALU.add), reads=[ub, cw, dst], writes=[dst])

        def ffn_s2(gi, c):
            nt = AGROUPS[gi][1]
            cg, cv, z1, z2 = cgs[c % 4], cvs[c % 4], z1s[c % 4], z2s[c % 4]
            kb.op("act", lambda e: e.activation(out=z1.ap[:, 0:nt, :], in_=cg.ap[:, 0:nt, :], func=AF.Square), reads=[cg], writes=[z1])
            kb.op("pool", lambda e: e.tensor_scalar(out=z1.ap[:, 0:nt, :], in0=z1.ap[:, 0:nt, :], scalar1=0.044715, scalar2=1.0,
                                                    op0=ALU.mult, op1=ALU.add), reads=[z1], writes=[z1])
            kb.op("pool", lambda e: e.tensor_tensor(out=z1.ap[:, 0:nt, :], in0=z1.ap[:, 0:nt, :], in1=cg.ap[:, 0:nt, :], op=ALU.mult),
                  reads=[z1, cg], writes=[z1])
            kb.op("pool", lambda e: e.tensor_tensor(out=z2.ap[:, 0:nt, :], in0=cg.ap[:, 0:nt, :], in1=cv.ap[:, 0:nt, :], op=ALU.mult),
                  reads=[cg, cv], writes=[z2])
            kb.op("dve", lambda e: e.tensor_scalar(out=z1.ap[:, 0:nt, :], in0=z1.ap[:, 0:nt, :], scalar1=-20.0, scalar2=None, op0=ALU.max),
                  reads=[z1], writes=[z1])

        def ffn_s3(gi, c):
            nt = AGROUPS[gi][1]
            N = nt * 128
            actT = actTs[gi % 2]
            z1, z2 = z1s[c % 4], z2s[c % 4]
            kb.op("act", lambda e: e.activation(out=z1.ap[:, 0:nt, :], in_=z1.ap[:, 0:nt, :], func=AF.Exp, scale=-2.0 * GELU_C),
                  reads=[z1], writes=[z1])
            kb.op("act", lambda e: e.activation(out=z1.ap[:, 0:nt, :], in_=z1.ap[:, 0:nt, :], func=AF.Ln, scale=1.0, bias=1.0),
                  reads=[z1], writes=[z1])
            kb.op("act", lambda e: e.activation(out=z1.ap[:, 0:nt, :], in_=z1.ap[:, 0:nt, :], func=AF.Exp, scale=-1.0),
                  reads=[z1], writes=[z1])
            kb.op("dve", lambda e: e.tensor_tensor(
                out=actT.ap[:, c, 0:N].rearrange("p (t c) -> p t c", c=128)[:, :, 2:128], in0=z2.ap[:, 0:nt, :], in1=z1.ap[:, 0:nt, :],
                op=ALU.mult), reads=[z1, z2], writes=[actT])

        fctx = {"u": 0}
        for gi, (j0, nt) in enumerate(AGROUPS):
            N = nt * 128
            actT = actTs[gi % 2]
            for step in range(NCH + 2):
                if gi + 1 < len(AGROUPS) and step in (2, 7, 12, 17) and (step - 2) // 5 < AGROUPS[gi + 1][1]:
                    prep3(gi + 1, (step - 2) // 5)
                if step < NCH:
                    ffn_s1(gi, step, fctx)
                if 1 <= step <= NCH:
                    ffn_s2(gi, step - 1)
                if step >= 2:
                    ffn_s3(gi, step - 2)
            kb.dma("sp", lambda e, actT=actT, j0=j0, N=N: e.dma_start(out=ACT_s[:, :, j0 * 128:j0 * 128 + N].rearrange("c p n -> p c n"),
                                                                    in_=actT.ap[:, :, 0:N]), reads=[actT], semtile=actT)
        kb.barrier()

        off[0] = persist_ffn
        g5 = carve("g5", [1024], F32, dma=True)
        load_g(g5, "g_ffn_post")
        Wdn = carve("Wdn", [NCH, 1024], BF16, dma=True)
        load_w(Wdn, wview(w_down, 0, 1024))
        actTs = [carve(f"actTb{i}", [NCH, 512], BF16, dma=True) for i in range(2)]
        xm4 = [carve(f"xmg{i}", [1024], F32, dma=True) for i in range(4)]
        ost = [carve(f"ost{i}", [1024], F32, dma=True) for i in range(2)]
        ssf = carve("ssfb", [1], F32)
        rsf = carve("rsfb", [1], F32)
        ocount = 0

        def load3b(gi):
            j0, nt = AGROUPS[gi]
            N = nt * 128
            actT = actTs[gi % 2]
            kb.dma("sp", lambda e: e.dma_start(out=actT.ap[:, :, 0:N], in_=ACT_s[:, :, j0 * 128:j0 * 128 + N].rearrange("c p n -> p c n")),
                   writes=[actT], semtile=actT)

        load3b(0)
        for gi, (j0, nt) in enumerate(AGROUPS):
            if gi + 1 < len(AGROUPS):
                load3b(gi + 1)
            actT = actTs[gi % 2]
            for t in range(nt):
                n = j0 + t
                xm = xm4[n % 4]
                kb.dma("sp", lambda e, xm=xm, n=n: e.dma_start(out=xm.ap, in_=XM_s[n * 128:(n + 1) * 128, :]), writes=[xm], semtile=xm)
            for t in range(nt):
                n = j0 + t
                xm = xm4[n % 4]
                o = ost[ocount % 2]
                ob = 4 + 2 * (ocount % 2)
                ocount += 1
                for half in range(2):
                    for c in range(NCH):
                        kb.op("pe", lambda e, half=half, c=c, t=t, actT=actT, ob=ob: e.matmul(
                            out=bank(ob + half), lhsT=actT.ap[:, c, t * 128:(t + 1) * 128], rhs=Wdn.ap[:, c, half * 512:(half + 1) * 512],
                            start=(c == 0), stop=(c == NCH - 1)), reads=[actT, Wdn], writes=[PB[ob + half]])
                kb.op("act", lambda e, o=o, ob=ob: e.activation(out=o.ap, in_=bank(ob, 2), func=AF.Square, accum_out=ssf.ap),
                      reads=[PB[ob], PB[ob + 1]], writes=[o, ssf])
                rstd_from_ss(rsf, ssf, D)
                kb.op("dve", lambda e, o=o, ob=ob: e.scalar_tensor_tensor(out=o.ap, in0=bank(ob, 2), scalar=rsf.ap, in1=g5.ap, op0=ALU.mult, op1=ALU.mult),
                      reads=[PB[ob], PB[ob + 1], rsf, g5], writes=[o])
                kb.op("pool", lambda e, o=o, xm=xm: e.tensor_tensor(out=o.ap, in0=o.ap, in1=xm.ap, op=ALU.add), reads=[o, xm], writes=[o])
                kb.dma("sp", lambda e, o=o, n=n: e.dma_start(out=out[n * 128:(n + 1) * 128, :], in_=o.ap), reads=[o], semtile=o)
        kb.emit()
    return nc


def _core_consts(r):
    pos = np.zeros((NT, 128), np.int64)
    val = np.zeros((NT, 128), np.float32)
    for j in range(NT):
        p = ST * (tile_base(j) + r) - 2 + np.arange(128)
        val[j] = ((p >= 0) & (p < S)).astype(np.float32)
        pos[j] = np.clip(p, 0, S - 1)
    fpos = pos.reshape(-1)
    qaug = np.zeros((NH, 4, NTOK), np.float32)
    for h in range(NH):
        s8 = 8.0 * SLOPES[h]
        qaug[h, 0] = s8 * 128.0
        qaug[h, 1] = s8
        qaug[h, 2] = -s8 * 128.0 * (fpos // 128)
        qaug[h, 3] = -s8 * (fpos % 128)
    masks = np.zeros((128, NT, NM, 128), np.float32)
    for j in range(NT):
        lo, hi = tile_bounds(j)
        for mi in range(NM):
            kpos = (lo + mi) * 128 + np.arange(128)
            masks[:, j, mi, :] = np.where(kpos[:, None] <= pos[j][None, :], 0.0, -30000.0).astype(np.float32)
    invc = np.zeros((4, NTOK), np.float32)
    for g, w in enumerate((2, 4, 8, 16)):
        invc[g] = 1.0 / np.minimum(fpos + 1, w)
    return pos, val, qaug, masks.reshape(128, -1), invc


_PROGRAM = None


def kernel(x, mem, norm_mix_pre, w_in, lambda_q1, lambda_k1, lambda_q2, lambda_k2, subln_g,
           w_attn_branch, pool_w, pool_scale, w_pool_branch, norm_mem, w_mem_kv, w_mem_branch,
           w_out, norm_mix_post, norm_ffn_pre, w_up, conv_w, conv_b, w_down, norm_ffn_post):
    global _PROGRAM
    f = lambda a: np.ascontiguousarray(np.asarray(a, dtype=np.float32))
    x = f(x)
    mem = f(mem)
    if _PROGRAM is None:
        _PROGRAM = build_program()
    nc = _PROGRAM
    kaug = np.zeros((4, S), np.float32)
    kp = np.arange(S)
    kaug[0] = kp // 128
    kaug[1] = kp % 128
    kaug[2] = 1.0
    kaug[3] = 1.0
    cwv = f(conv_w)[0]
    convw = np.ascontiguousarray(cwv.reshape(3, 2 * NCH, 128).transpose(2, 0, 1).reshape(128, 3 * 2 * NCH))
    convb = np.ascontiguousarray(f(conv_b)[0].reshape(2 * NCH, 128).T)
    shared = {
        "w_in": f(w_in)[0], "w_attn": f(w_attn_branch)[0], "pool_w": f(pool_w)[0], "w_pb": f(w_pool_branch)[0],
        "w_mkv": f(w_mem_kv)[0], "w_mb": f(w_mem_branch)[0], "w_out": f(w_out)[0], "w_up": f(w_up)[0],
        "w_down": f(w_down)[0],
        "g_mix_pre": f(norm_mix_pre), "g_mem": f(norm_mem), "g_mix_post": f(norm_mix_post),
        "g_ffn_pre": f(norm_ffn_pre), "g_ffn_post": f(norm_ffn_post),
        "lamv": np.ascontiguousarray(np.concatenate([f(lambda_q1), f(lambda_k1), f(lambda_q2), f(lambda_k2)], 0)),
        "subln": np.ascontiguousarray(f(subln_g)[0].reshape(128, 1)),
        "pscale": np.ascontiguousarray(f(pool_scale)[0].reshape(4, 128).T),
        "convw": convw, "convb": convb, "ident": np.eye(128, dtype=np.float32), "kaug": kaug,
    }
    consts = [_core_consts(r) for r in range(4)]
    in_maps = []
    for c in range(8):
        b, r = c // 4, c % 4
        pos, val, qaug, masks, invc = consts[r]
        xo = x[b][pos.reshape(-1)]
        xo[val.reshape(-1) == 0] = 0.0
        hp = (ST * (np.array([tile_base(j) for j in range(NT)]) + r) - 2)[:, None] - 16 + np.arange(16)[None, :]
        hv = ((hp >= 0) & (hp < S)).astype(np.float32)
        xhh = x[b][np.clip(hp, 0, S - 1).reshape(-1)]
        xhh[hv.reshape(-1) == 0] = 0.0
        m = dict(shared)
        m.update({"xb": x[b], "xo": np.ascontiguousarray(xo), "xh": np.ascontiguousarray(xhh), "memb": mem[b],
                  "qaug": qaug, "masks": masks, "invc": invc, "valid": np.ascontiguousarray(val.T)})
        in_maps.append(m)
    res = run_bass_kernel_spmd(nc, in_maps, core_ids=list(range(8)))
    outp = np.zeros((2, S, D), np.float32)
    for c in range(8):
        b, r = c // 4, c % 4
        o = np.asarray(res.results[c]["out"]).reshape(NT, 128, D)
        for j in range(NT):
            p0 = ST * (tile_base(j) + r)
            n = min(ST, S - p0)
            if n > 0:
                outp[b, p0:p0 + n] = o[j, 2:2 + n]
    return outp
```
